# Optimizing a Trainium2 kernel written in Bass

```python
import jax, jax.numpy as jnp
from jax import lax
import numpy as np

D_MODEL = 1024
BATCH = 8
SEQ = 4096
DEPTH = 2

N_MIXERS = 2
N_POOL_GROUPS = 4
POOL_GROUP = D_MODEL // N_POOL_GROUPS
POOL_WINDOWS = (2, 4, 8, 16)
N_HEADS = 16
QK_NOPE = 64
QK_ROPE = 32
V_HEAD = 64
Q_LORA = D_MODEL // 4
KV_LORA = D_MODEL // 8
ROPE_THETA = 10000.0
D_FF = 11 * D_MODEL // 4
Q_BLOCK = 128
EPS = 1e-6
N_MOD = 9
N_POOL_LAYERS = (DEPTH + 1) // 2
N_MLA_LAYERS = DEPTH // 2
ATTN_SCALE = (QK_NOPE + QK_ROPE) ** -0.5

kernel_name = "hybrid_pool_mla_macaron_encoder"


def rmsnorm(x, g):
    xf = x.astype(jnp.float32)
    y = xf * lax.rsqrt(jnp.mean(xf * xf, axis=-1, keepdims=True) + EPS)
    return (y * g.astype(jnp.float32)).astype(x.dtype)


def swiglu(h, w_in, w_out):
    gate, up = jnp.split(h @ w_in, 2, axis=-1)
    return (jax.nn.silu(gate) * up) @ w_out


def centred_mean(x, window):
    s = x.shape[1]
    cs = lax.cumsum(x.astype(jnp.float32), axis=1)
    cs = jnp.pad(cs, ((0, 0), (1, 0), (0, 0)))
    t = jnp.arange(s)
    hi = jnp.clip(t + window // 2, 0, s)
    lo = jnp.clip(t - window // 2, 0, s)
    tot = jnp.take(cs, hi, axis=1) - jnp.take(cs, lo, axis=1)
    cnt = (hi - lo).astype(jnp.float32)[None, :, None]
    return (tot / cnt).astype(x.dtype)


def pool_mixer(h, w, b, scale):
    B, S, _ = h.shape
    hg = h.reshape(B, S, N_POOL_GROUPS, POOL_GROUP)
    pooled = jnp.stack([centred_mean(hg[:, :, g], POOL_WINDOWS[g]) for g in range(N_POOL_GROUPS)], axis=2)
    y = jnp.einsum('bsgc,gcd->bsgd', pooled - hg, w) + b
    return y.reshape(B, S, D_MODEL) * scale


def rope_tables(s, dtype):
    inv = 1.0 / (ROPE_THETA ** (jnp.arange(0, QK_ROPE, 2, dtype=jnp.float32) / QK_ROPE))
    ang = jnp.arange(s, dtype=jnp.float32)[:, None] * inv[None, :]
    return jnp.cos(ang).astype(dtype), jnp.sin(ang).astype(dtype)


def apply_rope(x, cos, sin):
    x1, x2 = jnp.split(x, 2, axis=-1)
    return jnp.concatenate([x1 * cos - x2 * sin, x2 * cos + x1 * sin], axis=-1)


def mla_mixer(h, w_in, q_norm, kv_norm, w_uq, w_uk, w_uv, w_o, cos, sin):
    B, S, _ = h.shape
    lat = h @ w_in
    c_q, c_kv, k_r = jnp.split(lat, [Q_LORA, Q_LORA + KV_LORA], axis=-1)
    c_q = rmsnorm(c_q, q_norm)
    c_kv = rmsnorm(c_kv, kv_norm)
    q = jnp.einsum('bsc,chd->bshd', c_q, w_uq)
    q_nope, q_rope = q[..., :QK_NOPE], q[..., QK_NOPE:]
    q_rope = apply_rope(q_rope, cos[:, None, :], sin[:, None, :]) * ATTN_SCALE
    k_rope = apply_rope(k_r, cos, sin)
    q_lat = jnp.einsum('bshn,chn->bshc', q_nope, w_uk) * ATTN_SCALE
    nb = S // Q_BLOCK
    qlb = q_lat.reshape(B, nb, Q_BLOCK, N_HEADS, KV_LORA).transpose(1, 0, 2, 3, 4)
    qrb = q_rope.reshape(B, nb, Q_BLOCK, N_HEADS, QK_ROPE).transpose(1, 0, 2, 3, 4)

    def block(args):
        ql, qr = args
        s = (jnp.einsum('bqhc,bkc->bhqk', ql, c_kv)
             + jnp.einsum('bqhr,bkr->bhqk', qr, k_rope))
        p = jax.nn.softmax(s.astype(jnp.float32), axis=-1).astype(c_kv.dtype)
        return jnp.einsum('bhqk,bkc->bqhc', p, c_kv)

    o_lat = lax.map(block, (qlb, qrb))
    o_lat = o_lat.transpose(1, 0, 2, 3, 4).reshape(B, S, N_HEADS, KV_LORA)
    o = jnp.einsum('bshc,chv->bshv', o_lat, w_uv)
    return o.reshape(B, S, N_HEADS * V_HEAD) @ w_o


def modulated_sublayer(x, mod, g_pre, g_post, fn, weight):
    shift, scale, gate = mod[:, 0], mod[:, 1], mod[:, 2]
    h = rmsnorm(x, g_pre) * (1.0 + scale) + shift
    y = rmsnorm(fn(h), g_post)
    return x + weight * (1.0 + gate) * y


def setup_inputs(seed: int = 0) -> dict:
    key = jax.random.key(seed)
    ks = jax.random.split(key, 20)
    n = jax.random.normal
    f32 = jnp.float32
    return {
        "x": n(ks[0], (BATCH, SEQ, D_MODEL), f32),
        "c": n(ks[1], (BATCH, D_MODEL), f32),
        "ada_w": n(ks[2], (DEPTH, D_MODEL, N_MOD * D_MODEL), f32) * (0.5 * D_MODEL ** -0.5),
        "ada_b": n(ks[3], (DEPTH, N_MOD * D_MODEL), f32) * 0.01,
        "norm_g": 1.0 + 0.05 * n(ks[4], (DEPTH, 6, D_MODEL), f32),
        "ffn_w_in": n(ks[5], (DEPTH, 2, D_MODEL, 2 * D_FF), f32) * D_MODEL ** -0.5,
        "ffn_w_out": n(ks[6], (DEPTH, 2, D_FF, D_MODEL), f32) * D_FF ** -0.5,
        "pool_w": n(ks[7], (N_POOL_LAYERS, N_POOL_GROUPS, POOL_GROUP, POOL_GROUP), f32) * POOL_GROUP ** -0.5,
        "pool_b": n(ks[8], (N_POOL_LAYERS, N_POOL_GROUPS, POOL_GROUP), f32) * 0.01,
        "pool_scale": 1.0 + 0.05 * n(ks[9], (N_POOL_LAYERS, D_MODEL), f32),
        "mla_w_in": n(ks[10], (N_MLA_LAYERS, D_MODEL, Q_LORA + KV_LORA + QK_ROPE), f32) * D_MODEL ** -0.5,
        "mla_q_norm": 1.0 + 0.05 * n(ks[11], (N_MLA_LAYERS, Q_LORA), f32),
        "mla_kv_norm": 1.0 + 0.05 * n(ks[12], (N_MLA_LAYERS, KV_LORA), f32),
        "mla_w_uq": n(ks[13], (N_MLA_LAYERS, Q_LORA, N_HEADS, QK_NOPE + QK_ROPE), f32) * Q_LORA ** -0.5,
        "mla_w_uk": n(ks[14], (N_MLA_LAYERS, KV_LORA, N_HEADS, QK_NOPE), f32) * KV_LORA ** -0.5,
        "mla_w_uv": n(ks[15], (N_MLA_LAYERS, KV_LORA, N_HEADS, V_HEAD), f32) * KV_LORA ** -0.5,
        "mla_w_o": n(ks[16], (N_MLA_LAYERS, N_HEADS * V_HEAD, D_MODEL), f32) * (N_HEADS * V_HEAD) ** -0.5,
    }


def reference(x, c, ada_w, ada_b, norm_g, ffn_w_in, ffn_w_out, pool_w, pool_b, pool_scale,
              mla_w_in, mla_q_norm, mla_kv_norm, mla_w_uq, mla_w_uk, mla_w_uv, mla_w_o):
    B = x.shape[0]
    cos, sin = rope_tables(x.shape[1], x.dtype)
    sc = jax.nn.silu(c)
    for i in range(DEPTH):
        mod = (sc @ ada_w[i] + ada_b[i]).reshape(B, N_MOD, D_MODEL)[:, :, None, :]
        g = norm_g[i]
        x = modulated_sublayer(x, mod[:, 0:3], g[0], g[1],
                               lambda h: swiglu(h, ffn_w_in[i, 0], ffn_w_out[i, 0]), 0.5)
        if i % N_MIXERS == 0:
            li = i // N_MIXERS
            mixer = lambda h: pool_mixer(h, pool_w[li], pool_b[li], pool_scale[li])
        else:
            li = i // N_MIXERS
            mixer = lambda h: mla_mixer(h, mla_w_in[li], mla_q_norm[li], mla_kv_norm[li],
                                        mla_w_uq[li], mla_w_uk[li], mla_w_uv[li], mla_w_o[li],
                                        cos, sin)
        x = modulated_sublayer(x, mod[:, 3:6], g[2], g[3], mixer, 1.0)
        x = modulated_sublayer(x, mod[:, 6:9], g[4], g[5],
                               lambda h: swiglu(h, ffn_w_in[i, 1], ffn_w_out[i, 1]), 0.5)
    return x
```

```python
from contextlib import ExitStack
import numpy as np
import concourse.bass as bass
import concourse.mybir as mybir
from concourse.bass_utils import run_bass_kernel_spmd

F32 = mybir.dt.float32
BF16 = mybir.dt.bfloat16
U8 = mybir.dt.uint8
ALU = mybir.AluOpType
AF = mybir.ActivationFunctionType

PE, ACT, DVE, POOL, SP = "pe", "act", "dve", "pool", "sp"

D = 1024
S = 4096
T = 512
NT = S // T
KC = 8
DFF = 2816
FC = 22
NH = 16
EPS = 1e-6
ATTN_SCALE = float(96 ** -0.5)
POOL_WINDOWS = (2, 4, 8, 16)
NKC = S // 128


class Buf:
    __slots__ = ("name", "last_writer", "pwriters", "readers", "dsem", "dcount", "excl")

    def __init__(self, name):
        self.name = name
        self.excl = False
        self.last_writer = None
        self.pwriters = []
        self.readers = []
        self.dsem = None
        self.dcount = 0


class Op:
    __slots__ = ("idx", "eng", "fn", "deps", "is_dma", "sem_buf", "token", "signal")

    def __init__(self, idx, eng, fn, is_dma, sem_buf):
        self.idx = idx
        self.eng = eng
        self.fn = fn
        self.deps = []
        self.is_dma = is_dma
        self.sem_buf = sem_buf
        self.token = None
        self.signal = False


class Prog:
    SEM_ROLL = 6000
    DMA_SEM_ROLL = 2048
    SWDGE_WINDOW = 4

    def __init__(self, nc, same_engine_sync=True):
        self.nc = nc
        self.ops = []
        self.same_engine_sync = same_engine_sync
        self.pool_dmas = []

    def op(self, eng, fn, reads=(), writes=(), pwrites=(), dma=False, sem_buf=None):
        o = Op(len(self.ops), eng, fn, dma, sem_buf)
        if any(b.excl for b in reads):
            writes = list(writes) + [b for b in reads if b.excl and b not in writes and b not in pwrites]
            reads = [b for b in reads if not b.excl]
        deps = {}
        for b in reads:
            if b.last_writer is not None:
                deps[b.last_writer.idx] = b.last_writer
            for w in b.pwriters:
                deps[w.idx] = w
        for b in writes:
            if b.last_writer is not None:
                deps[b.last_writer.idx] = b.last_writer
            for w in b.pwriters:
                deps[w.idx] = w
            for r in b.readers:
                deps[r.idx] = r
        for b in pwrites:
            if b.last_writer is not None:
                deps[b.last_writer.idx] = b.last_writer
            for r in b.readers:
                deps[r.idx] = r
        o.deps = list(deps.values())
        for b in reads:
            b.readers.append(o)
        for b in writes:
            b.last_writer = o
            b.pwriters = []
            b.readers = []
        for b in pwrites:
            b.pwriters.append(o)
        self.ops.append(o)
        return o

    def dma(self, eng, out, in_, reads=(), writes=(), pwrites=(), sem_buf=None):
        if sem_buf is None:
            sem_buf = writes[0] if writes else (pwrites[0] if pwrites else reads[0])
        o = self.op(eng, lambda e: e.dma_start(out=out, in_=in_), reads, writes, pwrites,
                    dma=True, sem_buf=sem_buf)
        if eng == POOL:
            q = self.pool_dmas
            if len(q) >= self.SWDGE_WINDOW:
                o.deps.append(q[-self.SWDGE_WINDOW])
            q.append(o)
        return o

    def emit(self, stack):
        nc = self.nc
        ops = self.ops
        for o in ops:
            if o.is_dma:
                o.signal = True
            for d in o.deps:
                d.signal = True
        eng_sems, eng_cnt, nsem = {}, {}, [0]

        def new_sem(tag):
            nsem[0] += 1
            return stack.enter_context(nc.semaphore(f"s_{tag}_{nsem[0]}"))

        for o in ops:
            if not o.signal:
                continue
            if o.is_dma:
                b = o.sem_buf
                if b.dsem is None or b.dcount >= self.DMA_SEM_ROLL:
                    b.dsem = new_sem("d")
                    b.dcount = 0
                b.dcount += 16
                o.token = (b.dsem, b.dcount)
            else:
                if o.eng not in eng_sems or eng_cnt[o.eng] >= self.SEM_ROLL:
                    eng_sems[o.eng] = new_sem(o.eng)
                    eng_cnt[o.eng] = 0
                eng_cnt[o.eng] += 1
                o.token = (eng_sems[o.eng], eng_cnt[o.eng])
        self.n_sems = nsem[0]
        streams = {}
        for o in ops:
            streams.setdefault(o.eng, []).append(o)
        block = stack.enter_context(nc.Block())
        same = self.same_engine_sync
        nwaits = [0]

        def make(eng_name, lst):
            def body(e):
                waited_eng = {}
                waited_dma = {}
                for o in lst:
                    need = {}
                    need_dma = {}
                    for d in o.deps:
                        if d.is_dma:
                            sem, val = d.token
                            k = id(sem)
                            if waited_dma.get(k, 0) >= val:
                                continue
                            if k not in need_dma or need_dma[k][1] < val:
                                need_dma[k] = (sem, val)
                        else:
                            if d.eng == eng_name and (eng_name == PE or not same):
                                continue
                            if waited_eng.get(d.eng, -1) >= d.idx:
                                continue
                            if d.eng not in need or need[d.eng].idx < d.idx:
                                need[d.eng] = d
                    for k, (sem, val) in need_dma.items():
                        waited_dma[k] = val
                        e.wait_ge(sem, val)
                        nwaits[0] += 1
                    for src, d in need.items():
                        waited_eng[src] = d.idx
                        e.wait_ge(d.token[0], d.token[1])
                        nwaits[0] += 1
                    ins = o.fn(e)
                    if o.signal:
                        ins.then_inc(o.token[0], 16 if o.is_dma else 1)
            return body

        reg = {PE: block.tensor, ACT: block.scalar, DVE: block.vector, POOL: block.gpsimd, SP: block.sync}
        for eng_name, lst in streams.items():
            reg[eng_name](make(eng_name, lst))
        self.n_waits = nwaits[0]


class Arena:
    def __init__(self, tensor, size):
        self.t = tensor
        self.size = size
        self.off = 0

    def reset(self, off=0):
        self.off = off

    def alloc(self, shape, dtype, parts=128):
        esz = 2 if dtype == BF16 else 4
        n = 1
        for s in shape:
            n *= s
        nbytes = (n * esz + 63) // 64 * 64
        assert self.off + nbytes <= self.size, ("arena overflow", self.off, nbytes, self.size)
        ap = self.t[0:parts, self.off:self.off + n * esz].bitcast(dtype)
        self.off += nbytes
        if len(shape) == 2:
            ap = ap.rearrange("p (a b) -> p a b", a=shape[0])
        elif len(shape) == 3:
            ap = ap.rearrange("p (a b c) -> p a b c", a=shape[0], b=shape[1])
        return ap


def build_program(sublayers, debug=False, max_ops=None, marks=None):
    nc = bass.Bass("TRN2", target_bir_lowering=False)

    def din(name, shape, dt=F32):
        return nc.dram_tensor(name, list(shape), dt, kind="ExternalInput").ap()

    xT = din("xT", [D, S])
    cT = din("cT", [128, KC])
    ada_w = din("ada_w", [2, D, 9 * D])
    ada_bT = din("ada_bT", [2, 128, 72])
    norm_gT = din("norm_gT", [2, 128, 48])
    ffn_w_in = din("ffn_w_in", [2, 2, D, 2 * DFF])
    ffn_w_out = din("ffn_w_out", [2, 2, DFF, D])
    pool_w = din("pool_w", [4, 256, 256])
    pool_bT = din("pool_bT", [128, 8])
    pool_scT = din("pool_scT", [128, 8])
    mla_w_in_x = din("mla_w_in_x", [D, 448])
    q_normT = din("q_normT", [128, 2])
    kv_normT = din("kv_normT", [128, 1])
    w_uq_x = din("w_uq_x", [256, 2048])
    w_ukT = din("w_ukT", [64, NH, 128])
    w_uv = din("w_uv", [128, NH * 64])
    w_o = din("w_o", [NH * 64, D])
    rope_cos = din("rope_cos", [32, S])
    rope_sin = din("rope_sin", [32, S])
    outT = nc.dram_tensor("outT", [D, S], F32, kind="ExternalOutput").ap()

    xres = [nc.dram_tensor(f"xres{i}", [D, S], F32).ap() for i in range(2)]
    ffn_ids = sorted({(l, s // 2) for (l, s) in sublayers if s != 1})
    win_s = {k: nc.dram_tensor(f"win_s{k[0]}{k[1]}", [FC, 128, KC * 256], BF16).ap() for k in ffn_ids}
    wout_s = {k: nc.dram_tensor(f"wout_s{k[0]}{k[1]}", [KC, 128, FC * 128], BF16).ap() for k in ffn_ids}
    wo_s = nc.dram_tensor("wo_s", [KC, 64, NH * 128], BF16).ap()

    P = Prog(nc)
    bufs = {}

    REGION_KEYS = {"ada_ring", "actT", "win_ring", "wout_ring", "xe", "hE", "tE", "ua", "ub", "ua2", "ub2", "dT", "pw",
                   "sqE", "rsE", "cq_all", "ckvT", "krope", "winx", "wuq", "wuk", "wuv", "wo_ring", "Vg", "qnope",
                   "qlat", "qrope", "qrope2", "krope2", "pT", "osb", "rc", "oT", "tab_c", "tab_s", "t1", "t2", "cq32"}
    state = {"fence_op": None}

    def B(*key):
        if key not in bufs:
            b = Buf(str(key))
            if key[0] in REGION_KEYS:
                b.last_writer = state["fence_op"]
            bufs[key] = b
        return bufs[key]

    st = ExitStack()
    with st:
        COMMON = 89 * 1024
        REGION = 118 * 1024
        common_t = st.enter_context(nc.sbuf_tensor("common", [128, COMMON], U8))
        region_t = st.enter_context(nc.sbuf_tensor("region", [128, REGION], U8))
        CA = Arena(common_t, COMMON)
        RA = Arena(region_t, REGION)
        ps = [st.enter_context(nc.psum_tensor(f"ps{i}", [128, 512], F32)) for i in range(8)]
        PB = [B("psum", i) for i in range(8)]
        for b_ in PB:
            b_.excl = True

        xslot = [CA.alloc([KC, T], F32) for _ in range(2)]
        hT = CA.alloc([KC, T], BF16)
        tmp32 = CA.alloc([KC, T], F32)
        sq8 = tmp32.rearrange("p a b -> p (a b)")[:, 0:KC * T // 2].bitcast(BF16).rearrange("p (a b) -> p a b", a=KC)
        ysb = CA.alloc([KC, T], F32)
        rsA = CA.alloc([1, T], F32)[:, 0, :]
        rsB = CA.alloc([1, T], F32)[:, 0, :]
        sg = [CA.alloc([1, T], F32)[:, 0, :] for _ in range(2)]
        sqr = [CA.alloc([1, T], BF16)[:, 0, :] for _ in range(2)]
        ones_bf = CA.alloc([1, 128], BF16)[:, 0, :]
        ones_f = CA.alloc([1, 128], F32)[:, 0, :]
        epsc = CA.alloc([1, 1], F32)[:, 0, :]
        c_sb = CA.alloc([1, KC], F32)[:, 0, :]
        sc_bf = CA.alloc([1, KC], BF16)[:, 0, :]
        modT = [CA.alloc([1, 72], F32)[:, 0, :] for _ in range(2)]
        adab = [CA.alloc([1, 72], F32)[:, 0, :] for _ in range(2)]
        ng = [CA.alloc([1, 48], F32)[:, 0, :] for _ in range(2)]
        vecA = CA.alloc([6, KC], F32)
        vecB = CA.alloc([6, KC], F32)
        smallv = CA.alloc([1, 32], F32)[:, 0, :]
        pT_extra = CA.alloc([1, T], BF16)[:, 0, :]
        print("common arena used", CA.off, "of", COMMON)

        P.op(POOL, lambda e: e.memset(ones_bf, 1.0), writes=[B("ones_bf")])
        P.op(POOL, lambda e: e.memset(ones_f, 1.0), writes=[B("ones_f")])
        P.op(POOL, lambda e: e.memset(epsc, EPS), writes=[B("epsc")])
        P.dma(SP, c_sb, cT, writes=[B("c_sb")])
        for l in range(2):
            P.dma(SP, adab[l], ada_bT[l], writes=[B("adab", l)])
            P.dma(SP, ng[l], norm_gT[l], writes=[B("ng", l)])
        P.dma(SP, smallv[:, 0:8], pool_bT, pwrites=[B("smallv")])
        P.dma(SP, smallv[:, 8:16], pool_scT, pwrites=[B("smallv")])
        P.dma(SP, smallv[:, 16:18], q_normT, pwrites=[B("smallv")])
        P.dma(SP, smallv[:, 18:19], kv_normT, pwrites=[B("smallv")])
        P.op(ACT, lambda e: e.activation(out=sc_bf, in_=c_sb, func=AF.Silu), reads=[B("c_sb")], writes=[B("sc_bf")])

        pq = []

        def precast(k):
            l, f = k
            wi = ffn_w_in[l, f].rearrange("(kc p) n -> p kc n", p=128)
            wo = ffn_w_out[l, f].rearrange("(fc p) n -> p fc n", p=128)
            for j in range(FC):
                dst = win_s[k][j].rearrange("p (kc x) -> p kc x", x=256)
                for gu in range(2):
                    c0 = gu * DFF + j * 128
                    pq.append((("win_s", k), dst[:, :, gu * 128:(gu + 1) * 128], wi[:, :, c0:c0 + 128]))
            for c in range(KC):
                dst = wout_s[k][c].rearrange("p (fc d) -> p fc d", d=128)
                for h0 in (0, 11):
                    pq.append((("wout_s", k), dst[:, h0:h0 + 11, :], wo[:, h0:h0 + 11, c * 128:(c + 1) * 128]))

        def pump(n):
            for _ in range(min(n, len(pq))):
                key, dst, src = pq.pop(0)
                P.dma(POOL, dst, src, pwrites=[B(*key)])

        def flush_precast(k):
            while any(key[1] == k for (key, _, _) in pq):
                pump(1)

        def compute_mod(l, RAm):
            ada_ring = [RAm.alloc([KC, 512], BF16) for _ in range(2)]
            aw = ada_w[l].rearrange("(kc p) n -> p kc n", p=128)
            mps = ps[7]
            for bi in range(18):
                slot = ada_ring[bi % 2]
                sb = B("ada_ring", bi % 2)
                P.dma(POOL, slot, aw[:, :, bi * 512:(bi + 1) * 512], writes=[sb])
                for cl in range(4):
                    gc = bi * 4 + cl
                    for kc in range(KC):
                        first = (gc == 0 and kc == 0)
                        P.op(PE, lambda e, slot=slot, cl=cl, kc=kc, gc=gc: e.matmul(
                            mps[:, gc:gc + 1], lhsT=slot[:, kc, cl * 128:(cl + 1) * 128],
                            rhs=sc_bf[:, kc:kc + 1], start=(kc == 0), stop=(kc == KC - 1)),
                            reads=[sb, B("sc_bf")], writes=[PB[7]] if first else (),
                            pwrites=() if first else [PB[7]])
            P.op(DVE, lambda e: e.tensor_tensor(out=modT[l], in0=mps[:, 0:72], in1=adab[l], op=ALU.add),
                 reads=[PB[7], B("adab", l)], writes=[B("modT", l)])
            for sub in range(3):
                i = l * 3 + sub
                wgt = 1.0 if sub == 1 else 0.5
                scale = modT[l][:, (3 * sub + 1) * 8:(3 * sub + 2) * 8]
                gate = modT[l][:, (3 * sub + 2) * 8:(3 * sub + 3) * 8]
                gpre = ng[l][:, (2 * sub) * 8:(2 * sub + 1) * 8]
                gpost = ng[l][:, (2 * sub + 1) * 8:(2 * sub + 2) * 8]
                P.op(DVE, lambda e, i=i, scale=scale, gpre=gpre: e.scalar_tensor_tensor(
                    out=vecA[:, i, :], in0=scale, scalar=1.0, in1=gpre, op0=ALU.add, op1=ALU.mult),
                    reads=[B("modT", l), B("ng", l)], writes=[B("vecA", i)])
                P.op(DVE, lambda e, i=i, gate=gate, gpost=gpost: e.scalar_tensor_tensor(
                    out=vecB[:, i, :], in0=gate, scalar=1.0, in1=gpost, op0=ALU.add, op1=ALU.mult),
                    reads=[B("modT", l), B("ng", l)], writes=[B("vecB", i)])
                P.op(DVE, lambda e, i=i, wgt=wgt: e.tensor_scalar(
                    out=vecB[:, i, :], in0=vecB[:, i, :], scalar1=wgt, scalar2=None, op0=ALU.mult),
                    reads=[B("vecB", i)], writes=[B("vecB", i)])

        state.update({"src": xT, "src_key": "xT"})

        def dram_tile(ap, t):
            return ap.rearrange("(ch p) s -> p ch s", p=128)[:, :, t * T:(t + 1) * T]

        def XB(slot):
            return [B("xslot", slot, c) for c in range(KC)]

        def load_x(t, slot):
            P.dma(POOL, xslot[slot], dram_tile(state["src"], t), reads=[B(state["src_key"], t)],
                  writes=XB(slot))

        def store_x(t, slot, dst, dst_key):
            P.dma(POOL, dram_tile(dst, t), xslot[slot], reads=XB(slot), writes=[B(dst_key, t)],
                  sem_buf=B("xstore", slot))

        def prologue_steps(xap, xbuf, i, hdst, hbufs):
            l, sub = divmod(i, 3)
            shift = modT[l][:, (3 * sub) * 8:(3 * sub + 1) * 8]

            def s_sq():
                P.op(POOL, lambda e: e.tensor_tensor(out=sq8, in0=xap, in1=xap, op=ALU.mult), reads=xbuf, writes=[B("tmp32")])

            def s_mm():
                for kc in range(KC):
                    P.op(PE, lambda e, kc=kc: e.matmul(ps[6][:], lhsT=ones_bf, rhs=sq8[:, kc, :], start=(kc == 0), stop=(kc == KC - 1)),
                         reads=[B("tmp32"), B("ones_bf")], writes=[PB[6]])

            def s_sqrt():
                P.op(ACT, lambda e: e.activation(out=rsA, in_=ps[6][:], func=AF.Sqrt, bias=epsc, scale=1.0 / D),
                     reads=[PB[6], B("epsc")], writes=[B("rsA")])
                P.op(DVE, lambda e: e.reciprocal(out=rsA, in_=rsA), reads=[B("rsA")], writes=[B("rsA")])

            def s_mul():
                P.op(DVE, lambda e: e.tensor_tensor(out=tmp32, in0=xap, in1=rsA.unsqueeze(1).broadcast_to([128, KC, T]), op=ALU.mult),
                     reads=xbuf + [B("rsA")], writes=[B("tmp32")])

            def mk(c):
                def s_mod():
                    P.op(ACT, lambda e: e.activation(out=hdst[:, c, :], in_=tmp32[:, c, :], func=AF.Identity,
                                                     bias=shift[:, c:c + 1], scale=vecA[:, i, c:c + 1]),
                         reads=[B("tmp32"), B("vecA", i), B("modT", l)], writes=[hbufs[c]])
                return s_mod

            return [s_sq, s_mm, s_sqrt, s_mul] + [mk(c) for c in range(KC)]

        def prologue(xap, xbuf, i, hdst, hbufs):
            for st_ in prologue_steps(xap, xbuf, i, hdst, hbufs):
                st_()

        YB = [B("ysb", c) for c in range(KC)]

        def post_stats_sq(c):
            r, rb = sqr[c % 2], B("sqr", c % 2)
            P.op(POOL, lambda e: e.tensor_tensor(out=r, in0=ysb[:, c, :], in1=ysb[:, c, :], op=ALU.mult), reads=[YB[c]], writes=[rb])

        def post_stats_mm(c):
            r, rb = sqr[c % 2], B("sqr", c % 2)
            P.op(PE, lambda e: e.matmul(ps[7][:], lhsT=ones_bf, rhs=r, start=(c == 0), stop=(c == KC - 1)),
                 reads=[rb, B("ones_bf")], writes=[PB[7]])

        def epilogue(xap, xbuf, i):
            P.op(ACT, lambda e: e.activation(out=rsB, in_=ps[7][:], func=AF.Sqrt, bias=epsc, scale=1.0 / D),
                 reads=[PB[7], B("epsc")], writes=[B("rsB")])
            P.op(DVE, lambda e: e.reciprocal(out=rsB, in_=rsB), reads=[B("rsB")], writes=[B("rsB")])
            P.op(DVE, lambda e: e.tensor_tensor(out=ysb, in0=ysb, in1=rsB.unsqueeze(1).broadcast_to([128, KC, T]), op=ALU.mult),
                 reads=YB + [B("rsB")], writes=YB)
            for c in range(KC):
                P.op(DVE, lambda e, c=c: e.scalar_tensor_tensor(out=xap[:, c, :], in0=ysb[:, c, :], scalar=vecB[:, i, c:c + 1],
                                                                in1=xap[:, c, :], op0=ALU.mult, op1=ALU.add),
                     reads=[YB[c], B("vecB", i), xbuf[c]], writes=[xbuf[c]])

        def ffn_sublayer(l, sub, dst, dst_key):
            i = l * 3 + sub
            k = (l, sub // 2)
            flush_precast(k)
            RA.reset()
            actT = RA.alloc([FC, T], BF16)
            win_ring = [RA.alloc([KC, 256], BF16) for _ in range(3)]
            wout_ring = [RA.alloc([FC, 128], BF16) for _ in range(2)]
            hb = [B("hT", c) for c in range(KC)]
            n_in, n_out = NT * FC, NT * KC
            issued = {"in": 0, "out": 0}

            def issue_in(upto):
                while issued["in"] < min(upto, n_in):
                    n = issued["in"]
                    j_ = n % FC
                    P.dma(SP, win_ring[n % 3], win_s[k][j_].rearrange("p (kc x) -> p kc x", x=256),
                          reads=[B("win_s", k)], writes=[B("win_ring", n % 3)])
                    issued["in"] += 1

            def issue_out(upto):
                while issued["out"] < min(upto, n_out):
                    m = issued["out"]
                    c_ = m % KC
                    P.dma(SP, wout_ring[m % 2], wout_s[k][c_].rearrange("p (fc d) -> p fc d", d=128),
                          reads=[B("wout_s", k)], writes=[B("wout_ring", m % 2)])
                    issued["out"] += 1

            load_x(0, 0)
            issue_in(3)
            for st_ in prologue_steps(xslot[0], XB(0), i, hT, hb):
                st_()
            for t in range(NT):
                slot = t % 2
                xap, xbuf = xslot[slot], XB(slot)
                nxt = []
                if t + 1 < NT:
                    load_x(t + 1, (t + 1) % 2)
                    nxt = prologue_steps(xslot[(t + 1) % 2], XB((t + 1) % 2), i, hT, hb)
                issue_out(t * KC + 2)
                for j in range(FC):
                    n = t * FC + j
                    issue_in(n + 3)
                    wslot, wbuf = win_ring[n % 3], B("win_ring", n % 3)
                    gb, ub = j % 2, 2 + j % 2
                    for kc in range(KC):
                        P.op(PE, lambda e, kc=kc, wslot=wslot, gb=gb: e.matmul(ps[gb][:], lhsT=wslot[:, kc, 0:128], rhs=hT[:, kc, :],
                                                                             start=(kc == 0), stop=(kc == KC - 1)),
                             reads=[wbuf, hb[kc]], writes=[PB[gb]])
                    for kc in range(KC):
                        P.op(PE, lambda e, kc=kc, wslot=wslot, ub=ub: e.matmul(ps[ub][:], lhsT=wslot[:, kc, 128:256], rhs=hT[:, kc, :],
                                                                             start=(kc == 0), stop=(kc == KC - 1)),
                             reads=[wbuf, hb[kc]], writes=[PB[ub]])
                    sgt, sgb = sg[j % 2], B("sg", j % 2)
                    P.op(ACT, lambda e, gb=gb, sgt=sgt: e.activation(out=sgt, in_=ps[gb][:], func=AF.Silu), reads=[PB[gb]], writes=[sgb])
                    P.op(DVE, lambda e, ub=ub, sgt=sgt, j=j: e.tensor_tensor(out=actT[:, j, :], in0=sgt, in1=ps[ub][:], op=ALU.mult),
                         reads=[sgb, PB[ub]], writes=[B("actT", j)])
                    if j == 13 and nxt:
                        nxt.pop(0)()
                for c in range(KC):
                    m = t * KC + c
                    issue_out(m + 2)
                    wslot, wbuf = wout_ring[m % 2], B("wout_ring", m % 2)
                    yb = 4 + c % 2
                    for fc in range(FC):
                        P.op(PE, lambda e, fc=fc, wslot=wslot, yb=yb: e.matmul(ps[yb][:], lhsT=wslot[:, fc, :], rhs=actT[:, fc, :],
                                                                             start=(fc == 0), stop=(fc == FC - 1)),
                             reads=[wbuf, B("actT", fc)], writes=[PB[yb]])
                    P.op(DVE, lambda e, c=c, yb=yb: e.tensor_copy(out=ysb[:, c, :], in_=ps[yb][:]), reads=[PB[yb]], writes=[YB[c]])
                    post_stats_sq(c)
                    if c > 0:
                        post_stats_mm(c - 1)
                    take = {0: 1, 1: 2, 2: 1}.get(c, 2)
                    for _ in range(take):
                        if nxt:
                            nxt.pop(0)()
                while nxt:
                    nxt.pop(0)()
                post_stats_mm(KC - 1)
                epilogue(xap, xbuf, i)
                store_x(t, slot, dst, dst_key)
                pump(8)

        def pool_sublayer(l, sub, dst, dst_key):
            i = l * 3 + sub
            RA.reset()
            W = T + 16
            xe = RA.alloc([KC, W], F32)
            hE = RA.alloc([KC, W], F32)
            tE = RA.alloc([KC, W], F32)
            ua = RA.alloc([2, W], F32)
            ub_ = RA.alloc([2, W], F32)
            ua2 = RA.alloc([2, W], F32)
            ub2 = RA.alloc([2, W], F32)
            dT = RA.alloc([KC, T], BF16)
            pw = RA.alloc([4, 2, 256], BF16)
            sqE = RA.alloc([KC, W], BF16)
            rsE = RA.alloc([1, W], F32)[:, 0, :]
            shift = modT[l][:, (3 * sub) * 8:(3 * sub + 1) * 8]
            P.dma(POOL, pw, pool_w.rearrange("g (cc p) d -> p g cc d", p=128), writes=[B("pw")])
            src = state["src"].rearrange("(ch p) s -> p ch s", p=128)
            def p_front(t):
                slot = t % 2
                e0 = max(t * T - 8, 0)
                e1 = min((t + 1) * T + 8, S)
                c0 = e0 - (t * T - 8)
                c1 = c0 + (e1 - e0)
                e0 = max(t * T - 8, 0)
                e1 = min((t + 1) * T + 8, S)
                c0 = e0 - (t * T - 8)
                c1 = c0 + (e1 - e0)
                rd = [B(state["src_key"], tt) for tt in range(max(t - 1, 0), min(t + 2, NT))]
                P.dma(POOL, xe[:, :, c0:c1], src[:, :, e0:e1], reads=rd, writes=[B("xe")])
                slot = t % 2
                P.op(POOL, lambda e, slot=slot: e.tensor_copy(out=xslot[slot], in_=xe[:, :, 8:8 + T]), reads=[B("xe")], writes=XB(slot))
                P.op(ACT, lambda e, c0=c0, c1=c1: e.activation(out=sqE[:, :, c0:c1], in_=xe[:, :, c0:c1], func=AF.Square),
                     reads=[B("xe")], writes=[B("sqE")])
                for kc in range(KC):
                    P.op(PE, lambda e, kc=kc: e.matmul(ps[0][:], lhsT=ones_bf, rhs=sqE[:, kc, 8:8 + T], start=(kc == 0), stop=(kc == KC - 1)),
                         reads=[B("sqE"), B("ones_bf")], writes=[PB[0]])
                if c0 == 0:
                    for kc in range(KC):
                        P.op(PE, lambda e, kc=kc: e.matmul(ps[1][:, 0:8], lhsT=ones_bf, rhs=sqE[:, kc, 0:8], start=(kc == 0), stop=(kc == KC - 1)),
                             reads=[B("sqE"), B("ones_bf")], writes=[PB[1]])
                if c1 == W:
                    for kc in range(KC):
                        P.op(PE, lambda e, kc=kc: e.matmul(ps[1][:, 8:16], lhsT=ones_bf, rhs=sqE[:, kc, W - 8:W], start=(kc == 0), stop=(kc == KC - 1)),
                             reads=[B("sqE"), B("ones_bf")], writes=[PB[1]] if c0 != 0 else (), pwrites=[PB[1]] if c0 == 0 else ())
                P.op(ACT, lambda e: e.activation(out=rsE[:, 8:8 + T], in_=ps[0][:], func=AF.Sqrt, bias=epsc, scale=1.0 / D),
                     reads=[PB[0], B("epsc")], writes=[B("rsE")])
                if c0 == 0:
                    P.op(ACT, lambda e: e.activation(out=rsE[:, 0:8], in_=ps[1][:, 0:8], func=AF.Sqrt, bias=epsc, scale=1.0 / D),
                         reads=[PB[1], B("epsc")], pwrites=[B("rsE")])
                if c1 == W:
                    P.op(ACT, lambda e: e.activation(out=rsE[:, W - 8:W], in_=ps[1][:, 8:16], func=AF.Sqrt, bias=epsc, scale=1.0 / D),
                         reads=[PB[1], B("epsc")], pwrites=[B("rsE")])

            def p_front_b(t):
                e0 = max(t * T - 8, 0)
                e1 = min((t + 1) * T + 8, S)
                c0 = e0 - (t * T - 8)
                c1 = c0 + (e1 - e0)
                P.op(DVE, lambda e, c0=c0, c1=c1: e.reciprocal(out=rsE[:, c0:c1], in_=rsE[:, c0:c1]), reads=[B("rsE")], writes=[B("rsE")])
                P.op(DVE, lambda e, c0=c0, c1=c1: e.tensor_tensor(out=tE[:, :, c0:c1], in0=xe[:, :, c0:c1],
                                                                  in1=rsE[:, c0:c1].unsqueeze(1).broadcast_to([128, KC, c1 - c0]), op=ALU.mult),
                     reads=[B("xe"), B("rsE")], writes=[B("tE")])

            def p_mid(t):
                slot = t % 2
                e0 = max(t * T - 8, 0)
                e1 = min((t + 1) * T + 8, S)
                c0 = e0 - (t * T - 8)
                c1 = c0 + (e1 - e0)
                for c in range(KC):
                    P.op(ACT, lambda e, c=c, c0=c0, c1=c1: e.activation(out=hE[:, c, c0:c1], in_=tE[:, c, c0:c1], func=AF.Identity,
                                                                        bias=shift[:, c:c + 1], scale=vecA[:, i, c:c + 1]),
                         reads=[B("tE"), B("vecA", i), B("modT", l)], writes=[B("hE")] if c == 0 else (), pwrites=[B("hE")] if c else ())
                if c0 > 0:
                    P.op(POOL, lambda e, c0=c0: e.memset(hE[:, :, 0:c0], 0.0), reads=[B("hE")], writes=[B("hE")])
                if c1 < W:
                    P.op(POOL, lambda e, c1=c1: e.memset(hE[:, :, c1:W], 0.0), reads=[B("hE")], writes=[B("hE")])
                for g in range(4):
                    eng = DVE if g < 3 else POOL
                    hg = hE[:, 2 * g:2 * g + 2, :]
                    bufa, bufb = (ua, ub_) if g < 3 else (ua2, ub2)
                    ka, kb = ("ua", "ub") if g < 3 else ("ua2", "ub2")
                    P.op(eng, lambda e, hg=hg, bufa=bufa: e.tensor_tensor(out=bufa[:, :, 1:W], in0=hg[:, :, 0:W - 1], in1=hg[:, :, 1:W], op=ALU.add),
                         reads=[B("hE")], writes=[B(ka)])
                    cur, curk, oth, othk = bufa, ka, bufb, kb
                    lo, hi = 1, W
                    for step in range(g):
                        sh = 1 << step
                        nlo, nhi = lo + sh, hi - sh
                        P.op(eng, lambda e, cur=cur, oth=oth, nlo=nlo, nhi=nhi, sh=sh: e.tensor_tensor(
                            out=oth[:, :, nlo:nhi], in0=cur[:, :, nlo - sh:nhi - sh], in1=cur[:, :, nlo + sh:nhi + sh], op=ALU.add),
                            reads=[B(curk)], writes=[B(othk)])
                        cur, curk, oth, othk = oth, othk, cur, curk
                        lo, hi = nlo, nhi
                    w = POOL_WINDOWS[g]
                    P.op(DVE, lambda e, cur=cur, hg=hg, g=g, w=w: e.scalar_tensor_tensor(
                        out=dT[:, 2 * g:2 * g + 2, :], in0=cur[:, :, 8:8 + T], scalar=1.0 / w, in1=hg[:, :, 8:8 + T],
                        op0=ALU.mult, op1=ALU.subtract), reads=[B(curk), B("hE")], writes=[B("dT", g)])
                    fix = []
                    if t == 0:
                        fix += [(tt, tt + w // 2) for tt in range(w // 2)]
                    if t == NT - 1:
                        fix += [(T - 1 - u, u + 1 + w // 2) for u in range(w // 2 - 1)]
                    for (col, cntv) in fix:
                        P.op(DVE, lambda e, cur=cur, hg=hg, g=g, col=col, cntv=cntv: e.scalar_tensor_tensor(
                            out=dT[:, 2 * g:2 * g + 2, col:col + 1], in0=cur[:, :, 8 + col:9 + col], scalar=1.0 / cntv,
                            in1=hg[:, :, 8 + col:9 + col], op0=ALU.mult, op1=ALU.subtract),
                            reads=[B(curk), B("hE"), B("dT", g)], writes=[B("dT", g)])

            def p_back(t):
                slot = t % 2
                e0 = max(t * T - 8, 0)
                e1 = min((t + 1) * T + 8, S)
                c0 = e0 - (t * T - 8)
                c1 = c0 + (e1 - e0)
                for g in range(4):
                    for dch in range(2):
                        ch = 2 * g + dch
                        yb = 4 + ch % 2
                        for cc in range(2):
                            P.op(PE, lambda e, g=g, dch=dch, cc=cc, yb=yb: e.matmul(
                                ps[yb][:], lhsT=pw[:, g, cc, dch * 128:(dch + 1) * 128], rhs=dT[:, 2 * g + cc, :],
                                start=(cc == 0), stop=(cc == 1)), reads=[B("pw"), B("dT", g)], writes=[PB[yb]])
                        P.op(DVE, lambda e, ch=ch, yb=yb: e.tensor_scalar(out=ysb[:, ch, :], in0=ps[yb][:], scalar1=smallv[:, ch:ch + 1],
                                                                        scalar2=smallv[:, 8 + ch:9 + ch], op0=ALU.add, op1=ALU.mult),
                             reads=[PB[yb], B("smallv")], writes=[YB[ch]])
                        post_stats_sq(ch)
                        post_stats_mm(ch)
                epilogue(xslot[slot], XB(slot), i)
                store_x(t, slot, dst, dst_key)
                pump(8)

            p_front(0)
            p_front_b(0)
            for t in range(NT):
                p_mid(t)
                if t + 1 < NT:
                    p_front(t + 1)
                p_back(t)
                if t + 1 < NT:
                    p_front_b(t + 1)
                if t == 0 and state.get("defer_mod") is not None:
                    compute_mod(state["defer_mod"], RA)
                    state["defer_mod"] = None

        def mla_sublayer(l, sub, dst, dst_key):
            i = l * 3 + sub
            RA.reset()
            cq_all = RA.alloc([2, S], BF16)
            ckvT = RA.alloc([1, S], BF16)[:, 0, :]
            krope = RA.alloc([1, S], BF16)[:, 0, :]
            winx = RA.alloc([KC, 448], BF16)
            wuq = RA.alloc([2, 2048], BF16)
            wuk = RA.alloc([NH, 128], BF16, parts=64)
            wuv = RA.alloc([1, NH * 64], BF16)[:, 0, :]
            wo_ring = [RA.alloc([NH, 128], BF16, parts=64) for _ in range(2)]
            Vg = RA.alloc([NKC, 4, 65], BF16)
            qnope = RA.alloc([1, T], BF16, parts=64)[:, 0, :]
            qlat = [RA.alloc([1, T], BF16)[:, 0, :] for _ in range(2)]
            qrope = [RA.alloc([1, T], BF16)[:, 0, :] for _ in range(2)]
            pT = [RA.alloc([1, T], BF16)[:, 0, :] for _ in range(3)] + [pT_extra]
            osb = RA.alloc([1, T], F32, parts=65)[:, 0, :]
            rc = RA.alloc([1, T], F32, parts=64)[:, 0, :]
            oT = RA.alloc([NH, T], BF16, parts=64)
            tab = RA.alloc([2, T], F32, parts=32)
            t1 = RA.alloc([1, T], F32)[:, 0, :]
            t2 = RA.alloc([1, T], F32)[:, 0, :]
            cq32 = RA.alloc([2, T], F32)
            print("mla region used", RA.off, "of", REGION)
            qn = smallv[:, 16:18]
            kvn = smallv[:, 18:19]
            P.dma(POOL, winx, mla_w_in_x.rearrange("(kc p) n -> p kc n", p=128), writes=[B("winx")])
            P.dma(POOL, wuq, w_uq_x.rearrange("(k p) n -> p k n", p=128), writes=[B("wuq")])
            P.dma(POOL, wuk, w_ukT, writes=[B("wuk")])
            P.dma(POOL, wuv, w_uv, writes=[B("wuv")])
            wov = w_o.rearrange("(h v) d -> v h d", v=64)
            for c_ in range(KC):
                P.dma(POOL, wo_s[c_].rearrange("v (h d) -> v h d", d=128), wov[:, :, c_ * 128:(c_ + 1) * 128], pwrites=[B("wo_s")])
            wo_issued = [0]

            def issue_wo(upto):
                while wo_issued[0] < min(upto, NT * KC):
                    m_ = wo_issued[0]
                    P.dma(SP, wo_ring[m_ % 2], wo_s[m_ % KC].rearrange("v (h d) -> v h d", d=128),
                          reads=[B("wo_s")], writes=[B("wo_ring", m_ % 2)])
                    wo_issued[0] += 1
            hb = [B("hT", c) for c in range(KC)]
            P.op(POOL, lambda e: e.memset(krope, 0.0), writes=[B("krope", t_) for t_ in range(NT)])
            for q_ in range(2):
                P.op(POOL, lambda e, q_=q_: e.memset(qrope[q_], 0.0), writes=[B("qrope", q_)])

            TAB = [B("tab_c"), B("tab_s")]

            def load_tab(t):
                P.dma(SP, tab[:, 0, :], rope_cos[:, t * T:(t + 1) * T], writes=[TAB[0]])
                P.dma(SP, tab[:, 1, :], rope_sin[:, t * T:(t + 1) * T], writes=[TAB[1]])

            def rope_combine(psa, psb, pba, pbb, dst_ap, dst_buf, eng2=POOL):
                P.op(DVE, lambda e: e.tensor_tensor(out=t1[0:32, :], in0=psa, in1=tab[:, 0, :], op=ALU.mult),
                     reads=[pba, TAB[0]], writes=[B("t1")])
                P.op(DVE, lambda e: e.tensor_tensor(out=t2[0:32, :], in0=psb, in1=tab[:, 1, :], op=ALU.mult),
                     reads=[pbb, TAB[1]], writes=[B("t2")])
                P.op(eng2, lambda e: e.tensor_tensor(out=dst_ap, in0=t1[0:32, :], in1=t2[0:32, :], op=ALU.add),
                     reads=[B("t1"), B("t2")], writes=[dst_buf])

            def qside(h, tok):
                for k2 in range(2):
                    P.op(PE, lambda e, k2=k2: e.matmul(ps[4][0:64, :], lhsT=wuq[:, k2, h * 64:(h + 1) * 64], rhs=cq_all[:, k2, tok],
                                                       start=(k2 == 0), stop=(k2 == 1)),
                         reads=[B("wuq"), B("cq_all")], writes=[PB[4]])
                for k2 in range(2):
                    P.op(PE, lambda e, k2=k2: e.matmul(ps[5][0:32, :], lhsT=wuq[:, k2, 1024 + h * 32:1024 + (h + 1) * 32], rhs=cq_all[:, k2, tok],
                                                       start=(k2 == 0), stop=(k2 == 1)),
                         reads=[B("wuq"), B("cq_all")], writes=[PB[5]])
                P.op(DVE, lambda e: e.tensor_copy(out=qnope, in_=ps[4][0:64, :]), reads=[PB[4]], writes=[B("qnope")])
                P.op(DVE, lambda e: e.tensor_tensor(out=t1[0:32, :], in0=ps[5][0:32, :], in1=tab[:, 0, :], op=ALU.mult),
                     reads=[PB[5], TAB[0]], writes=[B("t1")])
                for k2 in range(2):
                    P.op(PE, lambda e, k2=k2: e.matmul(ps[4][0:32, :], lhsT=wuq[:, k2, 1536 + h * 32:1536 + (h + 1) * 32], rhs=cq_all[:, k2, tok],
                                                       start=(k2 == 0), stop=(k2 == 1)),
                         reads=[B("wuq"), B("cq_all")], writes=[PB[4]])
                P.op(PE, lambda e: e.matmul(ps[5][:], lhsT=wuk[:, h, :], rhs=qnope, start=True, stop=True),
                     reads=[B("wuk"), B("qnope")], writes=[PB[5]])
                P.op(DVE, lambda e: e.tensor_tensor(out=t2[0:32, :], in0=ps[4][0:32, :], in1=tab[:, 1, :], op=ALU.mult),
                     reads=[PB[4], TAB[1]], writes=[B("t2")])
                P.op(DVE, lambda e: e.tensor_copy(out=qlat[h % 2], in_=ps[5][:]), reads=[PB[5]], writes=[B("qlat", h % 2)])
                P.op(POOL, lambda e: e.tensor_tensor(out=qrope[h % 2][0:32, :], in0=t1[0:32, :], in1=t2[0:32, :], op=ALU.add),
                     reads=[B("t1"), B("t2")], writes=[B("qrope", h % 2)])
                P.dma(SP, qrope[h % 2][32:64, :], qrope[h % 2][0:32, :], reads=[B("qrope", h % 2)], writes=[B("qrope2", h % 2)])

            for t in range(NT):
                slot = t % 2
                tok = slice(t * T, (t + 1) * T)
                load_x(t, slot)
                load_tab(t)
                prologue(xslot[slot], XB(slot), i, hT, hb)
                outs = [(ps[0][:], 0, 128, PB[0]), (ps[1][:], 128, 128, PB[1]), (ps[2][:], 256, 128, PB[2]),
                        (ps[3][0:32, :], 384, 32, PB[3]), (ps[4][0:32, :], 416, 32, PB[4])]
                for (pap, c0, m, pb) in outs:
                    for kc in range(KC):
                        P.op(PE, lambda e, pap=pap, c0=c0, m=m, kc=kc: e.matmul(pap, lhsT=winx[:, kc, c0:c0 + m], rhs=hT[:, kc, :],
                                                                             start=(kc == 0), stop=(kc == KC - 1)),
                             reads=[B("winx"), hb[kc]], writes=[pb])
                for k2 in range(2):
                    P.op(DVE, lambda e, k2=k2: e.tensor_copy(out=cq32[:, k2, :], in_=ps[k2][:]), reads=[PB[k2]],
                         writes=[B("cq32")] if k2 == 0 else (), pwrites=[B("cq32")] if k2 else ())
                    r, rb = sqr[k2], B("sqr", k2)
                    P.op(ACT, lambda e, k2=k2, r=r: e.activation(out=r, in_=ps[k2][:], func=AF.Square), reads=[PB[k2]], writes=[rb])
                    P.op(PE, lambda e, k2=k2, r=r: e.matmul(ps[7][:], lhsT=ones_bf, rhs=r, start=(k2 == 0), stop=(k2 == 1)),
                         reads=[rb, B("ones_bf")], writes=[PB[7]])
                P.op(ACT, lambda e: e.activation(out=rsB, in_=ps[7][:], func=AF.Sqrt, bias=epsc, scale=1.0 / 256), reads=[PB[7], B("epsc")], writes=[B("rsB")])
                P.op(DVE, lambda e: e.reciprocal(out=rsB, in_=rsB), reads=[B("rsB")], writes=[B("rsB")])
                P.op(DVE, lambda e: e.tensor_tensor(out=cq32, in0=cq32, in1=rsB.unsqueeze(1).broadcast_to([128, 2, T]), op=ALU.mult),
                     reads=[B("cq32"), B("rsB")], writes=[B("cq32")])
                for k2 in range(2):
                    P.op(ACT, lambda e, k2=k2, tok=tok: e.activation(out=cq_all[:, k2, tok], in_=cq32[:, k2, :], func=AF.Identity, scale=qn[:, k2:k2 + 1]),
                         reads=[B("cq32"), B("smallv")], pwrites=[B("cq_all")])
                P.op(DVE, lambda e: e.tensor_copy(out=t1, in_=ps[2][:]), reads=[PB[2]], writes=[B("t1")])
                P.op(ACT, lambda e: e.activation(out=sqr[0], in_=ps[2][:], func=AF.Square), reads=[PB[2]], writes=[B("sqr", 0)])
                P.op(PE, lambda e: e.matmul(ps[7][:], lhsT=ones_bf, rhs=sqr[0], start=True, stop=True), reads=[B("sqr", 0), B("ones_bf")], writes=[PB[7]])
                P.op(ACT, lambda e: e.activation(out=rsB, in_=ps[7][:], func=AF.Sqrt, bias=epsc, scale=1.0 / 128), reads=[PB[7], B("epsc")], writes=[B("rsB")])
                P.op(DVE, lambda e: e.reciprocal(out=rsB, in_=rsB), reads=[B("rsB")], writes=[B("rsB")])
                P.op(DVE, lambda e: e.tensor_tensor(out=t1, in0=t1, in1=rsB, op=ALU.mult), reads=[B("t1"), B("rsB")], writes=[B("t1")])
                P.op(ACT, lambda e, tok=tok: e.activation(out=ckvT[:, tok], in_=t1, func=AF.Identity, scale=kvn[:, 0:1]),
                     reads=[B("t1"), B("smallv")], pwrites=[B("ckvT")])
                rope_combine(ps[3][0:32, :], ps[4][0:32, :], PB[3], PB[4], krope[0:32, tok], B("krope", t))

            kr_all = [B("krope", t) for t in range(NT)]
            P.dma(SP, krope[32:64, :], krope[0:32, :], reads=kr_all, writes=[B("krope2")])
            for t in range(NT):
                slot = t % 2
                tok = slice(t * T, (t + 1) * T)
                load_x(t, slot)
                load_tab(t)
                sc_i = [0]
                pending = [None]
                issue_wo(t * KC + 2)
                qside(0, tok)
                for hg in range(4):
                    P.op(POOL, lambda e: e.memset(Vg[:, :, :, 64:65], 1.0), reads=[B("Vg")], writes=[B("Vg")])
                    for kp in range(NKC // 2):
                        for kk in range(2):
                            kc = kp * 2 + kk
                            P.op(PE, lambda e, kc=kc, kk=kk, hg=hg: e.matmul(ps[4][:, kk * 256:(kk + 1) * 256], lhsT=ckvT[:, kc * 128:(kc + 1) * 128],
                                                                           rhs=wuv[:, hg * 256:(hg + 1) * 256], start=True, stop=True),
                                 reads=[B("ckvT"), B("wuv")], writes=[PB[4]] if kk == 0 else (), pwrites=[PB[4]] if kk else ())
                        P.op(DVE, lambda e, kp=kp: e.tensor_copy(out=Vg[:, 2 * kp:2 * kp + 2, :, 0:64],
                                                                 in_=ps[4][:].rearrange("p (k h v) -> p k h v", k=2, h=4)),
                             reads=[PB[4]], pwrites=[B("Vg")])
                    items = [(hh, kc) for hh in range(4) for kc in range(NKC)]
                    base = sc_i[0]

                    def S_pair(p):
                        for r in range(2):
                            idx = 2 * p + r
                            hh, kc = items[idx]
                            h = hg * 4 + hh
                            sb_ = (base + idx) % 4
                            ql, qlb = qlat[h % 2], B("qlat", h % 2)
                            P.op(PE, lambda e, sb_=sb_, kc=kc, ql=ql: e.matmul(ps[sb_][:], lhsT=ckvT[:, kc * 128:(kc + 1) * 128], rhs=ql,
                                                                             start=True, stop=False),
                                 reads=[B("ckvT"), qlb], writes=[PB[sb_]])
                        for r in range(2):
                            idx = 2 * p + r
                            hh, kc = items[idx]
                            h = hg * 4 + hh
                            sb_ = (base + idx) % 4
                            qr = qrope[h % 2]
                            rd = (kr_all + [B("qrope", h % 2)]) if r == 0 else [B("krope2"), B("qrope2", h % 2)]
                            P.op(PE, lambda e, sb_=sb_, kc=kc, qr=qr, r=r: e.matmul(
                                ps[sb_][:], lhsT=krope[32 * r:32 * r + 32, kc * 128:(kc + 1) * 128], rhs=qr[32 * r:32 * r + 32, :],
                                start=False, stop=True, tile_position=(32 * r, 0)),
                                reads=rd, writes=[PB[sb_]])

                    def make_norm(h, ob):
                        def norm():
                            P.op(DVE, lambda e: e.tensor_copy(out=osb, in_=ps[ob][0:65, :]), reads=[PB[ob]], writes=[B("osb")])
                            P.op(PE, lambda e: e.matmul(ps[5][0:64, :], lhsT=ones_f[64:65, 0:64], rhs=osb[64:65, :], start=True, stop=True),
                                 reads=[B("ones_f"), B("osb")], writes=[PB[5]])
                            P.op(DVE, lambda e: e.reciprocal(out=rc, in_=ps[5][0:64, :]), reads=[PB[5]], writes=[B("rc")])
                            P.op(POOL, lambda e: e.tensor_tensor(out=oT[:, h, :], in0=osb[0:64, :], in1=rc, op=ALU.mult),
                                 reads=[B("osb"), B("rc")], writes=[B("oT", h)])
                        return norm

                    npairs = len(items) // 2
                    S_pair(0)
                    S_pair(1)
                    for idx, (hh, kc) in enumerate(items):
                        h = hg * 4 + hh
                        ob = 6 + h % 2
                        if idx % 2 == 0 and idx >= 2 and idx // 2 + 1 < npairs:
                            S_pair(idx // 2 + 1)
                        if kc == 4 and pending[0] is not None:
                            pending[0]()
                            pending[0] = None
                        if kc == 8 and h + 1 < NH:
                            qside(h + 1, tok)
                        sb_ = (base + idx) % 4
                        p_, pb_ = pT[sb_], B("pT", sb_)
                        P.op(ACT, lambda e, p_=p_, sb_=sb_: e.activation(out=p_, in_=ps[sb_][:], func=AF.Exp, scale=ATTN_SCALE),
                             reads=[PB[sb_]], writes=[pb_])
                        P.op(PE, lambda e, kc=kc, hh=hh, p_=p_, ob=ob: e.matmul(ps[ob][0:65, :], lhsT=Vg[:, kc, hh, :], rhs=p_,
                                                                              start=(kc == 0), stop=(kc == NKC - 1)),
                             reads=[B("Vg"), pb_], writes=[PB[ob]])
                        if kc == NKC - 1:
                            if pending[0] is not None:
                                pending[0]()
                            pending[0] = make_norm(h, ob)
                    sc_i[0] = base + len(items)
                    if hg == 3 and pending[0] is not None:
                        pending[0]()
                        pending[0] = None
                for c in range(KC):
                    m_ = t * KC + c
                    issue_wo(m_ + 1)
                    ws_, wb_ = wo_ring[m_ % 2], B("wo_ring", m_ % 2)
                    yb = 4 + c % 2
                    for h in range(NH):
                        P.op(PE, lambda e, h=h, ws_=ws_, yb=yb: e.matmul(ps[yb][:], lhsT=ws_[:, h, :], rhs=oT[:, h, :], start=(h == 0), stop=(h == NH - 1)),
                             reads=[wb_, B("oT", h)], writes=[PB[yb]])
                    issue_wo(m_ + 3)
                    P.op(DVE, lambda e, c=c, yb=yb: e.tensor_copy(out=ysb[:, c, :], in_=ps[yb][:]), reads=[PB[yb]],
                         writes=[YB[c]])
                    post_stats_sq(c)
                    if c > 0:
                        post_stats_mm(c - 1)
                post_stats_mm(KC - 1)
                epilogue(xslot[slot], XB(slot), i)
                store_x(t, slot, dst, dst_key)
                pump(8)

        layers_needed = sorted({l for (l, s_) in sublayers})
        for k in ffn_ids:
            precast(k)
        if ffn_ids and sublayers[0][1] != 1:
            flush_precast(ffn_ids[0])
        RA.reset()
        state["defer_mod"] = None
        for l in layers_needed:
            if l == 1 and (0, 1) in sublayers:
                state["defer_mod"] = 1
                continue
            RA.reset()
            compute_mod(l, RA)
        for n, (l, sub) in enumerate(sublayers):
            last = (n == len(sublayers) - 1)
            dst, dst_key = (outT, "outT") if last else (xres[n % 2], f"xres{n % 2}")
            rb = [b for k_, b in bufs.items() if k_[0] in REGION_KEYS]
            fop = P.op(POOL, lambda e: e.memset(epsc, EPS), writes=rb + [B("epsc")])
            state["fence_op"] = fop
            nb0 = set(bufs.keys())
            if sub == 1 and l % 2 == 0:
                pool_sublayer(l, sub, dst, dst_key)
            elif sub == 1:
                mla_sublayer(l, sub, dst, dst_key)
            else:
                ffn_sublayer(l, sub, dst, dst_key)
            state["src"], state["src_key"] = dst, dst_key
        pump(len(pq))
        P.op(POOL, lambda e: e.memset(epsc, EPS), reads=[B("outT", t) for t in range(NT)], writes=[B("epsc")])
        P.op(SP, lambda e: e.nop(), reads=[B("epsc")] + [B("outT", t) for t in range(NT)])
        if max_ops is not None:
            P.ops = P.ops[:max_ops]
        P.emit(st)
        print("ops", len(P.ops), "sems", P.n_sems, "waits", P.n_waits)
    return nc


def _rope_tables():
    inv = (1.0 / (np.float32(10000.0) ** (np.arange(0, 32, 2, dtype=np.float32) / np.float32(32)))).astype(np.float32)
    ang = (np.arange(S, dtype=np.float32)[:, None] * inv[None, :]).astype(np.float32)
    cos = np.cos(ang).astype(np.float32).T
    sin = np.sin(ang).astype(np.float32).T
    return (np.ascontiguousarray(np.concatenate([cos, cos], axis=0)),
            np.ascontiguousarray(np.concatenate([-sin, sin], axis=0)))


def _shared_inputs(inp):
    f = lambda a: np.ascontiguousarray(a, dtype=np.float32)
    def vecT(v):
        return np.swapaxes(v.reshape(v.shape[:-1] + (8, 128)), -1, -2)
    ada_b = inp["ada_b"].reshape(2, 9, 1024)
    ada_bT = np.transpose(vecT(ada_b), (0, 2, 1, 3)).reshape(2, 128, 72)
    norm_gT = np.transpose(vecT(inp["norm_g"]), (0, 2, 1, 3)).reshape(2, 128, 48)
    w_in = inp["mla_w_in"][0]
    kr = w_in[:, 384:416]
    krp = np.concatenate([kr[:, 16:32], kr[:, 0:16]], axis=1)
    mla_w_in_x = np.concatenate([w_in[:, :384], kr, krp], axis=1)
    wuq = inp["mla_w_uq"][0]
    nope = wuq[:, :, :64].reshape(256, 1024)
    rope = wuq[:, :, 64:]
    ropep = np.concatenate([rope[:, :, 16:32], rope[:, :, 0:16]], axis=2)
    w_uq_x = np.concatenate([nope, rope.reshape(256, 512), ropep.reshape(256, 512)], axis=1)
    w_ukT = np.transpose(inp["mla_w_uk"][0], (2, 1, 0))
    cos, sin = _rope_tables()
    return {
        "ada_w": f(inp["ada_w"]), "ada_bT": f(ada_bT), "norm_gT": f(norm_gT),
        "ffn_w_in": f(inp["ffn_w_in"]), "ffn_w_out": f(inp["ffn_w_out"]),
        "pool_w": f(inp["pool_w"][0]), "pool_bT": f(vecT(inp["pool_b"][0].reshape(1024))),
        "pool_scT": f(vecT(inp["pool_scale"][0])),
        "mla_w_in_x": f(mla_w_in_x), "q_normT": f(inp["mla_q_norm"][0].reshape(2, 128).T),
        "kv_normT": f(inp["mla_kv_norm"][0].reshape(1, 128).T),
        "w_uq_x": f(w_uq_x), "w_ukT": f(w_ukT), "w_uv": f(inp["mla_w_uv"][0].reshape(128, 1024)),
        "w_o": f(inp["mla_w_o"][0]), "rope_cos": cos, "rope_sin": sin,
    }


FUSED = True
ALL_SUBLAYERS = [(0, 0), (0, 1), (0, 2), (1, 0), (1, 1), (1, 2)]
_NC_CACHE = {}


def run_sublayers(xT_list, c, shared, sublayers, core_ids=None):
    key = tuple(sublayers)
    if key not in _NC_CACHE:
        _NC_CACHE[key] = build_program(list(sublayers))
    nc = _NC_CACHE[key]
    n = len(xT_list)
    in_maps = []
    for b in range(n):
        m = dict(shared)
        m["xT"] = xT_list[b]
        m["cT"] = np.ascontiguousarray(c[b].reshape(8, 128).T, dtype=np.float32)
        in_maps.append(m)
    res = run_bass_kernel_spmd(nc, in_maps, core_ids=list(range(n)) if core_ids is None else core_ids)
    return [r["outT"] for r in res.results]


def kernel(**inputs):
    inp = {k: np.asarray(v) for k, v in inputs.items()}
    x = inp["x"]
    shared = _shared_inputs(inp)
    xT_list = [np.ascontiguousarray(x[b].T) for b in range(x.shape[0])]
    if FUSED:
        outs = run_sublayers(xT_list, inp["c"], shared, ALL_SUBLAYERS)
    else:
        mid = run_sublayers(xT_list, inp["c"], shared, ALL_SUBLAYERS[:3])
        outs = run_sublayers([np.ascontiguousarray(m) for m in mid], inp["c"], shared, ALL_SUBLAYERS[3:])
    return np.stack([np.ascontiguousarray(o.T) for o in outs], axis=0).astype(np.float32)
```

```python
from contextlib import ExitStack
import numpy as np
import concourse.bass as bass
import concourse.mybir as mybir
from concourse.bass_utils import run_bass_kernel_spmd

F32 = mybir.dt.float32
BF16 = mybir.dt.bfloat16
U8 = mybir.dt.uint8
ALU = mybir.AluOpType
AF = mybir.ActivationFunctionType

PE, ACT, DVE, POOL, SP = "pe", "act", "dve", "pool", "sp"

D = 1024
S = 4096
T = 512
NT = S // T
KC = 8
DFF = 2816
FC = 22
NH = 16
EPS = 1e-6
ATTN_SCALE = float(96 ** -0.5)
POOL_WINDOWS = (2, 4, 8, 16)
NKC = S // 128


class Buf:
    __slots__ = ("name", "last_writer", "pwriters", "readers", "dsem", "dcount", "excl")

    def __init__(self, name):
        self.name = name
        self.excl = False
        self.last_writer = None
        self.pwriters = []
        self.readers = []
        self.dsem = None
        self.dcount = 0


class Op:
    __slots__ = ("idx", "eng", "fn", "deps", "is_dma", "sem_buf", "token", "signal")

    def __init__(self, idx, eng, fn, is_dma, sem_buf):
        self.idx = idx
        self.eng = eng
        self.fn = fn
        self.deps = []
        self.is_dma = is_dma
        self.sem_buf = sem_buf
        self.token = None
        self.signal = False


class Prog:
    SEM_ROLL = 6000
    DMA_SEM_ROLL = 2048
    SWDGE_WINDOW = 4

    def __init__(self, nc, same_engine_sync=True):
        self.nc = nc
        self.ops = []
        self.same_engine_sync = same_engine_sync
        self.pool_dmas = []

    def op(self, eng, fn, reads=(), writes=(), pwrites=(), dma=False, sem_buf=None):
        o = Op(len(self.ops), eng, fn, dma, sem_buf)
        if any(b.excl for b in reads):
            writes = list(writes) + [b for b in reads if b.excl and b not in writes and b not in pwrites]
            reads = [b for b in reads if not b.excl]
        deps = {}
        for b in reads:
            if b.last_writer is not None:
                deps[b.last_writer.idx] = b.last_writer
            for w in b.pwriters:
                deps[w.idx] = w
        for b in writes:
            if b.last_writer is not None:
                deps[b.last_writer.idx] = b.last_writer
            for w in b.pwriters:
                deps[w.idx] = w
            for r in b.readers:
                deps[r.idx] = r
        for b in pwrites:
            if b.last_writer is not None:
                deps[b.last_writer.idx] = b.last_writer
            for r in b.readers:
                deps[r.idx] = r
        o.deps = list(deps.values())
        for b in reads:
            b.readers.append(o)
        for b in writes:
            b.last_writer = o
            b.pwriters = []
            b.readers = []
        for b in pwrites:
            b.pwriters.append(o)
        self.ops.append(o)
        return o

    def dma(self, eng, out, in_, reads=(), writes=(), pwrites=(), sem_buf=None):
        if sem_buf is None:
            sem_buf = writes[0] if writes else (pwrites[0] if pwrites else reads[0])
        o = self.op(eng, lambda e: e.dma_start(out=out, in_=in_), reads, writes, pwrites,
                    dma=True, sem_buf=sem_buf)
        if eng == POOL:
            q = self.pool_dmas
            if len(q) >= self.SWDGE_WINDOW:
                o.deps.append(q[-self.SWDGE_WINDOW])
            q.append(o)
        return o

    def emit(self, stack):
        nc = self.nc
        ops = self.ops
        for o in ops:
            if o.is_dma:
                o.signal = True
            for d in o.deps:
                d.signal = True
        eng_sems, eng_cnt, nsem = {}, {}, [0]

        def new_sem(tag):
            nsem[0] += 1
            return stack.enter_context(nc.semaphore(f"s_{tag}_{nsem[0]}"))

        for o in ops:
            if not o.signal:
                continue
            if o.is_dma:
                b = o.sem_buf
                if b.dsem is None or b.dcount >= self.DMA_SEM_ROLL:
                    b.dsem = new_sem("d")
                    b.dcount = 0
                b.dcount += 16
                o.token = (b.dsem, b.dcount)
            else:
                if o.eng not in eng_sems or eng_cnt[o.eng] >= self.SEM_ROLL:
                    eng_sems[o.eng] = new_sem(o.eng)
                    eng_cnt[o.eng] = 0
                eng_cnt[o.eng] += 1
                o.token = (eng_sems[o.eng], eng_cnt[o.eng])
        self.n_sems = nsem[0]
        streams = {}
        for o in ops:
            streams.setdefault(o.eng, []).append(o)
        block = stack.enter_context(nc.Block())
        same = self.same_engine_sync
        nwaits = [0]

        def make(eng_name, lst):
            def body(e):
                waited_eng = {}
                waited_dma = {}
                for o in lst:
                    need = {}
                    need_dma = {}
                    for d in o.deps:
                        if d.is_dma:
                            sem, val = d.token
                            k = id(sem)
                            if waited_dma.get(k, 0) >= val:
                                continue
                            if k not in need_dma or need_dma[k][1] < val:
                                need_dma[k] = (sem, val)
                        else:
                            if d.eng == eng_name and (eng_name == PE or not same):
                                continue
                            if waited_eng.get(d.eng, -1) >= d.idx:
                                continue
                            if d.eng not in need or need[d.eng].idx < d.idx:
                                need[d.eng] = d
                    for k, (sem, val) in need_dma.items():
                        waited_dma[k] = val
                        e.wait_ge(sem, val)
                        nwaits[0] += 1
                    for src, d in need.items():
                        waited_eng[src] = d.idx
                        e.wait_ge(d.token[0], d.token[1])
                        nwaits[0] += 1
                    ins = o.fn(e)
                    if o.signal:
                        ins.then_inc(o.token[0], 16 if o.is_dma else 1)
            return body

        reg = {PE: block.tensor, ACT: block.scalar, DVE: block.vector, POOL: block.gpsimd, SP: block.sync}
        for eng_name, lst in streams.items():
            reg[eng_name](make(eng_name, lst))
        self.n_waits = nwaits[0]


class Arena:
    def __init__(self, tensor, size):
        self.t = tensor
        self.size = size
        self.off = 0

    def reset(self, off=0):
        self.off = off

    def alloc(self, shape, dtype, parts=128):
        esz = 2 if dtype == BF16 else 4
        n = 1
        for s in shape:
            n *= s
        nbytes = (n * esz + 63) // 64 * 64
        assert self.off + nbytes <= self.size, ("arena overflow", self.off, nbytes, self.size)
        ap = self.t[0:parts, self.off:self.off + n * esz].bitcast(dtype)
        self.off += nbytes
        if len(shape) == 2:
            ap = ap.rearrange("p (a b) -> p a b", a=shape[0])
        elif len(shape) == 3:
            ap = ap.rearrange("p (a b c) -> p a b c", a=shape[0], b=shape[1])
        return ap


def build_program(sublayers, debug=False, max_ops=None, marks=None):
    nc = bass.Bass("TRN2", target_bir_lowering=False)

    def din(name, shape, dt=F32):
        return nc.dram_tensor(name, list(shape), dt, kind="ExternalInput").ap()

    xT = din("xT", [D, S])
    cT = din("cT", [128, KC])
    ada_w = din("ada_w", [2, D, 9 * D])
    ada_bT = din("ada_bT", [2, 128, 72])
    norm_gT = din("norm_gT", [2, 128, 48])
    ffn_w_in = din("ffn_w_in", [2, 2, D, 2 * DFF])
    ffn_w_out = din("ffn_w_out", [2, 2, DFF, D])
    pool_w = din("pool_w", [4, 256, 256])
    pool_bT = din("pool_bT", [128, 8])
    pool_scT = din("pool_scT", [128, 8])
    mla_w_in_x = din("mla_w_in_x", [D, 448])
    q_normT = din("q_normT", [128, 2])
    kv_normT = din("kv_normT", [128, 1])
    w_uq_x = din("w_uq_x", [256, 2048])
    w_ukT = din("w_ukT", [64, NH, 128])
    w_uv = din("w_uv", [128, NH * 64])
    w_o = din("w_o", [NH * 64, D])
    rope_cos = din("rope_cos", [32, S])
    rope_sin = din("rope_sin", [32, S])
    outT = nc.dram_tensor("outT", [D, S], F32, kind="ExternalOutput").ap()

    xres = [nc.dram_tensor(f"xres{i}", [D, S], F32).ap() for i in range(2)]
    ffn_ids = sorted({(l, s // 2) for (l, s) in sublayers if s != 1})
    win_s = {k: nc.dram_tensor(f"win_s{k[0]}{k[1]}", [FC, 128, KC * 256], BF16).ap() for k in ffn_ids}
    wout_s = {k: nc.dram_tensor(f"wout_s{k[0]}{k[1]}", [KC, 128, FC * 128], BF16).ap() for k in ffn_ids}
    wo_s = nc.dram_tensor("wo_s", [KC, 64, NH * 128], BF16).ap()

    P = Prog(nc)
    bufs = {}

    REGION_KEYS = {"ada_ring", "actT", "win_ring", "wout_ring", "xe", "hE", "tE", "ua", "ub", "ua2", "ub2", "dT", "pw",
                   "sqE", "rsE", "cq_all", "ckvT", "krope", "winx", "wuq", "wuk", "wuv", "wo_ring", "Vg", "qnope",
                   "qlat", "qrope", "pT", "osb", "rc", "oT", "tab_c", "tab_s", "t1", "t2", "cq32"}
    state = {"fence_op": None}

    def B(*key):
        if key not in bufs:
            b = Buf(str(key))
            if key[0] in REGION_KEYS:
                b.last_writer = state["fence_op"]
            bufs[key] = b
        return bufs[key]

    st = ExitStack()
    with st:
        COMMON = 89 * 1024
        REGION = 118 * 1024
        common_t = st.enter_context(nc.sbuf_tensor("common", [128, COMMON], U8))
        region_t = st.enter_context(nc.sbuf_tensor("region", [128, REGION], U8))
        CA = Arena(common_t, COMMON)
        RA = Arena(region_t, REGION)
        ps = [st.enter_context(nc.psum_tensor(f"ps{i}", [128, 512], F32)) for i in range(8)]
        PB = [B("psum", i) for i in range(8)]
        for b_ in PB:
            b_.excl = True

        xslot = [CA.alloc([KC, T], F32) for _ in range(2)]
        hT = CA.alloc([KC, T], BF16)
        tmp32 = CA.alloc([KC, T], F32)
        sq8 = tmp32.rearrange("p a b -> p (a b)")[:, 0:KC * T // 2].bitcast(BF16).rearrange("p (a b) -> p a b", a=KC)
        ysb = CA.alloc([KC, T], F32)
        rsA = CA.alloc([1, T], F32)[:, 0, :]
        rsB = CA.alloc([1, T], F32)[:, 0, :]
        sg = [CA.alloc([1, T], F32)[:, 0, :] for _ in range(2)]
        sqr = [CA.alloc([1, T], BF16)[:, 0, :] for _ in range(2)]
        ones_bf = CA.alloc([1, 128], BF16)[:, 0, :]
        ones_f = CA.alloc([1, 128], F32)[:, 0, :]
        epsc = CA.alloc([1, 1], F32)[:, 0, :]
        c_sb = CA.alloc([1, KC], F32)[:, 0, :]
        sc_bf = CA.alloc([1, KC], BF16)[:, 0, :]
        modT = [CA.alloc([1, 72], F32)[:, 0, :] for _ in range(2)]
        adab = [CA.alloc([1, 72], F32)[:, 0, :] for _ in range(2)]
        ng = [CA.alloc([1, 48], F32)[:, 0, :] for _ in range(2)]
        vecA = CA.alloc([6, KC], F32)
        vecB = CA.alloc([6, KC], F32)
        smallv = CA.alloc([1, 32], F32)[:, 0, :]
        pT_extra = CA.alloc([1, T], BF16)[:, 0, :]
        print("common arena used", CA.off, "of", COMMON)

        P.op(POOL, lambda e: e.memset(ones_bf, 1.0), writes=[B("ones_bf")])
        P.op(POOL, lambda e: e.memset(ones_f, 1.0), writes=[B("ones_f")])
        P.op(POOL, lambda e: e.memset(epsc, EPS), writes=[B("epsc")])
        P.dma(SP, c_sb, cT, writes=[B("c_sb")])
        for l in range(2):
            P.dma(SP, adab[l], ada_bT[l], writes=[B("adab", l)])
            P.dma(SP, ng[l], norm_gT[l], writes=[B("ng", l)])
        P.dma(SP, smallv[:, 0:8], pool_bT, pwrites=[B("smallv")])
        P.dma(SP, smallv[:, 8:16], pool_scT, pwrites=[B("smallv")])
        P.dma(SP, smallv[:, 16:18], q_normT, pwrites=[B("smallv")])
        P.dma(SP, smallv[:, 18:19], kv_normT, pwrites=[B("smallv")])
        P.op(ACT, lambda e: e.activation(out=sc_bf, in_=c_sb, func=AF.Silu), reads=[B("c_sb")], writes=[B("sc_bf")])

        pq = []

        def precast(k):
            l, f = k
            wi = ffn_w_in[l, f].rearrange("(kc p) n -> p kc n", p=128)
            wo = ffn_w_out[l, f].rearrange("(fc p) n -> p fc n", p=128)
            for j in range(FC):
                dst = win_s[k][j].rearrange("p (kc x) -> p kc x", x=256)
                for gu in range(2):
                    c0 = gu * DFF + j * 128
                    pq.append((("win_s", k), dst[:, :, gu * 128:(gu + 1) * 128], wi[:, :, c0:c0 + 128]))
            for c in range(KC):
                dst = wout_s[k][c].rearrange("p (fc d) -> p fc d", d=128)
                for h0 in (0, 11):
                    pq.append((("wout_s", k), dst[:, h0:h0 + 11, :], wo[:, h0:h0 + 11, c * 128:(c + 1) * 128]))

        def pump(n):
            for _ in range(min(n, len(pq))):
                key, dst, src = pq.pop(0)
                P.dma(POOL, dst, src, pwrites=[B(*key)])

        def flush_precast(k):
            while any(key[1] == k for (key, _, _) in pq):
                pump(1)

        def compute_mod(l, RAm):
            ada_ring = [RAm.alloc([KC, 512], BF16) for _ in range(2)]
            aw = ada_w[l].rearrange("(kc p) n -> p kc n", p=128)
            mps = ps[7]
            for bi in range(18):
                slot = ada_ring[bi % 2]
                sb = B("ada_ring", bi % 2)
                P.dma(POOL, slot, aw[:, :, bi * 512:(bi + 1) * 512], writes=[sb])
                for cl in range(4):
                    gc = bi * 4 + cl
                    for kc in range(KC):
                        first = (gc == 0 and kc == 0)
                        P.op(PE, lambda e, slot=slot, cl=cl, kc=kc, gc=gc: e.matmul(
                            mps[:, gc:gc + 1], lhsT=slot[:, kc, cl * 128:(cl + 1) * 128],
                            rhs=sc_bf[:, kc:kc + 1], start=(kc == 0), stop=(kc == KC - 1)),
                            reads=[sb, B("sc_bf")], writes=[PB[7]] if first else (),
                            pwrites=() if first else [PB[7]])
            P.op(DVE, lambda e: e.tensor_tensor(out=modT[l], in0=mps[:, 0:72], in1=adab[l], op=ALU.add),
                 reads=[PB[7], B("adab", l)], writes=[B("modT", l)])
            for sub in range(3):
                i = l * 3 + sub
                wgt = 1.0 if sub == 1 else 0.5
                scale = modT[l][:, (3 * sub + 1) * 8:(3 * sub + 2) * 8]
                gate = modT[l][:, (3 * sub + 2) * 8:(3 * sub + 3) * 8]
                gpre = ng[l][:, (2 * sub) * 8:(2 * sub + 1) * 8]
                gpost = ng[l][:, (2 * sub + 1) * 8:(2 * sub + 2) * 8]
                P.op(DVE, lambda e, i=i, scale=scale, gpre=gpre: e.scalar_tensor_tensor(
                    out=vecA[:, i, :], in0=scale, scalar=1.0, in1=gpre, op0=ALU.add, op1=ALU.mult),
                    reads=[B("modT", l), B("ng", l)], writes=[B("vecA", i)])
                P.op(DVE, lambda e, i=i, gate=gate, gpost=gpost: e.scalar_tensor_tensor(
                    out=vecB[:, i, :], in0=gate, scalar=1.0, in1=gpost, op0=ALU.add, op1=ALU.mult),
                    reads=[B("modT", l), B("ng", l)], writes=[B("vecB", i)])
                P.op(DVE, lambda e, i=i, wgt=wgt: e.tensor_scalar(
                    out=vecB[:, i, :], in0=vecB[:, i, :], scalar1=wgt, scalar2=None, op0=ALU.mult),
                    reads=[B("vecB", i)], writes=[B("vecB", i)])

        state.update({"src": xT, "src_key": "xT"})

        def dram_tile(ap, t):
            return ap.rearrange("(ch p) s -> p ch s", p=128)[:, :, t * T:(t + 1) * T]

        def XB(slot):
            return [B("xslot", slot, c) for c in range(KC)]

        def load_x(t, slot):
            P.dma(POOL, xslot[slot], dram_tile(state["src"], t), reads=[B(state["src_key"], t)],
                  writes=XB(slot))

        def store_x(t, slot, dst, dst_key):
            P.dma(POOL, dram_tile(dst, t), xslot[slot], reads=XB(slot), writes=[B(dst_key, t)],
                  sem_buf=B("xstore", slot))

        def prologue_steps(xap, xbuf, i, hdst, hbufs):
            l, sub = divmod(i, 3)
            shift = modT[l][:, (3 * sub) * 8:(3 * sub + 1) * 8]

            def s_sq():
                P.op(POOL, lambda e: e.tensor_tensor(out=sq8, in0=xap, in1=xap, op=ALU.mult), reads=xbuf, writes=[B("tmp32")])

            def s_mm():
                for kc in range(KC):
                    P.op(PE, lambda e, kc=kc: e.matmul(ps[6][:], lhsT=ones_bf, rhs=sq8[:, kc, :], start=(kc == 0), stop=(kc == KC - 1)),
                         reads=[B("tmp32"), B("ones_bf")], writes=[PB[6]])

            def s_sqrt():
                P.op(ACT, lambda e: e.activation(out=rsA, in_=ps[6][:], func=AF.Sqrt, bias=epsc, scale=1.0 / D),
                     reads=[PB[6], B("epsc")], writes=[B("rsA")])
                P.op(DVE, lambda e: e.reciprocal(out=rsA, in_=rsA), reads=[B("rsA")], writes=[B("rsA")])

            def s_mul():
                P.op(DVE, lambda e: e.tensor_tensor(out=tmp32, in0=xap, in1=rsA.unsqueeze(1).broadcast_to([128, KC, T]), op=ALU.mult),
                     reads=xbuf + [B("rsA")], writes=[B("tmp32")])

            def mk(c):
                def s_mod():
                    P.op(ACT, lambda e: e.activation(out=hdst[:, c, :], in_=tmp32[:, c, :], func=AF.Identity,
                                                     bias=shift[:, c:c + 1], scale=vecA[:, i, c:c + 1]),
                         reads=[B("tmp32"), B("vecA", i), B("modT", l)], writes=[hbufs[c]])
                return s_mod

            return [s_sq, s_mm, s_sqrt, s_mul] + [mk(c) for c in range(KC)]

        def prologue(xap, xbuf, i, hdst, hbufs):
            for st_ in prologue_steps(xap, xbuf, i, hdst, hbufs):
                st_()

        YB = [B("ysb", c) for c in range(KC)]

        def post_stats_sq(c):
            r, rb = sqr[c % 2], B("sqr", c % 2)
            P.op(POOL, lambda e: e.tensor_tensor(out=r, in0=ysb[:, c, :], in1=ysb[:, c, :], op=ALU.mult), reads=[YB[c]], writes=[rb])

        def post_stats_mm(c):
            r, rb = sqr[c % 2], B("sqr", c % 2)
            P.op(PE, lambda e: e.matmul(ps[7][:], lhsT=ones_bf, rhs=r, start=(c == 0), stop=(c == KC - 1)),
                 reads=[rb, B("ones_bf")], writes=[PB[7]])

        def epilogue(xap, xbuf, i):
            P.op(ACT, lambda e: e.activation(out=rsB, in_=ps[7][:], func=AF.Sqrt, bias=epsc, scale=1.0 / D),
                 reads=[PB[7], B("epsc")], writes=[B("rsB")])
            P.op(DVE, lambda e: e.reciprocal(out=rsB, in_=rsB), reads=[B("rsB")], writes=[B("rsB")])
            P.op(DVE, lambda e: e.tensor_tensor(out=ysb, in0=ysb, in1=rsB.unsqueeze(1).broadcast_to([128, KC, T]), op=ALU.mult),
                 reads=YB + [B("rsB")], writes=YB)
            for c in range(KC):
                P.op(DVE, lambda e, c=c: e.scalar_tensor_tensor(out=xap[:, c, :], in0=ysb[:, c, :], scalar=vecB[:, i, c:c + 1],
                                                                in1=xap[:, c, :], op0=ALU.mult, op1=ALU.add),
                     reads=[YB[c], B("vecB", i), xbuf[c]], writes=[xbuf[c]])

        def ffn_sublayer(l, sub, dst, dst_key):
            i = l * 3 + sub
            k = (l, sub // 2)
            flush_precast(k)
            RA.reset()
            actT = RA.alloc([FC, T], BF16)
            win_ring = [RA.alloc([KC, 256], BF16) for _ in range(3)]
            wout_ring = [RA.alloc([FC, 128], BF16) for _ in range(2)]
            hb = [B("hT", c) for c in range(KC)]
            n_in, n_out = NT * FC, NT * KC
            issued = {"in": 0, "out": 0}

            def issue_in(upto):
                while issued["in"] < min(upto, n_in):
                    n = issued["in"]
                    j_ = n % FC
                    P.dma(SP, win_ring[n % 3], win_s[k][j_].rearrange("p (kc x) -> p kc x", x=256),
                          reads=[B("win_s", k)], writes=[B("win_ring", n % 3)])
                    issued["in"] += 1

            def issue_out(upto):
                while issued["out"] < min(upto, n_out):
                    m = issued["out"]
                    c_ = m % KC
                    P.dma(SP, wout_ring[m % 2], wout_s[k][c_].rearrange("p (fc d) -> p fc d", d=128),
                          reads=[B("wout_s", k)], writes=[B("wout_ring", m % 2)])
                    issued["out"] += 1

            load_x(0, 0)
            issue_in(3)
            for st_ in prologue_steps(xslot[0], XB(0), i, hT, hb):
                st_()
            for t in range(NT):
                slot = t % 2
                xap, xbuf = xslot[slot], XB(slot)
                nxt = []
                if t + 1 < NT:
                    load_x(t + 1, (t + 1) % 2)
                    nxt = prologue_steps(xslot[(t + 1) % 2], XB((t + 1) % 2), i, hT, hb)
                issue_out(t * KC + 2)
                for j in range(FC):
                    n = t * FC + j
                    issue_in(n + 3)
                    wslot, wbuf = win_ring[n % 3], B("win_ring", n % 3)
                    gb, ub = j % 2, 2 + j % 2
                    for kc in range(KC):
                        P.op(PE, lambda e, kc=kc, wslot=wslot, gb=gb: e.matmul(ps[gb][:], lhsT=wslot[:, kc, 0:128], rhs=hT[:, kc, :],
                                                                             start=(kc == 0), stop=(kc == KC - 1)),
                             reads=[wbuf, hb[kc]], writes=[PB[gb]])
                    for kc in range(KC):
                        P.op(PE, lambda e, kc=kc, wslot=wslot, ub=ub: e.matmul(ps[ub][:], lhsT=wslot[:, kc, 128:256], rhs=hT[:, kc, :],
                                                                             start=(kc == 0), stop=(kc == KC - 1)),
                             reads=[wbuf, hb[kc]], writes=[PB[ub]])
                    sgt, sgb = sg[j % 2], B("sg", j % 2)
                    P.op(ACT, lambda e, gb=gb, sgt=sgt: e.activation(out=sgt, in_=ps[gb][:], func=AF.Silu), reads=[PB[gb]], writes=[sgb])
                    P.op(DVE, lambda e, ub=ub, sgt=sgt, j=j: e.tensor_tensor(out=actT[:, j, :], in0=sgt, in1=ps[ub][:], op=ALU.mult),
                         reads=[sgb, PB[ub]], writes=[B("actT", j)])
                    if j == 13 and nxt:
                        nxt.pop(0)()
                for c in range(KC):
                    m = t * KC + c
                    issue_out(m + 2)
                    wslot, wbuf = wout_ring[m % 2], B("wout_ring", m % 2)
                    yb = 4 + c % 2
                    for fc in range(FC):
                        P.op(PE, lambda e, fc=fc, wslot=wslot, yb=yb: e.matmul(ps[yb][:], lhsT=wslot[:, fc, :], rhs=actT[:, fc, :],
                                                                             start=(fc == 0), stop=(fc == FC - 1)),
                             reads=[wbuf, B("actT", fc)], writes=[PB[yb]])
                    P.op(DVE, lambda e, c=c, yb=yb: e.tensor_copy(out=ysb[:, c, :], in_=ps[yb][:]), reads=[PB[yb]], writes=[YB[c]])
                    post_stats_sq(c)
                    if c > 0:
                        post_stats_mm(c - 1)
                    take = {0: 1, 1: 2, 2: 1}.get(c, 2)
                    for _ in range(take):
                        if nxt:
                            nxt.pop(0)()
                while nxt:
                    nxt.pop(0)()
                post_stats_mm(KC - 1)
                epilogue(xap, xbuf, i)
                store_x(t, slot, dst, dst_key)
                pump(8)

        def pool_sublayer(l, sub, dst, dst_key):
            i = l * 3 + sub
            RA.reset()
            W = T + 16
            xe = RA.alloc([KC, W], F32)
            hE = RA.alloc([KC, W], F32)
            tE = RA.alloc([KC, W], F32)
            ua = RA.alloc([2, W], F32)
            ub_ = RA.alloc([2, W], F32)
            ua2 = RA.alloc([2, W], F32)
            ub2 = RA.alloc([2, W], F32)
            dT = RA.alloc([KC, T], BF16)
            pw = RA.alloc([4, 2, 256], BF16)
            sqE = RA.alloc([KC, W], BF16)
            rsE = RA.alloc([1, W], F32)[:, 0, :]
            shift = modT[l][:, (3 * sub) * 8:(3 * sub + 1) * 8]
            P.dma(POOL, pw, pool_w.rearrange("g (cc p) d -> p g cc d", p=128), writes=[B("pw")])
            src = state["src"].rearrange("(ch p) s -> p ch s", p=128)
            def p_front(t):
                slot = t % 2
                e0 = max(t * T - 8, 0)
                e1 = min((t + 1) * T + 8, S)
                c0 = e0 - (t * T - 8)
                c1 = c0 + (e1 - e0)
                e0 = max(t * T - 8, 0)
                e1 = min((t + 1) * T + 8, S)
                c0 = e0 - (t * T - 8)
                c1 = c0 + (e1 - e0)
                rd = [B(state["src_key"], tt) for tt in range(max(t - 1, 0), min(t + 2, NT))]
                P.dma(POOL, xe[:, :, c0:c1], src[:, :, e0:e1], reads=rd, writes=[B("xe")])
                slot = t % 2
                P.op(POOL, lambda e, slot=slot: e.tensor_copy(out=xslot[slot], in_=xe[:, :, 8:8 + T]), reads=[B("xe")], writes=XB(slot))
                P.op(ACT, lambda e, c0=c0, c1=c1: e.activation(out=sqE[:, :, c0:c1], in_=xe[:, :, c0:c1], func=AF.Square),
                     reads=[B("xe")], writes=[B("sqE")])
                for kc in range(KC):
                    P.op(PE, lambda e, kc=kc: e.matmul(ps[0][:], lhsT=ones_bf, rhs=sqE[:, kc, 8:8 + T], start=(kc == 0), stop=(kc == KC - 1)),
                         reads=[B("sqE"), B("ones_bf")], writes=[PB[0]])
                if c0 == 0:
                    for kc in range(KC):
                        P.op(PE, lambda e, kc=kc: e.matmul(ps[1][:, 0:8], lhsT=ones_bf, rhs=sqE[:, kc, 0:8], start=(kc == 0), stop=(kc == KC - 1)),
                             reads=[B("sqE"), B("ones_bf")], writes=[PB[1]])
                if c1 == W:
                    for kc in range(KC):
                        P.op(PE, lambda e, kc=kc: e.matmul(ps[1][:, 8:16], lhsT=ones_bf, rhs=sqE[:, kc, W - 8:W], start=(kc == 0), stop=(kc == KC - 1)),
                             reads=[B("sqE"), B("ones_bf")], writes=[PB[1]] if c0 != 0 else (), pwrites=[PB[1]] if c0 == 0 else ())
                P.op(ACT, lambda e: e.activation(out=rsE[:, 8:8 + T], in_=ps[0][:], func=AF.Sqrt, bias=epsc, scale=1.0 / D),
                     reads=[PB[0], B("epsc")], writes=[B("rsE")])
                if c0 == 0:
                    P.op(ACT, lambda e: e.activation(out=rsE[:, 0:8], in_=ps[1][:, 0:8], func=AF.Sqrt, bias=epsc, scale=1.0 / D),
                         reads=[PB[1], B("epsc")], pwrites=[B("rsE")])
                if c1 == W:
                    P.op(ACT, lambda e: e.activation(out=rsE[:, W - 8:W], in_=ps[1][:, 8:16], func=AF.Sqrt, bias=epsc, scale=1.0 / D),
                         reads=[PB[1], B("epsc")], pwrites=[B("rsE")])

            def p_front_b(t):
                e0 = max(t * T - 8, 0)
                e1 = min((t + 1) * T + 8, S)
                c0 = e0 - (t * T - 8)
                c1 = c0 + (e1 - e0)
                P.op(DVE, lambda e, c0=c0, c1=c1: e.reciprocal(out=rsE[:, c0:c1], in_=rsE[:, c0:c1]), reads=[B("rsE")], writes=[B("rsE")])
                P.op(DVE, lambda e, c0=c0, c1=c1: e.tensor_tensor(out=tE[:, :, c0:c1], in0=xe[:, :, c0:c1],
                                                                  in1=rsE[:, c0:c1].unsqueeze(1).broadcast_to([128, KC, c1 - c0]), op=ALU.mult),
                     reads=[B("xe"), B("rsE")], writes=[B("tE")])

            def p_mid(t):
                slot = t % 2
                e0 = max(t * T - 8, 0)
                e1 = min((t + 1) * T + 8, S)
                c0 = e0 - (t * T - 8)
                c1 = c0 + (e1 - e0)
                for c in range(KC):
                    P.op(ACT, lambda e, c=c, c0=c0, c1=c1: e.activation(out=hE[:, c, c0:c1], in_=tE[:, c, c0:c1], func=AF.Identity,
                                                                        bias=shift[:, c:c + 1], scale=vecA[:, i, c:c + 1]),
                         reads=[B("tE"), B("vecA", i), B("modT", l)], writes=[B("hE")] if c == 0 else (), pwrites=[B("hE")] if c else ())
                if c0 > 0:
                    P.op(POOL, lambda e, c0=c0: e.memset(hE[:, :, 0:c0], 0.0), reads=[B("hE")], writes=[B("hE")])
                if c1 < W:
                    P.op(POOL, lambda e, c1=c1: e.memset(hE[:, :, c1:W], 0.0), reads=[B("hE")], writes=[B("hE")])
                for g in range(4):
                    eng = DVE if g < 3 else POOL
                    hg = hE[:, 2 * g:2 * g + 2, :]
                    bufa, bufb = (ua, ub_) if g < 3 else (ua2, ub2)
                    ka, kb = ("ua", "ub") if g < 3 else ("ua2", "ub2")
                    P.op(eng, lambda e, hg=hg, bufa=bufa: e.tensor_tensor(out=bufa[:, :, 1:W], in0=hg[:, :, 0:W - 1], in1=hg[:, :, 1:W], op=ALU.add),
                         reads=[B("hE")], writes=[B(ka)])
                    cur, curk, oth, othk = bufa, ka, bufb, kb
                    lo, hi = 1, W
                    for step in range(g):
                        sh = 1 << step
                        nlo, nhi = lo + sh, hi - sh
                        P.op(eng, lambda e, cur=cur, oth=oth, nlo=nlo, nhi=nhi, sh=sh: e.tensor_tensor(
                            out=oth[:, :, nlo:nhi], in0=cur[:, :, nlo - sh:nhi - sh], in1=cur[:, :, nlo + sh:nhi + sh], op=ALU.add),
                            reads=[B(curk)], writes=[B(othk)])
                        cur, curk, oth, othk = oth, othk, cur, curk
                        lo, hi = nlo, nhi
                    w = POOL_WINDOWS[g]
                    P.op(DVE, lambda e, cur=cur, hg=hg, g=g, w=w: e.scalar_tensor_tensor(
                        out=dT[:, 2 * g:2 * g + 2, :], in0=cur[:, :, 8:8 + T], scalar=1.0 / w, in1=hg[:, :, 8:8 + T],
                        op0=ALU.mult, op1=ALU.subtract), reads=[B(curk), B("hE")], writes=[B("dT", g)])
                    fix = []
                    if t == 0:
                        fix += [(tt, tt + w // 2) for tt in range(w // 2)]
                    if t == NT - 1:
                        fix += [(T - 1 - u, u + 1 + w // 2) for u in range(w // 2 - 1)]
                    for (col, cntv) in fix:
                        P.op(DVE, lambda e, cur=cur, hg=hg, g=g, col=col, cntv=cntv: e.scalar_tensor_tensor(
                            out=dT[:, 2 * g:2 * g + 2, col:col + 1], in0=cur[:, :, 8 + col:9 + col], scalar=1.0 / cntv,
                            in1=hg[:, :, 8 + col:9 + col], op0=ALU.mult, op1=ALU.subtract),
                            reads=[B(curk), B("hE"), B("dT", g)], writes=[B("dT", g)])

            def p_back(t):
                slot = t % 2
                e0 = max(t * T - 8, 0)
                e1 = min((t + 1) * T + 8, S)
                c0 = e0 - (t * T - 8)
                c1 = c0 + (e1 - e0)
                for g in range(4):
                    for dch in range(2):
                        ch = 2 * g + dch
                        yb = 4 + ch % 2
                        for cc in range(2):
                            P.op(PE, lambda e, g=g, dch=dch, cc=cc, yb=yb: e.matmul(
                                ps[yb][:], lhsT=pw[:, g, cc, dch * 128:(dch + 1) * 128], rhs=dT[:, 2 * g + cc, :],
                                start=(cc == 0), stop=(cc == 1)), reads=[B("pw"), B("dT", g)], writes=[PB[yb]])
                        P.op(DVE, lambda e, ch=ch, yb=yb: e.tensor_scalar(out=ysb[:, ch, :], in0=ps[yb][:], scalar1=smallv[:, ch:ch + 1],
                                                                        scalar2=smallv[:, 8 + ch:9 + ch], op0=ALU.add, op1=ALU.mult),
                             reads=[PB[yb], B("smallv")], writes=[YB[ch]])
                        post_stats_sq(ch)
                        post_stats_mm(ch)
                epilogue(xslot[slot], XB(slot), i)
                store_x(t, slot, dst, dst_key)
                pump(8)

            p_front(0)
            p_front_b(0)
            for t in range(NT):
                p_mid(t)
                if t + 1 < NT:
                    p_front(t + 1)
                p_back(t)
                if t + 1 < NT:
                    p_front_b(t + 1)
                if t == 0 and state.get("defer_mod") is not None:
                    compute_mod(state["defer_mod"], RA)
                    state["defer_mod"] = None

        def mla_sublayer(l, sub, dst, dst_key):
            i = l * 3 + sub
            RA.reset()
            cq_all = RA.alloc([2, S], BF16)
            ckvT = RA.alloc([1, S], BF16)[:, 0, :]
            krope = RA.alloc([1, S], BF16)[:, 0, :]
            winx = RA.alloc([KC, 448], BF16)
            wuq = RA.alloc([2, 2048], BF16)
            wuk = RA.alloc([NH, 128], BF16, parts=64)
            wuv = RA.alloc([1, NH * 64], BF16)[:, 0, :]
            wo_ring = [RA.alloc([NH, 128], BF16) for _ in range(2)]
            Vg = RA.alloc([NKC, 4, 65], BF16)
            qnope = RA.alloc([1, T], BF16, parts=64)[:, 0, :]
            qlat = [RA.alloc([1, T], BF16)[:, 0, :] for _ in range(2)]
            qrope = [RA.alloc([1, T], BF16)[:, 0, :] for _ in range(2)]
            pT = [RA.alloc([1, T], BF16)[:, 0, :] for _ in range(3)] + [pT_extra]
            osb = RA.alloc([1, T], F32, parts=65)[:, 0, :]
            rc = RA.alloc([1, T], F32, parts=64)[:, 0, :]
            oT = RA.alloc([NH, T], BF16)
            tab = RA.alloc([2, T], F32, parts=32)
            t1 = RA.alloc([1, T], F32)[:, 0, :]
            t2 = RA.alloc([1, T], F32)[:, 0, :]
            cq32 = RA.alloc([2, T], F32)
            print("mla region used", RA.off, "of", REGION)
            qn = smallv[:, 16:18]
            kvn = smallv[:, 18:19]
            P.dma(POOL, winx, mla_w_in_x.rearrange("(kc p) n -> p kc n", p=128), writes=[B("winx")])
            P.dma(POOL, wuq, w_uq_x.rearrange("(k p) n -> p k n", p=128), writes=[B("wuq")])
            P.dma(POOL, wuk, w_ukT, writes=[B("wuk")])
            P.dma(POOL, wuv, w_uv, writes=[B("wuv")])
            wov = w_o.rearrange("(h v) d -> v h d", v=64)
            for c_ in range(KC):
                P.dma(POOL, wo_s[c_].rearrange("v (h d) -> v h d", d=128), wov[:, :, c_ * 128:(c_ + 1) * 128], pwrites=[B("wo_s")])
            wo_issued = [0]

            def issue_wo(upto):
                while wo_issued[0] < min(upto, NT * KC):
                    m_ = wo_issued[0]
                    P.dma(SP, wo_ring[m_ % 2][0:64], wo_s[m_ % KC].rearrange("v (h d) -> v h d", d=128),
                          reads=[B("wo_s")], writes=[B("wo_ring", m_ % 2)])
                    wo_issued[0] += 1
            hb = [B("hT", c) for c in range(KC)]
            P.op(POOL, lambda e: e.memset(krope, 0.0), writes=[B("krope", t_) for t_ in range(NT)])
            P.op(POOL, lambda e: e.memset(oT, 0.0), writes=[B("oT", h_) for h_ in range(NH)])
            for r_ in range(2):
                P.op(POOL, lambda e, r_=r_: e.memset(wo_ring[r_], 0.0), writes=[B("wo_ring", r_)])
            for q_ in range(2):
                P.op(POOL, lambda e, q_=q_: e.memset(qrope[q_], 0.0), writes=[B("qrope", q_)])

            TAB = [B("tab_c"), B("tab_s")]

            def load_tab(t):
                P.dma(SP, tab[:, 0, :], rope_cos[:, t * T:(t + 1) * T], writes=[TAB[0]])
                P.dma(SP, tab[:, 1, :], rope_sin[:, t * T:(t + 1) * T], writes=[TAB[1]])

            def rope_combine(psa, psb, pba, pbb, dst_ap, dst_buf, eng2=POOL):
                P.op(DVE, lambda e: e.tensor_tensor(out=t1[0:32, :], in0=psa, in1=tab[:, 0, :], op=ALU.mult),
                     reads=[pba, TAB[0]], writes=[B("t1")])
                P.op(DVE, lambda e: e.tensor_tensor(out=t2[0:32, :], in0=psb, in1=tab[:, 1, :], op=ALU.mult),
                     reads=[pbb, TAB[1]], writes=[B("t2")])
                P.op(eng2, lambda e: e.tensor_tensor(out=dst_ap, in0=t1[0:32, :], in1=t2[0:32, :], op=ALU.add),
                     reads=[B("t1"), B("t2")], writes=[dst_buf])

            def qside(h, tok):
                for k2 in range(2):
                    P.op(PE, lambda e, k2=k2: e.matmul(ps[4][0:64, :], lhsT=wuq[:, k2, h * 64:(h + 1) * 64], rhs=cq_all[:, k2, tok],
                                                       start=(k2 == 0), stop=(k2 == 1)),
                         reads=[B("wuq"), B("cq_all")], writes=[PB[4]])
                for k2 in range(2):
                    P.op(PE, lambda e, k2=k2: e.matmul(ps[5][0:32, :], lhsT=wuq[:, k2, 1024 + h * 32:1024 + (h + 1) * 32], rhs=cq_all[:, k2, tok],
                                                       start=(k2 == 0), stop=(k2 == 1)),
                         reads=[B("wuq"), B("cq_all")], writes=[PB[5]])
                P.op(DVE, lambda e: e.tensor_copy(out=qnope, in_=ps[4][0:64, :]), reads=[PB[4]], writes=[B("qnope")])
                P.op(DVE, lambda e: e.tensor_tensor(out=t1[0:32, :], in0=ps[5][0:32, :], in1=tab[:, 0, :], op=ALU.mult),
                     reads=[PB[5], TAB[0]], writes=[B("t1")])
                for k2 in range(2):
                    P.op(PE, lambda e, k2=k2: e.matmul(ps[4][0:32, :], lhsT=wuq[:, k2, 1536 + h * 32:1536 + (h + 1) * 32], rhs=cq_all[:, k2, tok],
                                                       start=(k2 == 0), stop=(k2 == 1)),
                         reads=[B("wuq"), B("cq_all")], writes=[PB[4]])
                P.op(PE, lambda e: e.matmul(ps[5][:], lhsT=wuk[:, h, :], rhs=qnope, start=True, stop=True),
                     reads=[B("wuk"), B("qnope")], writes=[PB[5]])
                P.op(DVE, lambda e: e.tensor_tensor(out=t2[0:32, :], in0=ps[4][0:32, :], in1=tab[:, 1, :], op=ALU.mult),
                     reads=[PB[4], TAB[1]], writes=[B("t2")])
                P.op(DVE, lambda e: e.tensor_copy(out=qlat[h % 2], in_=ps[5][:]), reads=[PB[5]], writes=[B("qlat", h % 2)])
                P.op(POOL, lambda e: e.tensor_tensor(out=qrope[h % 2][0:32, :], in0=t1[0:32, :], in1=t2[0:32, :], op=ALU.add),
                     reads=[B("t1"), B("t2")], writes=[B("qrope", h % 2)])

            for t in range(NT):
                slot = t % 2
                tok = slice(t * T, (t + 1) * T)
                load_x(t, slot)
                load_tab(t)
                prologue(xslot[slot], XB(slot), i, hT, hb)
                outs = [(ps[0][:], 0, 128, PB[0]), (ps[1][:], 128, 128, PB[1]), (ps[2][:], 256, 128, PB[2]),
                        (ps[3][0:32, :], 384, 32, PB[3]), (ps[4][0:32, :], 416, 32, PB[4])]
                for (pap, c0, m, pb) in outs:
                    for kc in range(KC):
                        P.op(PE, lambda e, pap=pap, c0=c0, m=m, kc=kc: e.matmul(pap, lhsT=winx[:, kc, c0:c0 + m], rhs=hT[:, kc, :],
                                                                             start=(kc == 0), stop=(kc == KC - 1)),
                             reads=[B("winx"), hb[kc]], writes=[pb])
                for k2 in range(2):
                    P.op(DVE, lambda e, k2=k2: e.tensor_copy(out=cq32[:, k2, :], in_=ps[k2][:]), reads=[PB[k2]],
                         writes=[B("cq32")] if k2 == 0 else (), pwrites=[B("cq32")] if k2 else ())
                    r, rb = sqr[k2], B("sqr", k2)
                    P.op(ACT, lambda e, k2=k2, r=r: e.activation(out=r, in_=ps[k2][:], func=AF.Square), reads=[PB[k2]], writes=[rb])
                    P.op(PE, lambda e, k2=k2, r=r: e.matmul(ps[7][:], lhsT=ones_bf, rhs=r, start=(k2 == 0), stop=(k2 == 1)),
                         reads=[rb, B("ones_bf")], writes=[PB[7]])
                P.op(ACT, lambda e: e.activation(out=rsB, in_=ps[7][:], func=AF.Sqrt, bias=epsc, scale=1.0 / 256), reads=[PB[7], B("epsc")], writes=[B("rsB")])
                P.op(DVE, lambda e: e.reciprocal(out=rsB, in_=rsB), reads=[B("rsB")], writes=[B("rsB")])
                P.op(DVE, lambda e: e.tensor_tensor(out=cq32, in0=cq32, in1=rsB.unsqueeze(1).broadcast_to([128, 2, T]), op=ALU.mult),
                     reads=[B("cq32"), B("rsB")], writes=[B("cq32")])
                for k2 in range(2):
                    P.op(ACT, lambda e, k2=k2, tok=tok: e.activation(out=cq_all[:, k2, tok], in_=cq32[:, k2, :], func=AF.Identity, scale=qn[:, k2:k2 + 1]),
                         reads=[B("cq32"), B("smallv")], pwrites=[B("cq_all")])
                P.op(DVE, lambda e: e.tensor_copy(out=t1, in_=ps[2][:]), reads=[PB[2]], writes=[B("t1")])
                P.op(ACT, lambda e: e.activation(out=sqr[0], in_=ps[2][:], func=AF.Square), reads=[PB[2]], writes=[B("sqr", 0)])
                P.op(PE, lambda e: e.matmul(ps[7][:], lhsT=ones_bf, rhs=sqr[0], start=True, stop=True), reads=[B("sqr", 0), B("ones_bf")], writes=[PB[7]])
                P.op(ACT, lambda e: e.activation(out=rsB, in_=ps[7][:], func=AF.Sqrt, bias=epsc, scale=1.0 / 128), reads=[PB[7], B("epsc")], writes=[B("rsB")])
                P.op(DVE, lambda e: e.reciprocal(out=rsB, in_=rsB), reads=[B("rsB")], writes=[B("rsB")])
                P.op(DVE, lambda e: e.tensor_tensor(out=t1, in0=t1, in1=rsB, op=ALU.mult), reads=[B("t1"), B("rsB")], writes=[B("t1")])
                P.op(ACT, lambda e, tok=tok: e.activation(out=ckvT[:, tok], in_=t1, func=AF.Identity, scale=kvn[:, 0:1]),
                     reads=[B("t1"), B("smallv")], pwrites=[B("ckvT")])
                rope_combine(ps[3][0:32, :], ps[4][0:32, :], PB[3], PB[4], krope[0:32, tok], B("krope", t))

            kr_all = [B("krope", t) for t in range(NT)]
            for t in range(NT):
                slot = t % 2
                tok = slice(t * T, (t + 1) * T)
                load_x(t, slot)
                if t == 0:
                    load_tab(t)
                    qside(0, tok)
                sc_i = [0]
                pending = [None]
                issue_wo(t * KC + 2)
                for hg in range(4):
                    P.op(POOL, lambda e: e.memset(Vg[:, :, :, 64:65], 1.0), reads=[B("Vg")], writes=[B("Vg")])
                    for kp in range(NKC // 2):
                        for kk in range(2):
                            kc = kp * 2 + kk
                            P.op(PE, lambda e, kc=kc, kk=kk, hg=hg: e.matmul(ps[4][:, kk * 256:(kk + 1) * 256], lhsT=ckvT[:, kc * 128:(kc + 1) * 128],
                                                                           rhs=wuv[:, hg * 256:(hg + 1) * 256], start=True, stop=True),
                                 reads=[B("ckvT"), B("wuv")], writes=[PB[4]] if kk == 0 else (), pwrites=[PB[4]] if kk else ())
                        P.op(DVE, lambda e, kp=kp: e.tensor_copy(out=Vg[:, 2 * kp:2 * kp + 2, :, 0:64],
                                                                 in_=ps[4][:].rearrange("p (k h v) -> p k h v", k=2, h=4)),
                             reads=[PB[4]], pwrites=[B("Vg")])
                    items = [(hh, kc) for hh in range(4) for kc in range(NKC)]
                    base = sc_i[0]

                    def S_(idx):
                        hh, kc = items[idx]
                        h = hg * 4 + hh
                        sb_ = (base + idx) % 4
                        ql, qlb = qlat[h % 2], B("qlat", h % 2)
                        qr, qrb = qrope[h % 2], B("qrope", h % 2)
                        P.op(PE, lambda e: e.matmul(ps[sb_][:], lhsT=ckvT[:, kc * 128:(kc + 1) * 128], rhs=ql, start=True, stop=False),
                             reads=[B("ckvT"), qlb], writes=[PB[sb_]])
                        P.op(PE, lambda e: e.matmul(ps[sb_][:], lhsT=krope[:, kc * 128:(kc + 1) * 128], rhs=qr, start=False, stop=True),
                             reads=kr_all + [qrb], writes=[PB[sb_]])

                    def make_norm(h, ob):
                        def norm():
                            P.op(DVE, lambda e: e.tensor_copy(out=osb, in_=ps[ob][0:65, :]), reads=[PB[ob]], writes=[B("osb")])
                            P.op(PE, lambda e: e.matmul(ps[5][0:64, :], lhsT=ones_f[64:65, 0:64], rhs=osb[64:65, :], start=True, stop=True),
                                 reads=[B("ones_f"), B("osb")], writes=[PB[5]])
                            P.op(DVE, lambda e: e.reciprocal(out=rc, in_=ps[5][0:64, :]), reads=[PB[5]], writes=[B("rc")])
                            P.op(POOL, lambda e: e.tensor_tensor(out=oT[0:64, h, :], in0=osb[0:64, :], in1=rc, op=ALU.mult),
                                 reads=[B("osb"), B("rc")], writes=[B("oT", h)])
                        return norm

                    LOOK = 3
                    for idx in range(LOOK):
                        S_(idx)
                    for idx, (hh, kc) in enumerate(items):
                        h = hg * 4 + hh
                        ob = 6 + h % 2
                        if idx + LOOK < len(items):
                            S_(idx + LOOK)
                        if kc == 4 and pending[0] is not None:
                            pending[0]()
                            pending[0] = None
                        if kc == 8 and h + 1 < NH:
                            qside(h + 1, tok)
                        if kc == 8 and h + 1 == NH and t + 1 < NT:
                            load_tab(t + 1)
                            qside(0, slice((t + 1) * T, (t + 2) * T))
                        sb_ = (base + idx) % 4
                        p_, pb_ = pT[sb_], B("pT", sb_)
                        P.op(ACT, lambda e, p_=p_, sb_=sb_: e.activation(out=p_, in_=ps[sb_][:], func=AF.Exp, scale=ATTN_SCALE),
                             reads=[PB[sb_]], writes=[pb_])
                        P.op(PE, lambda e, kc=kc, hh=hh, p_=p_, ob=ob: e.matmul(ps[ob][0:65, :], lhsT=Vg[:, kc, hh, :], rhs=p_,
                                                                              start=(kc == 0), stop=(kc == NKC - 1)),
                             reads=[B("Vg"), pb_], writes=[PB[ob]])
                        if kc == NKC - 1:
                            if pending[0] is not None:
                                pending[0]()
                            pending[0] = make_norm(h, ob)
                    sc_i[0] = base + len(items)
                    if hg == 3 and pending[0] is not None:
                        pending[0]()
                        pending[0] = None
                for c in range(KC):
                    m_ = t * KC + c
                    issue_wo(m_ + 1)
                    ws_, wb_ = wo_ring[m_ % 2], B("wo_ring", m_ % 2)
                    yb = 4 + c % 2
                    for h in range(NH):
                        P.op(PE, lambda e, h=h, ws_=ws_, yb=yb: e.matmul(ps[yb][:], lhsT=ws_[:, h, :], rhs=oT[:, h, :], start=(h == 0), stop=(h == NH - 1)),
                             reads=[wb_, B("oT", h)], writes=[PB[yb]])
                    issue_wo(m_ + 3)
                    P.op(DVE, lambda e, c=c, yb=yb: e.tensor_copy(out=ysb[:, c, :], in_=ps[yb][:]), reads=[PB[yb]],
                         writes=[YB[c]])
                    post_stats_sq(c)
                    if c > 0:
                        post_stats_mm(c - 1)
                post_stats_mm(KC - 1)
                epilogue(xslot[slot], XB(slot), i)
                store_x(t, slot, dst, dst_key)
                pump(8)

        layers_needed = sorted({l for (l, s_) in sublayers})
        for k in ffn_ids:
            precast(k)
        if ffn_ids and sublayers[0][1] != 1:
            flush_precast(ffn_ids[0])
        RA.reset()
        state["defer_mod"] = None
        for l in layers_needed:
            if l == 1 and (0, 1) in sublayers:
                state["defer_mod"] = 1
                continue
            RA.reset()
            compute_mod(l, RA)
        for n, (l, sub) in enumerate(sublayers):
            last = (n == len(sublayers) - 1)
            dst, dst_key = (outT, "outT") if last else (xres[n % 2], f"xres{n % 2}")
            rb = [b for k_, b in bufs.items() if k_[0] in REGION_KEYS]
            fop = P.op(POOL, lambda e: e.memset(epsc, EPS), writes=rb + [B("epsc")])
            state["fence_op"] = fop
            nb0 = set(bufs.keys())
            if sub == 1 and l % 2 == 0:
                pool_sublayer(l, sub, dst, dst_key)
            elif sub == 1:
                mla_sublayer(l, sub, dst, dst_key)
            else:
                ffn_sublayer(l, sub, dst, dst_key)
            state["src"], state["src_key"] = dst, dst_key
        pump(len(pq))
        P.op(POOL, lambda e: e.memset(epsc, EPS), reads=[B("outT", t) for t in range(NT)], writes=[B("epsc")])
        P.op(SP, lambda e: e.nop(), reads=[B("epsc")] + [B("outT", t) for t in range(NT)])
        if max_ops is not None:
            P.ops = P.ops[:max_ops]
        P.emit(st)
        print("ops", len(P.ops), "sems", P.n_sems, "waits", P.n_waits)
    return nc


def _rope_tables():
    inv = (1.0 / (np.float32(10000.0) ** (np.arange(0, 32, 2, dtype=np.float32) / np.float32(32)))).astype(np.float32)
    ang = (np.arange(S, dtype=np.float32)[:, None] * inv[None, :]).astype(np.float32)
    cos = np.cos(ang).astype(np.float32).T
    sin = np.sin(ang).astype(np.float32).T
    return (np.ascontiguousarray(np.concatenate([cos, cos], axis=0)),
            np.ascontiguousarray(np.concatenate([-sin, sin], axis=0)))


def _shared_inputs(inp):
    f = lambda a: np.ascontiguousarray(a, dtype=np.float32)
    def vecT(v):
        return np.swapaxes(v.reshape(v.shape[:-1] + (8, 128)), -1, -2)
    ada_b = inp["ada_b"].reshape(2, 9, 1024)
    ada_bT = np.transpose(vecT(ada_b), (0, 2, 1, 3)).reshape(2, 128, 72)
    norm_gT = np.transpose(vecT(inp["norm_g"]), (0, 2, 1, 3)).reshape(2, 128, 48)
    w_in = inp["mla_w_in"][0]
    kr = w_in[:, 384:416]
    krp = np.concatenate([kr[:, 16:32], kr[:, 0:16]], axis=1)
    mla_w_in_x = np.concatenate([w_in[:, :384], kr, krp], axis=1)
    wuq = inp["mla_w_uq"][0]
    nope = wuq[:, :, :64].reshape(256, 1024)
    rope = wuq[:, :, 64:]
    ropep = np.concatenate([rope[:, :, 16:32], rope[:, :, 0:16]], axis=2)
    w_uq_x = np.concatenate([nope, rope.reshape(256, 512), ropep.reshape(256, 512)], axis=1)
    w_ukT = np.transpose(inp["mla_w_uk"][0], (2, 1, 0))
    cos, sin = _rope_tables()
    return {
        "ada_w": f(inp["ada_w"]), "ada_bT": f(ada_bT), "norm_gT": f(norm_gT),
        "ffn_w_in": f(inp["ffn_w_in"]), "ffn_w_out": f(inp["ffn_w_out"]),
        "pool_w": f(inp["pool_w"][0]), "pool_bT": f(vecT(inp["pool_b"][0].reshape(1024))),
        "pool_scT": f(vecT(inp["pool_scale"][0])),
        "mla_w_in_x": f(mla_w_in_x), "q_normT": f(inp["mla_q_norm"][0].reshape(2, 128).T),
        "kv_normT": f(inp["mla_kv_norm"][0].reshape(1, 128).T),
        "w_uq_x": f(w_uq_x), "w_ukT": f(w_ukT), "w_uv": f(inp["mla_w_uv"][0].reshape(128, 1024)),
        "w_o": f(inp["mla_w_o"][0]), "rope_cos": cos, "rope_sin": sin,
    }


FUSED = True
ALL_SUBLAYERS = [(0, 0), (0, 1), (0, 2), (1, 0), (1, 1), (1, 2)]
_NC_CACHE = {}


def run_sublayers(xT_list, c, shared, sublayers, core_ids=None):
    key = tuple(sublayers)
    if key not in _NC_CACHE:
        _NC_CACHE[key] = build_program(list(sublayers))
    nc = _NC_CACHE[key]
    n = len(xT_list)
    in_maps = []
    for b in range(n):
        m = dict(shared)
        m["xT"] = xT_list[b]
        m["cT"] = np.ascontiguousarray(c[b].reshape(8, 128).T, dtype=np.float32)
        in_maps.append(m)
    res = run_bass_kernel_spmd(nc, in_maps, core_ids=list(range(n)) if core_ids is None else core_ids)
    return [r["outT"] for r in res.results]


def kernel(**inputs):
    inp = {k: np.asarray(v) for k, v in inputs.items()}
    x = inp["x"]
    shared = _shared_inputs(inp)
    xT_list = [np.ascontiguousarray(x[b].T) for b in range(x.shape[0])]
    if FUSED:
        outs = run_sublayers(xT_list, inp["c"], shared, ALL_SUBLAYERS)
    else:
        mid = run_sublayers(xT_list, inp["c"], shared, ALL_SUBLAYERS[:3])
        outs = run_sublayers([np.ascontiguousarray(m) for m in mid], inp["c"], shared, ALL_SUBLAYERS[3:])
    return np.stack([np.ascontiguousarray(o.T) for o in outs], axis=0).astype(np.float32)
```

```python
from contextlib import ExitStack
import numpy as np
import concourse.bass as bass
import concourse.mybir as mybir
from concourse.bass_utils import run_bass_kernel_spmd

F32 = mybir.dt.float32
BF16 = mybir.dt.bfloat16
U8 = mybir.dt.uint8
ALU = mybir.AluOpType
AF = mybir.ActivationFunctionType

PE, ACT, DVE, POOL, SP = "pe", "act", "dve", "pool", "sp"

D = 1024
S = 4096
T = 512
NT = S // T
KC = 8
DFF = 2816
FC = 22
NH = 16
EPS = 1e-6
ATTN_SCALE = float(96 ** -0.5)
POOL_WINDOWS = (2, 4, 8, 16)
NKC = S // 128


class Buf:
    __slots__ = ("name", "last_writer", "pwriters", "readers", "dsem", "dcount", "excl")

    def __init__(self, name):
        self.name = name
        self.excl = False
        self.last_writer = None
        self.pwriters = []
        self.readers = []
        self.dsem = None
        self.dcount = 0


class Op:
    __slots__ = ("idx", "eng", "fn", "deps", "is_dma", "sem_buf", "token", "signal")

    def __init__(self, idx, eng, fn, is_dma, sem_buf):
        self.idx = idx
        self.eng = eng
        self.fn = fn
        self.deps = []
        self.is_dma = is_dma
        self.sem_buf = sem_buf
        self.token = None
        self.signal = False


class Prog:
    SEM_ROLL = 6000
    DMA_SEM_ROLL = 2048
    SWDGE_WINDOW = 4

    def __init__(self, nc, same_engine_sync=True):
        self.nc = nc
        self.ops = []
        self.same_engine_sync = same_engine_sync
        self.pool_dmas = []

    def op(self, eng, fn, reads=(), writes=(), pwrites=(), dma=False, sem_buf=None):
        o = Op(len(self.ops), eng, fn, dma, sem_buf)
        if any(b.excl for b in reads):
            writes = list(writes) + [b for b in reads if b.excl and b not in writes and b not in pwrites]
            reads = [b for b in reads if not b.excl]
        deps = {}
        for b in reads:
            if b.last_writer is not None:
                deps[b.last_writer.idx] = b.last_writer
            for w in b.pwriters:
                deps[w.idx] = w
        for b in writes:
            if b.last_writer is not None:
                deps[b.last_writer.idx] = b.last_writer
            for w in b.pwriters:
                deps[w.idx] = w
            for r in b.readers:
                deps[r.idx] = r
        for b in pwrites:
            if b.last_writer is not None:
                deps[b.last_writer.idx] = b.last_writer
            for r in b.readers:
                deps[r.idx] = r
        o.deps = list(deps.values())
        for b in reads:
            b.readers.append(o)
        for b in writes:
            b.last_writer = o
            b.pwriters = []
            b.readers = []
        for b in pwrites:
            b.pwriters.append(o)
        self.ops.append(o)
        return o

    def dma(self, eng, out, in_, reads=(), writes=(), pwrites=(), sem_buf=None):
        if sem_buf is None:
            sem_buf = writes[0] if writes else (pwrites[0] if pwrites else reads[0])
        o = self.op(eng, lambda e: e.dma_start(out=out, in_=in_), reads, writes, pwrites,
                    dma=True, sem_buf=sem_buf)
        if eng == POOL:
            q = self.pool_dmas
            if len(q) >= self.SWDGE_WINDOW:
                o.deps.append(q[-self.SWDGE_WINDOW])
            q.append(o)
        return o

    def emit(self, stack):
        nc = self.nc
        ops = self.ops
        for o in ops:
            if o.is_dma:
                o.signal = True
            for d in o.deps:
                d.signal = True
        eng_sems, eng_cnt, nsem = {}, {}, [0]

        def new_sem(tag):
            nsem[0] += 1
            return stack.enter_context(nc.semaphore(f"s_{tag}_{nsem[0]}"))

        for o in ops:
            if not o.signal:
                continue
            if o.is_dma:
                b = o.sem_buf
                if b.dsem is None or b.dcount >= self.DMA_SEM_ROLL:
                    b.dsem = new_sem("d")
                    b.dcount = 0
                b.dcount += 16
                o.token = (b.dsem, b.dcount)
            else:
                if o.eng not in eng_sems or eng_cnt[o.eng] >= self.SEM_ROLL:
                    eng_sems[o.eng] = new_sem(o.eng)
                    eng_cnt[o.eng] = 0
                eng_cnt[o.eng] += 1
                o.token = (eng_sems[o.eng], eng_cnt[o.eng])
        self.n_sems = nsem[0]
        streams = {}
        for o in ops:
            streams.setdefault(o.eng, []).append(o)
        block = stack.enter_context(nc.Block())
        same = self.same_engine_sync
        nwaits = [0]

        def make(eng_name, lst):
            def body(e):
                waited_eng = {}
                waited_dma = {}
                for o in lst:
                    need = {}
                    need_dma = {}
                    for d in o.deps:
                        if d.is_dma:
                            sem, val = d.token
                            k = id(sem)
                            if waited_dma.get(k, 0) >= val:
                                continue
                            if k not in need_dma or need_dma[k][1] < val:
                                need_dma[k] = (sem, val)
                        else:
                            if d.eng == eng_name and (eng_name == PE or not same):
                                continue
                            if waited_eng.get(d.eng, -1) >= d.idx:
                                continue
                            if d.eng not in need or need[d.eng].idx < d.idx:
                                need[d.eng] = d
                    for k, (sem, val) in need_dma.items():
                        waited_dma[k] = val
                        e.wait_ge(sem, val)
                        nwaits[0] += 1
                    for src, d in need.items():
                        waited_eng[src] = d.idx
                        e.wait_ge(d.token[0], d.token[1])
                        nwaits[0] += 1
                    ins = o.fn(e)
                    if o.signal:
                        ins.then_inc(o.token[0], 16 if o.is_dma else 1)
            return body

        reg = {PE: block.tensor, ACT: block.scalar, DVE: block.vector, POOL: block.gpsimd, SP: block.sync}
        for eng_name, lst in streams.items():
            reg[eng_name](make(eng_name, lst))
        self.n_waits = nwaits[0]


class Arena:
    def __init__(self, tensor, size):
        self.t = tensor
        self.size = size
        self.off = 0

    def reset(self, off=0):
        self.off = off

    def alloc(self, shape, dtype, parts=128):
        esz = 2 if dtype == BF16 else 4
        n = 1
        for s in shape:
            n *= s
        nbytes = (n * esz + 63) // 64 * 64
        assert self.off + nbytes <= self.size, ("arena overflow", self.off, nbytes, self.size)
        ap = self.t[0:parts, self.off:self.off + n * esz].bitcast(dtype)
        self.off += nbytes
        if len(shape) == 2:
            ap = ap.rearrange("p (a b) -> p a b", a=shape[0])
        elif len(shape) == 3:
            ap = ap.rearrange("p (a b c) -> p a b c", a=shape[0], b=shape[1])
        return ap


def build_program(sublayers, debug=False, max_ops=None, marks=None):
    nc = bass.Bass("TRN2", target_bir_lowering=False)

    def din(name, shape, dt=F32):
        return nc.dram_tensor(name, list(shape), dt, kind="ExternalInput").ap()

    xT = din("xT", [D, S])
    cT = din("cT", [128, KC])
    ada_w = din("ada_w", [2, D, 9 * D])
    ada_bT = din("ada_bT", [2, 128, 72])
    norm_gT = din("norm_gT", [2, 128, 48])
    ffn_w_in = din("ffn_w_in", [2, 2, D, 2 * DFF])
    ffn_w_out = din("ffn_w_out", [2, 2, DFF, D])
    pool_w = din("pool_w", [4, 256, 256])
    pool_bT = din("pool_bT", [128, 8])
    pool_scT = din("pool_scT", [128, 8])
    mla_w_in_x = din("mla_w_in_x", [D, 448])
    q_normT = din("q_normT", [128, 2])
    kv_normT = din("kv_normT", [128, 1])
    w_uq_x = din("w_uq_x", [256, 2048])
    w_ukT = din("w_ukT", [64, NH, 128])
    w_uv = din("w_uv", [128, NH * 64])
    w_o = din("w_o", [NH * 64, D])
    rope_cos = din("rope_cos", [32, S])
    rope_sin = din("rope_sin", [32, S])
    outT = nc.dram_tensor("outT", [D, S], F32, kind="ExternalOutput").ap()

    xres = [nc.dram_tensor(f"xres{i}", [D, S], F32).ap() for i in range(2)]
    ffn_ids = sorted({(l, s // 2) for (l, s) in sublayers if s != 1})
    win_s = {k: nc.dram_tensor(f"win_s{k[0]}{k[1]}", [FC, 128, KC * 256], BF16).ap() for k in ffn_ids}
    wout_s = {k: nc.dram_tensor(f"wout_s{k[0]}{k[1]}", [KC, 128, FC * 128], BF16).ap() for k in ffn_ids}
    wo_s = nc.dram_tensor("wo_s", [KC, 64, NH * 128], BF16).ap()

    P = Prog(nc)
    bufs = {}

    REGION_KEYS = {"ada_ring", "actT", "win_ring", "wout_ring", "xe", "hE", "tE", "ua", "ub", "ua2", "ub2", "dT", "pw",
                   "sqE", "rsE", "cq_all", "ckvT", "krope", "winx", "wuq", "wuk", "wuv", "wo_ring", "Vg", "qnope",
                   "qlat", "qrope", "pT", "osb", "rc", "oT", "tab_c", "tab_s", "t1", "t2", "cq32"}
    state = {"fence_op": None}

    def B(*key):
        if key not in bufs:
            b = Buf(str(key))
            if key[0] in REGION_KEYS:
                b.last_writer = state["fence_op"]
            bufs[key] = b
        return bufs[key]

    st = ExitStack()
    with st:
        COMMON = 89 * 1024
        REGION = 118 * 1024
        common_t = st.enter_context(nc.sbuf_tensor("common", [128, COMMON], U8))
        region_t = st.enter_context(nc.sbuf_tensor("region", [128, REGION], U8))
        CA = Arena(common_t, COMMON)
        RA = Arena(region_t, REGION)
        ps = [st.enter_context(nc.psum_tensor(f"ps{i}", [128, 512], F32)) for i in range(8)]
        PB = [B("psum", i) for i in range(8)]
        for b_ in PB:
            b_.excl = True

        xslot = [CA.alloc([KC, T], F32) for _ in range(2)]
        hT = CA.alloc([KC, T], BF16)
        tmp32 = CA.alloc([KC, T], F32)
        sq8 = tmp32.rearrange("p a b -> p (a b)")[:, 0:KC * T // 2].bitcast(BF16).rearrange("p (a b) -> p a b", a=KC)
        ysb = CA.alloc([KC, T], F32)
        rsA = CA.alloc([1, T], F32)[:, 0, :]
        rsB = CA.alloc([1, T], F32)[:, 0, :]
        sg = [CA.alloc([1, T], F32)[:, 0, :] for _ in range(2)]
        sqr = [CA.alloc([1, T], BF16)[:, 0, :] for _ in range(4)]
        ones_bf = CA.alloc([1, 128], BF16)[:, 0, :]
        ones_f = CA.alloc([1, 128], F32)[:, 0, :]
        epsc = CA.alloc([1, 1], F32)[:, 0, :]
        c_sb = CA.alloc([1, KC], F32)[:, 0, :]
        sc_bf = CA.alloc([1, KC], BF16)[:, 0, :]
        modT = [CA.alloc([1, 72], F32)[:, 0, :] for _ in range(2)]
        adab = [CA.alloc([1, 72], F32)[:, 0, :] for _ in range(2)]
        ng = [CA.alloc([1, 48], F32)[:, 0, :] for _ in range(2)]
        vecA = CA.alloc([6, KC], F32)
        vecB = CA.alloc([6, KC], F32)
        smallv = CA.alloc([1, 32], F32)[:, 0, :]
        pT_extra = CA.alloc([1, T], BF16)[:, 0, :]
        print("common arena used", CA.off, "of", COMMON)

        P.op(POOL, lambda e: e.memset(ones_bf, 1.0), writes=[B("ones_bf")])
        P.op(POOL, lambda e: e.memset(ones_f, 1.0), writes=[B("ones_f")])
        P.op(POOL, lambda e: e.memset(epsc, EPS), writes=[B("epsc")])
        P.dma(SP, c_sb, cT, writes=[B("c_sb")])
        for l in range(2):
            P.dma(SP, adab[l], ada_bT[l], writes=[B("adab", l)])
            P.dma(SP, ng[l], norm_gT[l], writes=[B("ng", l)])
        P.dma(SP, smallv[:, 0:8], pool_bT, pwrites=[B("smallv")])
        P.dma(SP, smallv[:, 8:16], pool_scT, pwrites=[B("smallv")])
        P.dma(SP, smallv[:, 16:18], q_normT, pwrites=[B("smallv")])
        P.dma(SP, smallv[:, 18:19], kv_normT, pwrites=[B("smallv")])
        P.op(ACT, lambda e: e.activation(out=sc_bf, in_=c_sb, func=AF.Silu), reads=[B("c_sb")], writes=[B("sc_bf")])

        pq = []

        def precast(k):
            l, f = k
            wi = ffn_w_in[l, f].rearrange("(kc p) n -> p kc n", p=128)
            wo = ffn_w_out[l, f].rearrange("(fc p) n -> p fc n", p=128)
            for j in range(FC):
                dst = win_s[k][j].rearrange("p (kc x) -> p kc x", x=256)
                for gu in range(2):
                    c0 = gu * DFF + j * 128
                    pq.append((("win_s", k), dst[:, :, gu * 128:(gu + 1) * 128], wi[:, :, c0:c0 + 128]))
            for c in range(KC):
                dst = wout_s[k][c].rearrange("p (fc d) -> p fc d", d=128)
                for h0 in (0, 11):
                    pq.append((("wout_s", k), dst[:, h0:h0 + 11, :], wo[:, h0:h0 + 11, c * 128:(c + 1) * 128]))

        def pump(n):
            for _ in range(min(n, len(pq))):
                key, dst, src = pq.pop(0)
                P.dma(POOL, dst, src, pwrites=[B(*key)])

        def flush_precast(k):
            while any(key[1] == k for (key, _, _) in pq):
                pump(1)

        def compute_mod(l, RAm):
            ada_ring = [RAm.alloc([KC, 512], BF16) for _ in range(2)]
            aw = ada_w[l].rearrange("(kc p) n -> p kc n", p=128)
            mps = ps[7]
            for bi in range(18):
                slot = ada_ring[bi % 2]
                sb = B("ada_ring", bi % 2)
                P.dma(POOL, slot, aw[:, :, bi * 512:(bi + 1) * 512], writes=[sb])
                for cl in range(4):
                    gc = bi * 4 + cl
                    for kc in range(KC):
                        first = (gc == 0 and kc == 0)
                        P.op(PE, lambda e, slot=slot, cl=cl, kc=kc, gc=gc: e.matmul(
                            mps[:, gc:gc + 1], lhsT=slot[:, kc, cl * 128:(cl + 1) * 128],
                            rhs=sc_bf[:, kc:kc + 1], start=(kc == 0), stop=(kc == KC - 1)),
                            reads=[sb, B("sc_bf")], writes=[PB[7]] if first else (),
                            pwrites=() if first else [PB[7]])
            P.op(DVE, lambda e: e.tensor_tensor(out=modT[l], in0=mps[:, 0:72], in1=adab[l], op=ALU.add),
                 reads=[PB[7], B("adab", l)], writes=[B("modT", l)])
            for sub in range(3):
                i = l * 3 + sub
                wgt = 1.0 if sub == 1 else 0.5
                scale = modT[l][:, (3 * sub + 1) * 8:(3 * sub + 2) * 8]
                gate = modT[l][:, (3 * sub + 2) * 8:(3 * sub + 3) * 8]
                gpre = ng[l][:, (2 * sub) * 8:(2 * sub + 1) * 8]
                gpost = ng[l][:, (2 * sub + 1) * 8:(2 * sub + 2) * 8]
                P.op(DVE, lambda e, i=i, scale=scale, gpre=gpre: e.scalar_tensor_tensor(
                    out=vecA[:, i, :], in0=scale, scalar=1.0, in1=gpre, op0=ALU.add, op1=ALU.mult),
                    reads=[B("modT", l), B("ng", l)], writes=[B("vecA", i)])
                P.op(DVE, lambda e, i=i, gate=gate, gpost=gpost: e.scalar_tensor_tensor(
                    out=vecB[:, i, :], in0=gate, scalar=1.0, in1=gpost, op0=ALU.add, op1=ALU.mult),
                    reads=[B("modT", l), B("ng", l)], writes=[B("vecB", i)])
                P.op(DVE, lambda e, i=i, wgt=wgt: e.tensor_scalar(
                    out=vecB[:, i, :], in0=vecB[:, i, :], scalar1=wgt, scalar2=None, op0=ALU.mult),
                    reads=[B("vecB", i)], writes=[B("vecB", i)])

        state.update({"src": xT, "src_key": "xT"})

        def dram_tile(ap, t):
            return ap.rearrange("(ch p) s -> p ch s", p=128)[:, :, t * T:(t + 1) * T]

        def XB(slot):
            return [B("xslot", slot, c) for c in range(KC)]

        def load_x(t, slot):
            P.dma(POOL, xslot[slot], dram_tile(state["src"], t), reads=[B(state["src_key"], t)],
                  writes=XB(slot))

        def store_x(t, slot, dst, dst_key):
            P.dma(POOL, dram_tile(dst, t), xslot[slot], reads=XB(slot), writes=[B(dst_key, t)],
                  sem_buf=B("xstore", slot))

        def prologue_steps(xap, xbuf, i, hdst, hbufs):
            l, sub = divmod(i, 3)
            shift = modT[l][:, (3 * sub) * 8:(3 * sub + 1) * 8]

            def s_sq():
                P.op(POOL, lambda e: e.tensor_tensor(out=sq8, in0=xap, in1=xap, op=ALU.mult), reads=xbuf, writes=[B("tmp32")])

            def s_mm():
                for kc in range(KC):
                    P.op(PE, lambda e, kc=kc: e.matmul(ps[6][:], lhsT=ones_bf, rhs=sq8[:, kc, :], start=(kc == 0), stop=(kc == KC - 1)),
                         reads=[B("tmp32"), B("ones_bf")], writes=[PB[6]])

            def s_sqrt():
                P.op(ACT, lambda e: e.activation(out=rsA, in_=ps[6][:], func=AF.Sqrt, bias=epsc, scale=1.0 / D),
                     reads=[PB[6], B("epsc")], writes=[B("rsA")])
                P.op(DVE, lambda e: e.reciprocal(out=rsA, in_=rsA), reads=[B("rsA")], writes=[B("rsA")])

            def s_mul():
                P.op(DVE, lambda e: e.tensor_tensor(out=tmp32, in0=xap, in1=rsA.unsqueeze(1).broadcast_to([128, KC, T]), op=ALU.mult),
                     reads=xbuf + [B("rsA")], writes=[B("tmp32")])

            def mk(c):
                def s_mod():
                    P.op(ACT, lambda e: e.activation(out=hdst[:, c, :], in_=tmp32[:, c, :], func=AF.Identity,
                                                     bias=shift[:, c:c + 1], scale=vecA[:, i, c:c + 1]),
                         reads=[B("tmp32"), B("vecA", i), B("modT", l)], writes=[hbufs[c]])
                return s_mod

            return [s_sq, s_mm, s_sqrt, s_mul] + [mk(c) for c in range(KC)]

        def prologue(xap, xbuf, i, hdst, hbufs):
            for st_ in prologue_steps(xap, xbuf, i, hdst, hbufs):
                st_()

        YB = [B("ysb", c) for c in range(KC)]

        def post_stats_sq(c):
            r, rb = sqr[c % 4], B("sqr", c % 4)
            P.op(POOL, lambda e: e.tensor_tensor(out=r, in0=ysb[:, c, :], in1=ysb[:, c, :], op=ALU.mult), reads=[YB[c]], writes=[rb])

        def post_stats_mm(c):
            r, rb = sqr[c % 4], B("sqr", c % 4)
            P.op(PE, lambda e: e.matmul(ps[7][:], lhsT=ones_bf, rhs=r, start=(c == 0), stop=(c == KC - 1)),
                 reads=[rb, B("ones_bf")], writes=[PB[7]])

        def epilogue(xap, xbuf, i):
            P.op(ACT, lambda e: e.activation(out=rsB, in_=ps[7][:], func=AF.Sqrt, bias=epsc, scale=1.0 / D),
                 reads=[PB[7], B("epsc")], writes=[B("rsB")])
            P.op(DVE, lambda e: e.reciprocal(out=rsB, in_=rsB), reads=[B("rsB")], writes=[B("rsB")])
            P.op(DVE, lambda e: e.tensor_tensor(out=ysb, in0=ysb, in1=rsB.unsqueeze(1).broadcast_to([128, KC, T]), op=ALU.mult),
                 reads=YB + [B("rsB")], writes=YB)
            for c in range(KC):
                P.op(DVE, lambda e, c=c: e.scalar_tensor_tensor(out=xap[:, c, :], in0=ysb[:, c, :], scalar=vecB[:, i, c:c + 1],
                                                                in1=xap[:, c, :], op0=ALU.mult, op1=ALU.add),
                     reads=[YB[c], B("vecB", i), xbuf[c]], writes=[xbuf[c]])

        def ffn_sublayer(l, sub, dst, dst_key):
            i = l * 3 + sub
            k = (l, sub // 2)
            flush_precast(k)
            RA.reset()
            actT = RA.alloc([FC, T], BF16)
            win_ring = [RA.alloc([KC, 256], BF16) for _ in range(3)]
            wout_ring = [RA.alloc([FC, 128], BF16) for _ in range(2)]
            hb = [B("hT", c) for c in range(KC)]
            n_in, n_out = NT * FC, NT * KC
            issued = {"in": 0, "out": 0}

            def issue_in(upto):
                while issued["in"] < min(upto, n_in):
                    n = issued["in"]
                    j_ = n % FC
                    P.dma(SP, win_ring[n % 3], win_s[k][j_].rearrange("p (kc x) -> p kc x", x=256),
                          reads=[B("win_s", k)], writes=[B("win_ring", n % 3)])
                    issued["in"] += 1

            def issue_out(upto):
                while issued["out"] < min(upto, n_out):
                    m = issued["out"]
                    c_ = m % KC
                    P.dma(SP, wout_ring[m % 2], wout_s[k][c_].rearrange("p (fc d) -> p fc d", d=128),
                          reads=[B("wout_s", k)], writes=[B("wout_ring", m % 2)])
                    issued["out"] += 1

            load_x(0, 0)
            issue_in(3)
            for st_ in prologue_steps(xslot[0], XB(0), i, hT, hb):
                st_()
            for t in range(NT):
                slot = t % 2
                xap, xbuf = xslot[slot], XB(slot)
                nxt = []
                if t + 1 < NT:
                    load_x(t + 1, (t + 1) % 2)
                    nxt = prologue_steps(xslot[(t + 1) % 2], XB((t + 1) % 2), i, hT, hb)
                issue_out(t * KC + 2)
                for j in range(FC):
                    n = t * FC + j
                    issue_in(n + 3)
                    wslot, wbuf = win_ring[n % 3], B("win_ring", n % 3)
                    gb, ub = j % 2, 2 + j % 2
                    for kc in range(KC):
                        P.op(PE, lambda e, kc=kc, wslot=wslot, gb=gb: e.matmul(ps[gb][:], lhsT=wslot[:, kc, 0:128], rhs=hT[:, kc, :],
                                                                             start=(kc == 0), stop=(kc == KC - 1)),
                             reads=[wbuf, hb[kc]], writes=[PB[gb]])
                    for kc in range(KC):
                        P.op(PE, lambda e, kc=kc, wslot=wslot, ub=ub: e.matmul(ps[ub][:], lhsT=wslot[:, kc, 128:256], rhs=hT[:, kc, :],
                                                                             start=(kc == 0), stop=(kc == KC - 1)),
                             reads=[wbuf, hb[kc]], writes=[PB[ub]])
                    sgt, sgb = sg[j % 2], B("sg", j % 2)
                    P.op(ACT, lambda e, gb=gb, sgt=sgt: e.activation(out=sgt, in_=ps[gb][:], func=AF.Silu), reads=[PB[gb]], writes=[sgb])
                    P.op(DVE, lambda e, ub=ub, sgt=sgt, j=j: e.tensor_tensor(out=actT[:, j, :], in0=sgt, in1=ps[ub][:], op=ALU.mult),
                         reads=[sgb, PB[ub]], writes=[B("actT", j)])
                    if j == 13 and nxt:
                        nxt.pop(0)()
                for c in range(KC):
                    m = t * KC + c
                    issue_out(m + 2)
                    wslot, wbuf = wout_ring[m % 2], B("wout_ring", m % 2)
                    yb = 4 + c % 2
                    for fc in range(FC):
                        P.op(PE, lambda e, fc=fc, wslot=wslot, yb=yb: e.matmul(ps[yb][:], lhsT=wslot[:, fc, :], rhs=actT[:, fc, :],
                                                                             start=(fc == 0), stop=(fc == FC - 1)),
                             reads=[wbuf, B("actT", fc)], writes=[PB[yb]])
                    P.op(DVE, lambda e, c=c, yb=yb: e.tensor_copy(out=ysb[:, c, :], in_=ps[yb][:]), reads=[PB[yb]], writes=[YB[c]])
                    post_stats_sq(c)
                    if c > 0:
                        post_stats_mm(c - 1)
                    take = {0: 1, 1: 2, 2: 1}.get(c, 2)
                    for _ in range(take):
                        if nxt:
                            nxt.pop(0)()
                while nxt:
                    nxt.pop(0)()
                post_stats_mm(KC - 1)
                epilogue(xap, xbuf, i)
                store_x(t, slot, dst, dst_key)
                pump(8)

        def pool_sublayer(l, sub, dst, dst_key):
            i = l * 3 + sub
            RA.reset()
            W = T + 16
            xe = RA.alloc([KC, W], F32)
            hE = RA.alloc([KC, W], F32)
            tE = RA.alloc([KC, W], F32)
            ua = RA.alloc([2, W], F32)
            ub_ = RA.alloc([2, W], F32)
            ua2 = RA.alloc([2, W], F32)
            ub2 = RA.alloc([2, W], F32)
            dT = RA.alloc([KC, T], BF16)
            pw = RA.alloc([4, 2, 256], BF16)
            sqE = RA.alloc([KC, W], BF16)
            rsE = RA.alloc([1, W], F32)[:, 0, :]
            shift = modT[l][:, (3 * sub) * 8:(3 * sub + 1) * 8]
            P.dma(POOL, pw, pool_w.rearrange("g (cc p) d -> p g cc d", p=128), writes=[B("pw")])
            src = state["src"].rearrange("(ch p) s -> p ch s", p=128)
            def p_load(t):
                slot = t % 2
                e0 = max(t * T - 8, 0)
                e1 = min((t + 1) * T + 8, S)
                c0 = e0 - (t * T - 8)
                c1 = c0 + (e1 - e0)
                rd = [B(state["src_key"], tt) for tt in range(max(t - 1, 0), min(t + 2, NT))]
                P.dma(POOL, xe[:, :, c0:c1], src[:, :, e0:e1], reads=rd, writes=[B("xe")])

            def p_front(t):
                slot = t % 2
                e0 = max(t * T - 8, 0)
                e1 = min((t + 1) * T + 8, S)
                c0 = e0 - (t * T - 8)
                c1 = c0 + (e1 - e0)
                P.op(ACT, lambda e, c0=c0, c1=c1: e.activation(out=sqE[:, :, c0:c1], in_=xe[:, :, c0:c1], func=AF.Square),
                     reads=[B("xe")], writes=[B("sqE")])
                P.op(ACT, lambda e, slot=slot: e.activation(out=xslot[slot], in_=xe[:, :, 8:8 + T], func=AF.Identity),
                     reads=[B("xe")], writes=XB(slot))
                for kc in range(KC):
                    P.op(PE, lambda e, kc=kc: e.matmul(ps[0][:], lhsT=ones_bf, rhs=sqE[:, kc, 8:8 + T], start=(kc == 0), stop=(kc == KC - 1)),
                         reads=[B("sqE"), B("ones_bf")], writes=[PB[0]])
                if c0 == 0:
                    for kc in range(KC):
                        P.op(PE, lambda e, kc=kc: e.matmul(ps[1][:, 0:8], lhsT=ones_bf, rhs=sqE[:, kc, 0:8], start=(kc == 0), stop=(kc == KC - 1)),
                             reads=[B("sqE"), B("ones_bf")], writes=[PB[1]])
                if c1 == W:
                    for kc in range(KC):
                        P.op(PE, lambda e, kc=kc: e.matmul(ps[1][:, 8:16], lhsT=ones_bf, rhs=sqE[:, kc, W - 8:W], start=(kc == 0), stop=(kc == KC - 1)),
                             reads=[B("sqE"), B("ones_bf")], writes=[PB[1]] if c0 != 0 else (), pwrites=[PB[1]] if c0 == 0 else ())
                P.op(ACT, lambda e: e.activation(out=rsE[:, 8:8 + T], in_=ps[0][:], func=AF.Sqrt, bias=epsc, scale=1.0 / D),
                     reads=[PB[0], B("epsc")], writes=[B("rsE")])
                if c0 == 0:
                    P.op(ACT, lambda e: e.activation(out=rsE[:, 0:8], in_=ps[1][:, 0:8], func=AF.Sqrt, bias=epsc, scale=1.0 / D),
                         reads=[PB[1], B("epsc")], pwrites=[B("rsE")])
                if c1 == W:
                    P.op(ACT, lambda e: e.activation(out=rsE[:, W - 8:W], in_=ps[1][:, 8:16], func=AF.Sqrt, bias=epsc, scale=1.0 / D),
                         reads=[PB[1], B("epsc")], pwrites=[B("rsE")])

            def p_front_b(t):
                e0 = max(t * T - 8, 0)
                e1 = min((t + 1) * T + 8, S)
                c0 = e0 - (t * T - 8)
                c1 = c0 + (e1 - e0)
                P.op(DVE, lambda e, c0=c0, c1=c1: e.reciprocal(out=rsE[:, c0:c1], in_=rsE[:, c0:c1]), reads=[B("rsE")], writes=[B("rsE")])
                P.op(DVE, lambda e, c0=c0, c1=c1: e.tensor_tensor(out=tE[:, :, c0:c1], in0=xe[:, :, c0:c1],
                                                                  in1=rsE[:, c0:c1].unsqueeze(1).broadcast_to([128, KC, c1 - c0]), op=ALU.mult),
                     reads=[B("xe"), B("rsE")], writes=[B("tE")])

            def p_mid(t):
                slot = t % 2
                e0 = max(t * T - 8, 0)
                e1 = min((t + 1) * T + 8, S)
                c0 = e0 - (t * T - 8)
                c1 = c0 + (e1 - e0)
                for c in range(KC):
                    P.op(ACT, lambda e, c=c, c0=c0, c1=c1: e.activation(out=hE[:, c, c0:c1], in_=tE[:, c, c0:c1], func=AF.Identity,
                                                                        bias=shift[:, c:c + 1], scale=vecA[:, i, c:c + 1]),
                         reads=[B("tE"), B("vecA", i), B("modT", l)], writes=[B("hE")] if c == 0 else (), pwrites=[B("hE")] if c else ())
                if c0 > 0:
                    P.op(POOL, lambda e, c0=c0: e.memset(hE[:, :, 0:c0], 0.0), reads=[B("hE")], writes=[B("hE")])
                if c1 < W:
                    P.op(POOL, lambda e, c1=c1: e.memset(hE[:, :, c1:W], 0.0), reads=[B("hE")], writes=[B("hE")])
                for g in range(4):
                    eng = DVE if g < 3 else POOL
                    hg = hE[:, 2 * g:2 * g + 2, :]
                    bufa, bufb = (ua, ub_) if g < 3 else (ua2, ub2)
                    ka, kb = ("ua", "ub") if g < 3 else ("ua2", "ub2")
                    P.op(eng, lambda e, hg=hg, bufa=bufa: e.tensor_tensor(out=bufa[:, :, 1:W], in0=hg[:, :, 0:W - 1], in1=hg[:, :, 1:W], op=ALU.add),
                         reads=[B("hE")], writes=[B(ka)])
                    cur, curk, oth, othk = bufa, ka, bufb, kb
                    lo, hi = 1, W
                    for step in range(g):
                        sh = 1 << step
                        nlo, nhi = lo + sh, hi - sh
                        P.op(eng, lambda e, cur=cur, oth=oth, nlo=nlo, nhi=nhi, sh=sh: e.tensor_tensor(
                            out=oth[:, :, nlo:nhi], in0=cur[:, :, nlo - sh:nhi - sh], in1=cur[:, :, nlo + sh:nhi + sh], op=ALU.add),
                            reads=[B(curk)], writes=[B(othk)])
                        cur, curk, oth, othk = oth, othk, cur, curk
                        lo, hi = nlo, nhi
                    w = POOL_WINDOWS[g]
                    P.op(DVE, lambda e, cur=cur, hg=hg, g=g, w=w: e.scalar_tensor_tensor(
                        out=dT[:, 2 * g:2 * g + 2, :], in0=cur[:, :, 8:8 + T], scalar=1.0 / w, in1=hg[:, :, 8:8 + T],
                        op0=ALU.mult, op1=ALU.subtract), reads=[B(curk), B("hE")], writes=[B("dT", g)])
                    fix = []
                    if t == 0:
                        fix += [(tt, tt + w // 2) for tt in range(w // 2)]
                    if t == NT - 1:
                        fix += [(T - 1 - u, u + 1 + w // 2) for u in range(w // 2 - 1)]
                    for (col, cntv) in fix:
                        P.op(DVE, lambda e, cur=cur, hg=hg, g=g, col=col, cntv=cntv: e.scalar_tensor_tensor(
                            out=dT[:, 2 * g:2 * g + 2, col:col + 1], in0=cur[:, :, 8 + col:9 + col], scalar=1.0 / cntv,
                            in1=hg[:, :, 8 + col:9 + col], op0=ALU.mult, op1=ALU.subtract),
                            reads=[B(curk), B("hE"), B("dT", g)], writes=[B("dT", g)])

            def p_back(t):
                slot = t % 2
                e0 = max(t * T - 8, 0)
                e1 = min((t + 1) * T + 8, S)
                c0 = e0 - (t * T - 8)
                c1 = c0 + (e1 - e0)
                for g in range(4):
                    for dch in range(2):
                        ch = 2 * g + dch
                        yb = 4 + ch % 2
                        for cc in range(2):
                            P.op(PE, lambda e, g=g, dch=dch, cc=cc, yb=yb: e.matmul(
                                ps[yb][:], lhsT=pw[:, g, cc, dch * 128:(dch + 1) * 128], rhs=dT[:, 2 * g + cc, :],
                                start=(cc == 0), stop=(cc == 1)), reads=[B("pw"), B("dT", g)], writes=[PB[yb]])
                        P.op(DVE, lambda e, ch=ch, yb=yb: e.tensor_scalar(out=ysb[:, ch, :], in0=ps[yb][:], scalar1=smallv[:, ch:ch + 1],
                                                                        scalar2=smallv[:, 8 + ch:9 + ch], op0=ALU.add, op1=ALU.mult),
                             reads=[PB[yb], B("smallv")], writes=[YB[ch]])
                        post_stats_sq(ch)
                        if ch >= 2:
                            post_stats_mm(ch - 2)
                post_stats_mm(KC - 2)
                post_stats_mm(KC - 1)
                epilogue(xslot[slot], XB(slot), i)
                store_x(t, slot, dst, dst_key)
                pump(8)

            p_load(0)
            p_front(0)
            p_front_b(0)
            for t in range(NT):
                if t + 1 < NT:
                    p_load(t + 1)
                p_mid(t)
                if t + 1 < NT:
                    p_front(t + 1)
                p_back(t)
                if t + 1 < NT:
                    p_front_b(t + 1)
                if t == 0 and state.get("defer_mod") is not None:
                    compute_mod(state["defer_mod"], RA)
                    state["defer_mod"] = None

        def mla_sublayer(l, sub, dst, dst_key):
            i = l * 3 + sub
            RA.reset()
            cq_all = RA.alloc([2, S], BF16)
            ckvT = RA.alloc([1, S], BF16)[:, 0, :]
            krope = RA.alloc([1, S], BF16)[:, 0, :]
            winx = RA.alloc([KC, 448], BF16)
            wuq = RA.alloc([2, 2048], BF16)
            wuk = RA.alloc([NH, 128], BF16, parts=64)
            wuv = RA.alloc([1, NH * 64], BF16)[:, 0, :]
            wo_ring = [RA.alloc([NH, 128], BF16) for _ in range(2)]
            Vg = RA.alloc([NKC, 4, 65], BF16)
            qnope = RA.alloc([1, T], BF16, parts=64)[:, 0, :]
            qlat = [RA.alloc([1, T], BF16)[:, 0, :] for _ in range(2)]
            qrope = [RA.alloc([1, T], BF16)[:, 0, :] for _ in range(2)]
            pT = [RA.alloc([1, T], BF16)[:, 0, :] for _ in range(3)] + [pT_extra]
            osb = RA.alloc([1, T], F32, parts=65)[:, 0, :]
            rc = RA.alloc([1, T], F32, parts=64)[:, 0, :]
            oT = RA.alloc([NH, T], BF16)
            tab = RA.alloc([2, T], F32, parts=32)
            t1 = RA.alloc([1, T], F32)[:, 0, :]
            t2 = RA.alloc([1, T], F32)[:, 0, :]
            cq32 = RA.alloc([2, T], F32)
            print("mla region used", RA.off, "of", REGION)
            qn = smallv[:, 16:18]
            kvn = smallv[:, 18:19]
            P.dma(POOL, winx, mla_w_in_x.rearrange("(kc p) n -> p kc n", p=128), writes=[B("winx")])
            P.dma(POOL, wuq, w_uq_x.rearrange("(k p) n -> p k n", p=128), writes=[B("wuq")])
            P.dma(POOL, wuk, w_ukT, writes=[B("wuk")])
            P.dma(POOL, wuv, w_uv, writes=[B("wuv")])
            wov = w_o.rearrange("(h v) d -> v h d", v=64)
            for c_ in range(KC):
                P.dma(POOL, wo_s[c_].rearrange("v (h d) -> v h d", d=128), wov[:, :, c_ * 128:(c_ + 1) * 128], pwrites=[B("wo_s")])
            wo_issued = [0]

            def issue_wo(upto):
                while wo_issued[0] < min(upto, NT * KC):
                    m_ = wo_issued[0]
                    P.dma(SP, wo_ring[m_ % 2][0:64], wo_s[m_ % KC].rearrange("v (h d) -> v h d", d=128),
                          reads=[B("wo_s")], writes=[B("wo_ring", m_ % 2)])
                    wo_issued[0] += 1
            hb = [B("hT", c) for c in range(KC)]
            P.op(POOL, lambda e: e.memset(krope, 0.0), writes=[B("krope", t_) for t_ in range(NT)])
            P.op(POOL, lambda e: e.memset(oT, 0.0), writes=[B("oT", h_) for h_ in range(NH)])
            for r_ in range(2):
                P.op(POOL, lambda e, r_=r_: e.memset(wo_ring[r_], 0.0), writes=[B("wo_ring", r_)])
            for q_ in range(2):
                P.op(POOL, lambda e, q_=q_: e.memset(qrope[q_], 0.0), writes=[B("qrope", q_)])

            TAB = [B("tab_c"), B("tab_s")]

            def load_tab(t):
                P.dma(SP, tab[:, 0, :], rope_cos[:, t * T:(t + 1) * T], writes=[TAB[0]])
                P.dma(SP, tab[:, 1, :], rope_sin[:, t * T:(t + 1) * T], writes=[TAB[1]])

            def rope_combine(psa, psb, pba, pbb, dst_ap, dst_buf, eng2=POOL):
                P.op(DVE, lambda e: e.tensor_tensor(out=t1[0:32, :], in0=psa, in1=tab[:, 0, :], op=ALU.mult),
                     reads=[pba, TAB[0]], writes=[B("t1")])
                P.op(DVE, lambda e: e.tensor_tensor(out=t2[0:32, :], in0=psb, in1=tab[:, 1, :], op=ALU.mult),
                     reads=[pbb, TAB[1]], writes=[B("t2")])
                P.op(eng2, lambda e: e.tensor_tensor(out=dst_ap, in0=t1[0:32, :], in1=t2[0:32, :], op=ALU.add),
                     reads=[B("t1"), B("t2")], writes=[dst_buf])

            def qside(h, tok):
                for k2 in range(2):
                    P.op(PE, lambda e, k2=k2: e.matmul(ps[4][0:64, :], lhsT=wuq[:, k2, h * 64:(h + 1) * 64], rhs=cq_all[:, k2, tok],
                                                       start=(k2 == 0), stop=(k2 == 1)),
                         reads=[B("wuq"), B("cq_all")], writes=[PB[4]])
                for k2 in range(2):
                    P.op(PE, lambda e, k2=k2: e.matmul(ps[5][0:32, :], lhsT=wuq[:, k2, 1024 + h * 32:1024 + (h + 1) * 32], rhs=cq_all[:, k2, tok],
                                                       start=(k2 == 0), stop=(k2 == 1)),
                         reads=[B("wuq"), B("cq_all")], writes=[PB[5]])
                P.op(DVE, lambda e: e.tensor_copy(out=qnope, in_=ps[4][0:64, :]), reads=[PB[4]], writes=[B("qnope")])
                P.op(DVE, lambda e: e.tensor_tensor(out=t1[0:32, :], in0=ps[5][0:32, :], in1=tab[:, 0, :], op=ALU.mult),
                     reads=[PB[5], TAB[0]], writes=[B("t1")])
                for k2 in range(2):
                    P.op(PE, lambda e, k2=k2: e.matmul(ps[4][0:32, :], lhsT=wuq[:, k2, 1536 + h * 32:1536 + (h + 1) * 32], rhs=cq_all[:, k2, tok],
                                                       start=(k2 == 0), stop=(k2 == 1)),
                         reads=[B("wuq"), B("cq_all")], writes=[PB[4]])
                P.op(PE, lambda e: e.matmul(ps[5][:], lhsT=wuk[:, h, :], rhs=qnope, start=True, stop=True),
                     reads=[B("wuk"), B("qnope")], writes=[PB[5]])
                P.op(DVE, lambda e: e.tensor_tensor(out=t2[0:32, :], in0=ps[4][0:32, :], in1=tab[:, 1, :], op=ALU.mult),
                     reads=[PB[4], TAB[1]], writes=[B("t2")])
                P.op(DVE, lambda e: e.tensor_copy(out=qlat[h % 2], in_=ps[5][:]), reads=[PB[5]], writes=[B("qlat", h % 2)])
                P.op(POOL, lambda e: e.tensor_tensor(out=qrope[h % 2][0:32, :], in0=t1[0:32, :], in1=t2[0:32, :], op=ALU.add),
                     reads=[B("t1"), B("t2")], writes=[B("qrope", h % 2)])

            for t in range(NT):
                slot = t % 2
                tok = slice(t * T, (t + 1) * T)
                load_x(t, slot)
                load_tab(t)
                prologue(xslot[slot], XB(slot), i, hT, hb)
                outs = [(ps[0][:], 0, 128, PB[0]), (ps[1][:], 128, 128, PB[1]), (ps[2][:], 256, 128, PB[2]),
                        (ps[3][0:32, :], 384, 32, PB[3]), (ps[4][0:32, :], 416, 32, PB[4])]
                for (pap, c0, m, pb) in outs:
                    for kc in range(KC):
                        P.op(PE, lambda e, pap=pap, c0=c0, m=m, kc=kc: e.matmul(pap, lhsT=winx[:, kc, c0:c0 + m], rhs=hT[:, kc, :],
                                                                             start=(kc == 0), stop=(kc == KC - 1)),
                             reads=[B("winx"), hb[kc]], writes=[pb])
                for k2 in range(2):
                    P.op(DVE, lambda e, k2=k2: e.tensor_copy(out=cq32[:, k2, :], in_=ps[k2][:]), reads=[PB[k2]],
                         writes=[B("cq32")] if k2 == 0 else (), pwrites=[B("cq32")] if k2 else ())
                    r, rb = sqr[k2], B("sqr", k2)
                    P.op(ACT, lambda e, k2=k2, r=r: e.activation(out=r, in_=ps[k2][:], func=AF.Square), reads=[PB[k2]], writes=[rb])
                    P.op(PE, lambda e, k2=k2, r=r: e.matmul(ps[7][:], lhsT=ones_bf, rhs=r, start=(k2 == 0), stop=(k2 == 1)),
                         reads=[rb, B("ones_bf")], writes=[PB[7]])
                P.op(ACT, lambda e: e.activation(out=rsB, in_=ps[7][:], func=AF.Sqrt, bias=epsc, scale=1.0 / 256), reads=[PB[7], B("epsc")], writes=[B("rsB")])
                P.op(DVE, lambda e: e.reciprocal(out=rsB, in_=rsB), reads=[B("rsB")], writes=[B("rsB")])
                P.op(DVE, lambda e: e.tensor_tensor(out=cq32, in0=cq32, in1=rsB.unsqueeze(1).broadcast_to([128, 2, T]), op=ALU.mult),
                     reads=[B("cq32"), B("rsB")], writes=[B("cq32")])
                for k2 in range(2):
                    P.op(ACT, lambda e, k2=k2, tok=tok: e.activation(out=cq_all[:, k2, tok], in_=cq32[:, k2, :], func=AF.Identity, scale=qn[:, k2:k2 + 1]),
                         reads=[B("cq32"), B("smallv")], pwrites=[B("cq_all")])
                P.op(DVE, lambda e: e.tensor_copy(out=t1, in_=ps[2][:]), reads=[PB[2]], writes=[B("t1")])
                P.op(ACT, lambda e: e.activation(out=sqr[0], in_=ps[2][:], func=AF.Square), reads=[PB[2]], writes=[B("sqr", 0)])
                P.op(PE, lambda e: e.matmul(ps[7][:], lhsT=ones_bf, rhs=sqr[0], start=True, stop=True), reads=[B("sqr", 0), B("ones_bf")], writes=[PB[7]])
                P.op(ACT, lambda e: e.activation(out=rsB, in_=ps[7][:], func=AF.Sqrt, bias=epsc, scale=1.0 / 128), reads=[PB[7], B("epsc")], writes=[B("rsB")])
                P.op(DVE, lambda e: e.reciprocal(out=rsB, in_=rsB), reads=[B("rsB")], writes=[B("rsB")])
                P.op(DVE, lambda e: e.tensor_tensor(out=t1, in0=t1, in1=rsB, op=ALU.mult), reads=[B("t1"), B("rsB")], writes=[B("t1")])
                P.op(ACT, lambda e, tok=tok: e.activation(out=ckvT[:, tok], in_=t1, func=AF.Identity, scale=kvn[:, 0:1]),
                     reads=[B("t1"), B("smallv")], pwrites=[B("ckvT")])
                rope_combine(ps[3][0:32, :], ps[4][0:32, :], PB[3], PB[4], krope[0:32, tok], B("krope", t))

            kr_all = [B("krope", t) for t in range(NT)]
            for t in range(NT):
                slot = t % 2
                tok = slice(t * T, (t + 1) * T)
                load_x(t, slot)
                if t == 0:
                    load_tab(t)
                    qside(0, tok)
                sc_i = [0]
                pending = [None]
                issue_wo(t * KC + 2)
                for hg in range(4):
                    P.op(POOL, lambda e: e.memset(Vg[:, :, :, 64:65], 1.0), reads=[B("Vg")], writes=[B("Vg")])
                    for kp in range(NKC // 2):
                        for kk in range(2):
                            kc = kp * 2 + kk
                            P.op(PE, lambda e, kc=kc, kk=kk, hg=hg: e.matmul(ps[4][:, kk * 256:(kk + 1) * 256], lhsT=ckvT[:, kc * 128:(kc + 1) * 128],
                                                                           rhs=wuv[:, hg * 256:(hg + 1) * 256], start=True, stop=True),
                                 reads=[B("ckvT"), B("wuv")], writes=[PB[4]] if kk == 0 else (), pwrites=[PB[4]] if kk else ())
                        P.op(DVE, lambda e, kp=kp: e.tensor_copy(out=Vg[:, 2 * kp:2 * kp + 2, :, 0:64],
                                                                 in_=ps[4][:].rearrange("p (k h v) -> p k h v", k=2, h=4)),
                             reads=[PB[4]], pwrites=[B("Vg")])
                    items = [(hh, kc) for hh in range(4) for kc in range(NKC)]
                    base = sc_i[0]

                    def S_(idx):
                        hh, kc = items[idx]
                        h = hg * 4 + hh
                        sb_ = (base + idx) % 4
                        ql, qlb = qlat[h % 2], B("qlat", h % 2)
                        qr, qrb = qrope[h % 2], B("qrope", h % 2)
                        P.op(PE, lambda e: e.matmul(ps[sb_][:], lhsT=ckvT[:, kc * 128:(kc + 1) * 128], rhs=ql, start=True, stop=False),
                             reads=[B("ckvT"), qlb], writes=[PB[sb_]])
                        P.op(PE, lambda e: e.matmul(ps[sb_][:], lhsT=krope[:, kc * 128:(kc + 1) * 128], rhs=qr, start=False, stop=True),
                             reads=kr_all + [qrb], writes=[PB[sb_]])

                    def make_norm(h, ob):
                        def norm():
                            P.op(DVE, lambda e: e.tensor_copy(out=osb, in_=ps[ob][0:65, :]), reads=[PB[ob]], writes=[B("osb")])
                            P.op(PE, lambda e: e.matmul(ps[5][0:64, :], lhsT=ones_f[64:65, 0:64], rhs=osb[64:65, :], start=True, stop=True),
                                 reads=[B("ones_f"), B("osb")], writes=[PB[5]])
                            P.op(DVE, lambda e: e.reciprocal(out=rc, in_=ps[5][0:64, :]), reads=[PB[5]], writes=[B("rc")])
                            P.op(POOL, lambda e: e.tensor_tensor(out=oT[0:64, h, :], in0=osb[0:64, :], in1=rc, op=ALU.mult),
                                 reads=[B("osb"), B("rc")], writes=[B("oT", h)])
                        return norm

                    LOOK = 3
                    for idx in range(LOOK):
                        S_(idx)
                    for idx, (hh, kc) in enumerate(items):
                        h = hg * 4 + hh
                        ob = 6 + h % 2
                        if idx + LOOK < len(items):
                            S_(idx + LOOK)
                        if kc == 4 and pending[0] is not None:
                            pending[0]()
                            pending[0] = None
                        if kc == 8 and h + 1 < NH:
                            qside(h + 1, tok)
                        if kc == 8 and h + 1 == NH and t + 1 < NT:
                            load_tab(t + 1)
                            qside(0, slice((t + 1) * T, (t + 2) * T))
                        sb_ = (base + idx) % 4
                        p_, pb_ = pT[sb_], B("pT", sb_)
                        P.op(ACT, lambda e, p_=p_, sb_=sb_: e.activation(out=p_, in_=ps[sb_][:], func=AF.Exp, scale=ATTN_SCALE),
                             reads=[PB[sb_]], writes=[pb_])
                        P.op(PE, lambda e, kc=kc, hh=hh, p_=p_, ob=ob: e.matmul(ps[ob][0:65, :], lhsT=Vg[:, kc, hh, :], rhs=p_,
                                                                              start=(kc == 0), stop=(kc == NKC - 1)),
                             reads=[B("Vg"), pb_], writes=[PB[ob]])
                        if kc == NKC - 1:
                            if pending[0] is not None:
                                pending[0]()
                            pending[0] = make_norm(h, ob)
                    sc_i[0] = base + len(items)
                    if hg == 3 and pending[0] is not None:
                        pending[0]()
                        pending[0] = None
                for c in range(KC):
                    m_ = t * KC + c
                    issue_wo(m_ + 1)
                    ws_, wb_ = wo_ring[m_ % 2], B("wo_ring", m_ % 2)
                    yb = 4 + c % 2
                    for h in range(NH):
                        P.op(PE, lambda e, h=h, ws_=ws_, yb=yb: e.matmul(ps[yb][:], lhsT=ws_[:, h, :], rhs=oT[:, h, :], start=(h == 0), stop=(h == NH - 1)),
                             reads=[wb_, B("oT", h)], writes=[PB[yb]])
                    issue_wo(m_ + 3)
                    P.op(DVE, lambda e, c=c, yb=yb: e.tensor_copy(out=ysb[:, c, :], in_=ps[yb][:]), reads=[PB[yb]],
                         writes=[YB[c]])
                    post_stats_sq(c)
                    if c > 0:
                        post_stats_mm(c - 1)
                post_stats_mm(KC - 1)
                epilogue(xslot[slot], XB(slot), i)
                store_x(t, slot, dst, dst_key)
                pump(8)

        layers_needed = sorted({l for (l, s_) in sublayers})
        for k in ffn_ids:
            precast(k)
        if ffn_ids and sublayers[0][1] != 1:
            flush_precast(ffn_ids[0])
        RA.reset()
        state["defer_mod"] = None
        for l in layers_needed:
            if l == 1 and (0, 1) in sublayers:
                state["defer_mod"] = 1
                continue
            RA.reset()
            compute_mod(l, RA)
        for n, (l, sub) in enumerate(sublayers):
            last = (n == len(sublayers) - 1)
            dst, dst_key = (outT, "outT") if last else (xres[n % 2], f"xres{n % 2}")
            rb = [b for k_, b in bufs.items() if k_[0] in REGION_KEYS]
            fop = P.op(POOL, lambda e: e.memset(epsc, EPS), writes=rb + [B("epsc")])
            state["fence_op"] = fop
            nb0 = set(bufs.keys())
            if sub == 1 and l % 2 == 0:
                pool_sublayer(l, sub, dst, dst_key)
            elif sub == 1:
                mla_sublayer(l, sub, dst, dst_key)
            else:
                ffn_sublayer(l, sub, dst, dst_key)
            state["src"], state["src_key"] = dst, dst_key
        pump(len(pq))
        P.op(POOL, lambda e: e.memset(epsc, EPS), reads=[B("outT", t) for t in range(NT)], writes=[B("epsc")])
        P.op(SP, lambda e: e.nop(), reads=[B("epsc")] + [B("outT", t) for t in range(NT)])
        if max_ops is not None:
            P.ops = P.ops[:max_ops]
        P.emit(st)
        print("ops", len(P.ops), "sems", P.n_sems, "waits", P.n_waits)
    return nc


def _rope_tables():
    inv = (1.0 / (np.float32(10000.0) ** (np.arange(0, 32, 2, dtype=np.float32) / np.float32(32)))).astype(np.float32)
    ang = (np.arange(S, dtype=np.float32)[:, None] * inv[None, :]).astype(np.float32)
    cos = np.cos(ang).astype(np.float32).T
    sin = np.sin(ang).astype(np.float32).T
    return (np.ascontiguousarray(np.concatenate([cos, cos], axis=0)),
            np.ascontiguousarray(np.concatenate([-sin, sin], axis=0)))


def _shared_inputs(inp):
    f = lambda a: np.ascontiguousarray(a, dtype=np.float32)
    def vecT(v):
        return np.swapaxes(v.reshape(v.shape[:-1] + (8, 128)), -1, -2)
    ada_b = inp["ada_b"].reshape(2, 9, 1024)
    ada_bT = np.transpose(vecT(ada_b), (0, 2, 1, 3)).reshape(2, 128, 72)
    norm_gT = np.transpose(vecT(inp["norm_g"]), (0, 2, 1, 3)).reshape(2, 128, 48)
    w_in = inp["mla_w_in"][0]
    kr = w_in[:, 384:416]
    krp = np.concatenate([kr[:, 16:32], kr[:, 0:16]], axis=1)
    mla_w_in_x = np.concatenate([w_in[:, :384], kr, krp], axis=1)
    wuq = inp["mla_w_uq"][0]
    nope = wuq[:, :, :64].reshape(256, 1024)
    rope = wuq[:, :, 64:]
    ropep = np.concatenate([rope[:, :, 16:32], rope[:, :, 0:16]], axis=2)
    w_uq_x = np.concatenate([nope, rope.reshape(256, 512), ropep.reshape(256, 512)], axis=1)
    w_ukT = np.transpose(inp["mla_w_uk"][0], (2, 1, 0))
    cos, sin = _rope_tables()
    return {
        "ada_w": f(inp["ada_w"]), "ada_bT": f(ada_bT), "norm_gT": f(norm_gT),
        "ffn_w_in": f(inp["ffn_w_in"]), "ffn_w_out": f(inp["ffn_w_out"]),
        "pool_w": f(inp["pool_w"][0]), "pool_bT": f(vecT(inp["pool_b"][0].reshape(1024))),
        "pool_scT": f(vecT(inp["pool_scale"][0])),
        "mla_w_in_x": f(mla_w_in_x), "q_normT": f(inp["mla_q_norm"][0].reshape(2, 128).T),
        "kv_normT": f(inp["mla_kv_norm"][0].reshape(1, 128).T),
        "w_uq_x": f(w_uq_x), "w_ukT": f(w_ukT), "w_uv": f(inp["mla_w_uv"][0].reshape(128, 1024)),
        "w_o": f(inp["mla_w_o"][0]), "rope_cos": cos, "rope_sin": sin,
    }


FUSED = True
ALL_SUBLAYERS = [(0, 0), (0, 1), (0, 2), (1, 0), (1, 1), (1, 2)]
_NC_CACHE = {}


def run_sublayers(xT_list, c, shared, sublayers, core_ids=None):
    key = tuple(sublayers)
    if key not in _NC_CACHE:
        _NC_CACHE[key] = build_program(list(sublayers))
    nc = _NC_CACHE[key]
    n = len(xT_list)
    in_maps = []
    for b in range(n):
        m = dict(shared)
        m["xT"] = xT_list[b]
        m["cT"] = np.ascontiguousarray(c[b].reshape(8, 128).T, dtype=np.float32)
        in_maps.append(m)
    res = run_bass_kernel_spmd(nc, in_maps, core_ids=list(range(n)) if core_ids is None else core_ids)
    return [r["outT"] for r in res.results]


def kernel(**inputs):
    inp = {k: np.asarray(v) for k, v in inputs.items()}
    x = inp["x"]
    shared = _shared_inputs(inp)
    xT_list = [np.ascontiguousarray(x[b].T) for b in range(x.shape[0])]
    if FUSED:
        outs = run_sublayers(xT_list, inp["c"], shared, ALL_SUBLAYERS)
    else:
        mid = run_sublayers(xT_list, inp["c"], shared, ALL_SUBLAYERS[:3])
        outs = run_sublayers([np.ascontiguousarray(m) for m in mid], inp["c"], shared, ALL_SUBLAYERS[3:])
    return np.stack([np.ascontiguousarray(o.T) for o in outs], axis=0).astype(np.float32)
```

```python
from contextlib import ExitStack
import numpy as np
import concourse.bass as bass
import concourse.mybir as mybir
from concourse.bass_utils import run_bass_kernel_spmd

F32 = mybir.dt.float32
BF16 = mybir.dt.bfloat16
U8 = mybir.dt.uint8
ALU = mybir.AluOpType
AF = mybir.ActivationFunctionType

PE, ACT, DVE, POOL, SP = "pe", "act", "dve", "pool", "sp"

D = 1024
S = 4096
T = 512
NT = S // T
KC = 8
DFF = 2816
FC = 22
NH = 16
EPS = 1e-6
ATTN_SCALE = float(96 ** -0.5)
POOL_WINDOWS = (2, 4, 8, 16)
NKC = S // 128


class Buf:
    __slots__ = ("name", "last_writer", "pwriters", "readers", "dsem", "dcount", "excl")

    def __init__(self, name):
        self.name = name
        self.excl = False
        self.last_writer = None
        self.pwriters = []
        self.readers = []
        self.dsem = None
        self.dcount = 0


class Op:
    __slots__ = ("idx", "eng", "fn", "deps", "is_dma", "sem_buf", "token", "signal")

    def __init__(self, idx, eng, fn, is_dma, sem_buf):
        self.idx = idx
        self.eng = eng
        self.fn = fn
        self.deps = []
        self.is_dma = is_dma
        self.sem_buf = sem_buf
        self.token = None
        self.signal = False


class Prog:
    SEM_ROLL = 6000
    DMA_SEM_ROLL = 2048
    SWDGE_WINDOW = 4

    def __init__(self, nc, same_engine_sync=True):
        self.nc = nc
        self.ops = []
        self.same_engine_sync = same_engine_sync
        self.pool_dmas = []
        self.pool_slots = []

    def op(self, eng, fn, reads=(), writes=(), pwrites=(), dma=False, sem_buf=None):
        o = Op(len(self.ops), eng, fn, dma, sem_buf)
        if any(b.excl for b in reads):
            writes = list(writes) + [b for b in reads if b.excl and b not in writes and b not in pwrites]
            reads = [b for b in reads if not b.excl]
        deps = {}
        for b in reads:
            if b.last_writer is not None:
                deps[b.last_writer.idx] = b.last_writer
            for w in b.pwriters:
                deps[w.idx] = w
        for b in writes:
            if b.last_writer is not None:
                deps[b.last_writer.idx] = b.last_writer
            for w in b.pwriters:
                deps[w.idx] = w
            for r in b.readers:
                deps[r.idx] = r
        for b in pwrites:
            if b.last_writer is not None:
                deps[b.last_writer.idx] = b.last_writer
            for r in b.readers:
                deps[r.idx] = r
        o.deps = list(deps.values())
        for b in reads:
            b.readers.append(o)
        for b in writes:
            b.last_writer = o
            b.pwriters = []
            b.readers = []
        for b in pwrites:
            b.pwriters.append(o)
        self.ops.append(o)
        return o

    def dma(self, eng, out, in_, reads=(), writes=(), pwrites=(), sem_buf=None):
        if sem_buf is None:
            sem_buf = writes[0] if writes else (pwrites[0] if pwrites else reads[0])
        if eng == POOL:
            q = self.pool_dmas
            if not self.pool_slots:
                self.pool_slots = [Buf(f"swdge_slot{i_}") for i_ in range(self.SWDGE_WINDOW)]
            sem_buf = self.pool_slots[len(q) % self.SWDGE_WINDOW]
        o = self.op(eng, lambda e: e.dma_start(out=out, in_=in_), reads, writes, pwrites,
                    dma=True, sem_buf=sem_buf)
        if eng == POOL:
            if len(q) >= self.SWDGE_WINDOW:
                o.deps.append(q[-self.SWDGE_WINDOW])
            q.append(o)
        return o

    def emit(self, stack):
        nc = self.nc
        ops = self.ops
        for o in ops:
            if o.is_dma:
                o.signal = True
            for d in o.deps:
                d.signal = True
        eng_sems, eng_cnt, nsem = {}, {}, [0]

        def new_sem(tag):
            nsem[0] += 1
            return stack.enter_context(nc.semaphore(f"s_{tag}_{nsem[0]}"))

        for o in ops:
            if not o.signal:
                continue
            if o.is_dma:
                b = o.sem_buf
                if b.dsem is None or b.dcount >= self.DMA_SEM_ROLL:
                    b.dsem = new_sem("d")
                    b.dcount = 0
                b.dcount += 16
                o.token = (b.dsem, b.dcount)
            else:
                if o.eng not in eng_sems or eng_cnt[o.eng] >= self.SEM_ROLL:
                    eng_sems[o.eng] = new_sem(o.eng)
                    eng_cnt[o.eng] = 0
                eng_cnt[o.eng] += 1
                o.token = (eng_sems[o.eng], eng_cnt[o.eng])
        self.n_sems = nsem[0]
        streams = {}
        for o in ops:
            streams.setdefault(o.eng, []).append(o)
        block = stack.enter_context(nc.Block())
        same = self.same_engine_sync
        nwaits = [0]

        def make(eng_name, lst):
            def body(e):
                waited_eng = {}
                waited_dma = {}
                for o in lst:
                    need = {}
                    need_dma = {}
                    for d in o.deps:
                        if d.is_dma:
                            sem, val = d.token
                            k = id(sem)
                            if waited_dma.get(k, 0) >= val:
                                continue
                            if k not in need_dma or need_dma[k][1] < val:
                                need_dma[k] = (sem, val)
                        else:
                            if d.eng == eng_name and (eng_name == PE or not same):
                                continue
                            if waited_eng.get(d.eng, -1) >= d.idx:
                                continue
                            if d.eng not in need or need[d.eng].idx < d.idx:
                                need[d.eng] = d
                    for k, (sem, val) in need_dma.items():
                        waited_dma[k] = val
                        e.wait_ge(sem, val)
                        nwaits[0] += 1
                    for src, d in need.items():
                        waited_eng[src] = d.idx
                        e.wait_ge(d.token[0], d.token[1])
                        nwaits[0] += 1
                    ins = o.fn(e)
                    if o.signal:
                        ins.then_inc(o.token[0], 16 if o.is_dma else 1)
            return body

        reg = {PE: block.tensor, ACT: block.scalar, DVE: block.vector, POOL: block.gpsimd, SP: block.sync}
        for eng_name, lst in streams.items():
            reg[eng_name](make(eng_name, lst))
        self.n_waits = nwaits[0]


class Arena:
    def __init__(self, tensor, size):
        self.t = tensor
        self.size = size
        self.off = 0

    def reset(self, off=0):
        self.off = off

    def alloc(self, shape, dtype, parts=128):
        esz = 2 if dtype == BF16 else 4
        n = 1
        for s in shape:
            n *= s
        nbytes = (n * esz + 63) // 64 * 64
        assert self.off + nbytes <= self.size, ("arena overflow", self.off, nbytes, self.size)
        ap = self.t[0:parts, self.off:self.off + n * esz].bitcast(dtype)
        self.off += nbytes
        if len(shape) == 2:
            ap = ap.rearrange("p (a b) -> p a b", a=shape[0])
        elif len(shape) == 3:
            ap = ap.rearrange("p (a b c) -> p a b c", a=shape[0], b=shape[1])
        return ap


def build_program(sublayers, debug=False, max_ops=None, marks=None):
    nc = bass.Bass("TRN2", target_bir_lowering=False)

    def din(name, shape, dt=F32):
        return nc.dram_tensor(name, list(shape), dt, kind="ExternalInput").ap()

    xT = din("xT", [D, S])
    cT = din("cT", [128, KC])
    ada_w = din("ada_w", [2, D, 9 * D])
    ada_bT = din("ada_bT", [2, 128, 72])
    norm_gT = din("norm_gT", [2, 128, 48])
    ffn_w_in = din("ffn_w_in", [2, 2, D, 2 * DFF])
    ffn_w_out = din("ffn_w_out", [2, 2, DFF, D])
    pool_w = din("pool_w", [4, 256, 256])
    pool_bT = din("pool_bT", [128, 8])
    pool_scT = din("pool_scT", [128, 8])
    mla_w_in_x = din("mla_w_in_x", [D, 448])
    q_normT = din("q_normT", [128, 2])
    kv_normT = din("kv_normT", [128, 1])
    w_uq_x = din("w_uq_x", [256, 2048])
    w_ukT = din("w_ukT", [64, NH, 128])
    w_uv = din("w_uv", [128, NH * 64])
    w_o = din("w_o", [NH * 64, D])
    rope_cos = din("rope_cos", [32, S])
    rope_sin = din("rope_sin", [32, S])
    outT = nc.dram_tensor("outT", [D, S], F32, kind="ExternalOutput").ap()

    xres = [nc.dram_tensor(f"xres{i}", [D, S], F32).ap() for i in range(2)]
    ffn_ids = sorted({(l, s // 2) for (l, s) in sublayers if s != 1})
    win_s = {k: nc.dram_tensor(f"win_s{k[0]}{k[1]}", [FC, 128, KC * 256], BF16).ap() for k in ffn_ids}
    wout_s = {k: nc.dram_tensor(f"wout_s{k[0]}{k[1]}", [KC, 128, FC * 128], BF16).ap() for k in ffn_ids}
    wo_s = nc.dram_tensor("wo_s", [KC, 64, NH * 128], BF16).ap()

    P = Prog(nc)
    bufs = {}

    REGION_KEYS = {"ada_ring", "actT", "win_ring", "wout_ring", "xe", "hE", "tE", "ua", "ub", "ua2", "ub2", "dT", "pw",
                   "sqE", "rsE", "cq_all", "ckvT", "krope", "winx", "wuq", "wuk", "wuv", "wo_ring", "Vg", "qnope",
                   "qlat", "qrope", "pT", "osb", "rc", "oT", "tab_c", "tab_s", "t1", "t2", "cq32"}
    state = {"fence_op": None}

    def B(*key):
        if key not in bufs:
            b = Buf(str(key))
            if key[0] in REGION_KEYS:
                b.last_writer = state["fence_op"]
            bufs[key] = b
        return bufs[key]

    st = ExitStack()
    with st:
        COMMON = 89 * 1024
        REGION = 118 * 1024
        common_t = st.enter_context(nc.sbuf_tensor("common", [128, COMMON], U8))
        region_t = st.enter_context(nc.sbuf_tensor("region", [128, REGION], U8))
        CA = Arena(common_t, COMMON)
        RA = Arena(region_t, REGION)
        ps = [st.enter_context(nc.psum_tensor(f"ps{i}", [128, 512], F32)) for i in range(8)]
        PB = [B("psum", i) for i in range(8)]
        for b_ in PB:
            b_.excl = True

        xslot = [CA.alloc([KC, T], F32) for _ in range(2)]
        hT = CA.alloc([KC, T], BF16)
        tmp32 = CA.alloc([KC, T], F32)
        sq8 = tmp32.rearrange("p a b -> p (a b)")[:, 0:KC * T // 2].bitcast(BF16).rearrange("p (a b) -> p a b", a=KC)
        ysb = CA.alloc([KC, T], F32)
        rsA = CA.alloc([1, T], F32)[:, 0, :]
        rsB = CA.alloc([1, T], F32)[:, 0, :]
        sg = [CA.alloc([1, T], F32)[:, 0, :] for _ in range(2)]
        sqr = [CA.alloc([1, T], BF16)[:, 0, :] for _ in range(4)]
        ones_bf = CA.alloc([1, 128], BF16)[:, 0, :]
        ones_f = CA.alloc([1, 128], F32)[:, 0, :]
        epsc = CA.alloc([1, 1], F32)[:, 0, :]
        c_sb = CA.alloc([1, KC], F32)[:, 0, :]
        sc_bf = CA.alloc([1, KC], BF16)[:, 0, :]
        modT = [CA.alloc([1, 72], F32)[:, 0, :] for _ in range(2)]
        adab = [CA.alloc([1, 72], F32)[:, 0, :] for _ in range(2)]
        ng = [CA.alloc([1, 48], F32)[:, 0, :] for _ in range(2)]
        vecA = CA.alloc([6, KC], F32)
        vecB = CA.alloc([6, KC], F32)
        smallv = CA.alloc([1, 32], F32)[:, 0, :]
        pT_extra = CA.alloc([1, T], BF16)[:, 0, :]
        print("common arena used", CA.off, "of", COMMON)

        P.op(POOL, lambda e: e.memset(ones_bf, 1.0), writes=[B("ones_bf")])
        P.op(POOL, lambda e: e.memset(ones_f, 1.0), writes=[B("ones_f")])
        P.op(POOL, lambda e: e.memset(epsc, EPS), writes=[B("epsc")])
        P.dma(SP, c_sb, cT, writes=[B("c_sb")])
        for l in range(2):
            P.dma(SP, adab[l], ada_bT[l], writes=[B("adab", l)])
            P.dma(SP, ng[l], norm_gT[l], writes=[B("ng", l)])
        P.dma(SP, smallv[:, 0:8], pool_bT, pwrites=[B("smallv")])
        P.dma(SP, smallv[:, 8:16], pool_scT, pwrites=[B("smallv")])
        P.dma(SP, smallv[:, 16:18], q_normT, pwrites=[B("smallv")])
        P.dma(SP, smallv[:, 18:19], kv_normT, pwrites=[B("smallv")])
        P.op(ACT, lambda e: e.activation(out=sc_bf, in_=c_sb, func=AF.Silu), reads=[B("c_sb")], writes=[B("sc_bf")])

        pq = []

        def precast(k):
            l, f = k
            wi = ffn_w_in[l, f].rearrange("(kc p) n -> p kc n", p=128)
            wo = ffn_w_out[l, f].rearrange("(fc p) n -> p fc n", p=128)
            for j in range(FC):
                dst = win_s[k][j].rearrange("p (kc x) -> p kc x", x=256)
                for gu in range(2):
                    c0 = gu * DFF + j * 128
                    pq.append((("win_s", k), dst[:, :, gu * 128:(gu + 1) * 128], wi[:, :, c0:c0 + 128]))
            for c in range(KC):
                dst = wout_s[k][c].rearrange("p (fc d) -> p fc d", d=128)
                for h0 in (0, 11):
                    pq.append((("wout_s", k), dst[:, h0:h0 + 11, :], wo[:, h0:h0 + 11, c * 128:(c + 1) * 128]))

        def pump(n):
            for _ in range(min(n, len(pq))):
                key, dst, src = pq.pop(0)
                P.dma(POOL, dst, src, pwrites=[B(*key)])

        def flush_precast(k):
            while any(key[1] == k for (key, _, _) in pq):
                pump(1)

        def compute_mod(l, RAm):
            ada_ring = [RAm.alloc([KC, 512], BF16) for _ in range(2)]
            aw = ada_w[l].rearrange("(kc p) n -> p kc n", p=128)
            mps = ps[7]
            for bi in range(18):
                slot = ada_ring[bi % 2]
                sb = B("ada_ring", bi % 2)
                P.dma(POOL, slot, aw[:, :, bi * 512:(bi + 1) * 512], writes=[sb])
                for cl in range(4):
                    gc = bi * 4 + cl
                    for kc in range(KC):
                        first = (gc == 0 and kc == 0)
                        P.op(PE, lambda e, slot=slot, cl=cl, kc=kc, gc=gc: e.matmul(
                            mps[:, gc:gc + 1], lhsT=slot[:, kc, cl * 128:(cl + 1) * 128],
                            rhs=sc_bf[:, kc:kc + 1], start=(kc == 0), stop=(kc == KC - 1)),
                            reads=[sb, B("sc_bf")], writes=[PB[7]] if first else (),
                            pwrites=() if first else [PB[7]])
            P.op(DVE, lambda e: e.tensor_tensor(out=modT[l], in0=mps[:, 0:72], in1=adab[l], op=ALU.add),
                 reads=[PB[7], B("adab", l)], writes=[B("modT", l)])
            for sub in range(3):
                i = l * 3 + sub
                wgt = 1.0 if sub == 1 else 0.5
                scale = modT[l][:, (3 * sub + 1) * 8:(3 * sub + 2) * 8]
                gate = modT[l][:, (3 * sub + 2) * 8:(3 * sub + 3) * 8]
                gpre = ng[l][:, (2 * sub) * 8:(2 * sub + 1) * 8]
                gpost = ng[l][:, (2 * sub + 1) * 8:(2 * sub + 2) * 8]
                P.op(DVE, lambda e, i=i, scale=scale, gpre=gpre: e.scalar_tensor_tensor(
                    out=vecA[:, i, :], in0=scale, scalar=1.0, in1=gpre, op0=ALU.add, op1=ALU.mult),
                    reads=[B("modT", l), B("ng", l)], writes=[B("vecA", i)])
                P.op(DVE, lambda e, i=i, gate=gate, gpost=gpost: e.scalar_tensor_tensor(
                    out=vecB[:, i, :], in0=gate, scalar=1.0, in1=gpost, op0=ALU.add, op1=ALU.mult),
                    reads=[B("modT", l), B("ng", l)], writes=[B("vecB", i)])
                P.op(DVE, lambda e, i=i, wgt=wgt: e.tensor_scalar(
                    out=vecB[:, i, :], in0=vecB[:, i, :], scalar1=wgt, scalar2=None, op0=ALU.mult),
                    reads=[B("vecB", i)], writes=[B("vecB", i)])

        state.update({"src": xT, "src_key": "xT"})

        def dram_tile(ap, t):
            return ap.rearrange("(ch p) s -> p ch s", p=128)[:, :, t * T:(t + 1) * T]

        def XB(slot):
            return [B("xslot", slot, c) for c in range(KC)]

        def load_x(t, slot):
            P.dma(POOL, xslot[slot], dram_tile(state["src"], t), reads=[B(state["src_key"], t)],
                  writes=XB(slot))

        def store_x(t, slot, dst, dst_key):
            P.dma(POOL, dram_tile(dst, t), xslot[slot], reads=XB(slot), writes=[B(dst_key, t)],
                  sem_buf=B("xstore", slot))

        def prologue_steps(xap, xbuf, i, hdst, hbufs):
            l, sub = divmod(i, 3)
            shift = modT[l][:, (3 * sub) * 8:(3 * sub + 1) * 8]

            def s_sq():
                P.op(POOL, lambda e: e.tensor_tensor(out=sq8, in0=xap, in1=xap, op=ALU.mult), reads=xbuf, writes=[B("tmp32")])

            def s_mm():
                for kc in range(KC):
                    P.op(PE, lambda e, kc=kc: e.matmul(ps[6][:], lhsT=ones_bf, rhs=sq8[:, kc, :], start=(kc == 0), stop=(kc == KC - 1)),
                         reads=[B("tmp32"), B("ones_bf")], writes=[PB[6]])

            def s_sqrt():
                P.op(ACT, lambda e: e.activation(out=rsA, in_=ps[6][:], func=AF.Sqrt, bias=epsc, scale=1.0 / D),
                     reads=[PB[6], B("epsc")], writes=[B("rsA")])
                P.op(DVE, lambda e: e.reciprocal(out=rsA, in_=rsA), reads=[B("rsA")], writes=[B("rsA")])

            def s_mul():
                P.op(DVE, lambda e: e.tensor_tensor(out=tmp32, in0=xap, in1=rsA.unsqueeze(1).broadcast_to([128, KC, T]), op=ALU.mult),
                     reads=xbuf + [B("rsA")], writes=[B("tmp32")])

            def mk(c):
                def s_mod():
                    P.op(ACT, lambda e: e.activation(out=hdst[:, c, :], in_=tmp32[:, c, :], func=AF.Identity,
                                                     bias=shift[:, c:c + 1], scale=vecA[:, i, c:c + 1]),
                         reads=[B("tmp32"), B("vecA", i), B("modT", l)], writes=[hbufs[c]])
                return s_mod

            return [s_sq, s_mm, s_sqrt, s_mul] + [mk(c) for c in range(KC)]

        def prologue(xap, xbuf, i, hdst, hbufs):
            for st_ in prologue_steps(xap, xbuf, i, hdst, hbufs):
                st_()

        YB = [B("ysb", c) for c in range(KC)]

        def post_stats_sq(c):
            r, rb = sqr[c % 4], B("sqr", c % 4)
            P.op(POOL, lambda e: e.tensor_tensor(out=r, in0=ysb[:, c, :], in1=ysb[:, c, :], op=ALU.mult), reads=[YB[c]], writes=[rb])

        def post_stats_mm(c):
            r, rb = sqr[c % 4], B("sqr", c % 4)
            P.op(PE, lambda e: e.matmul(ps[7][:], lhsT=ones_bf, rhs=r, start=(c == 0), stop=(c == KC - 1)),
                 reads=[rb, B("ones_bf")], writes=[PB[7]])

        def epilogue(xap, xbuf, i):
            P.op(ACT, lambda e: e.activation(out=rsB, in_=ps[7][:], func=AF.Sqrt, bias=epsc, scale=1.0 / D),
                 reads=[PB[7], B("epsc")], writes=[B("rsB")])
            P.op(DVE, lambda e: e.reciprocal(out=rsB, in_=rsB), reads=[B("rsB")], writes=[B("rsB")])
            P.op(DVE, lambda e: e.tensor_tensor(out=ysb, in0=ysb, in1=rsB.unsqueeze(1).broadcast_to([128, KC, T]), op=ALU.mult),
                 reads=YB + [B("rsB")], writes=YB)
            for c in range(KC):
                P.op(DVE, lambda e, c=c: e.scalar_tensor_tensor(out=xap[:, c, :], in0=ysb[:, c, :], scalar=vecB[:, i, c:c + 1],
                                                                in1=xap[:, c, :], op0=ALU.mult, op1=ALU.add),
                     reads=[YB[c], B("vecB", i), xbuf[c]], writes=[xbuf[c]])

        def ffn_sublayer(l, sub, dst, dst_key):
            i = l * 3 + sub
            k = (l, sub // 2)
            flush_precast(k)
            RA.reset()
            actT = RA.alloc([FC, T], BF16)
            win_ring = [RA.alloc([KC, 256], BF16) for _ in range(3)]
            wout_ring = [RA.alloc([FC, 128], BF16) for _ in range(2)]
            hb = [B("hT", c) for c in range(KC)]
            n_in, n_out = NT * FC, NT * KC
            issued = {"in": 0, "out": 0}

            def issue_in(upto):
                while issued["in"] < min(upto, n_in):
                    n = issued["in"]
                    j_ = n % FC
                    P.dma(SP, win_ring[n % 3], win_s[k][j_].rearrange("p (kc x) -> p kc x", x=256),
                          reads=[B("win_s", k)], writes=[B("win_ring", n % 3)])
                    issued["in"] += 1

            def issue_out(upto):
                while issued["out"] < min(upto, n_out):
                    m = issued["out"]
                    c_ = m % KC
                    P.dma(SP, wout_ring[m % 2], wout_s[k][c_].rearrange("p (fc d) -> p fc d", d=128),
                          reads=[B("wout_s", k)], writes=[B("wout_ring", m % 2)])
                    issued["out"] += 1

            load_x(0, 0)
            issue_in(3)
            for st_ in prologue_steps(xslot[0], XB(0), i, hT, hb):
                st_()
            for t in range(NT):
                slot = t % 2
                xap, xbuf = xslot[slot], XB(slot)
                nxt = []
                if t + 1 < NT:
                    load_x(t + 1, (t + 1) % 2)
                    nxt = prologue_steps(xslot[(t + 1) % 2], XB((t + 1) % 2), i, hT, hb)
                issue_out(t * KC + 2)
                for j in range(FC):
                    n = t * FC + j
                    issue_in(n + 3)
                    wslot, wbuf = win_ring[n % 3], B("win_ring", n % 3)
                    gb, ub = j % 2, 2 + j % 2
                    for kc in range(KC):
                        P.op(PE, lambda e, kc=kc, wslot=wslot, gb=gb: e.matmul(ps[gb][:], lhsT=wslot[:, kc, 0:128], rhs=hT[:, kc, :],
                                                                             start=(kc == 0), stop=(kc == KC - 1)),
                             reads=[wbuf, hb[kc]], writes=[PB[gb]])
                    for kc in range(KC):
                        P.op(PE, lambda e, kc=kc, wslot=wslot, ub=ub: e.matmul(ps[ub][:], lhsT=wslot[:, kc, 128:256], rhs=hT[:, kc, :],
                                                                             start=(kc == 0), stop=(kc == KC - 1)),
                             reads=[wbuf, hb[kc]], writes=[PB[ub]])
                    sgt, sgb = sg[j % 2], B("sg", j % 2)
                    P.op(ACT, lambda e, gb=gb, sgt=sgt: e.activation(out=sgt, in_=ps[gb][:], func=AF.Silu), reads=[PB[gb]], writes=[sgb])
                    P.op(DVE, lambda e, ub=ub, sgt=sgt, j=j: e.tensor_tensor(out=actT[:, j, :], in0=sgt, in1=ps[ub][:], op=ALU.mult),
                         reads=[sgb, PB[ub]], writes=[B("actT", j)])
                    if j == 13 and nxt:
                        nxt.pop(0)()
                for c in range(KC):
                    m = t * KC + c
                    issue_out(m + 2)
                    wslot, wbuf = wout_ring[m % 2], B("wout_ring", m % 2)
                    yb = 4 + c % 2
                    for fc in range(FC):
                        P.op(PE, lambda e, fc=fc, wslot=wslot, yb=yb: e.matmul(ps[yb][:], lhsT=wslot[:, fc, :], rhs=actT[:, fc, :],
                                                                             start=(fc == 0), stop=(fc == FC - 1)),
                             reads=[wbuf, B("actT", fc)], writes=[PB[yb]])
                    P.op(DVE, lambda e, c=c, yb=yb: e.tensor_copy(out=ysb[:, c, :], in_=ps[yb][:]), reads=[PB[yb]], writes=[YB[c]])
                    post_stats_sq(c)
                    if c > 0:
                        post_stats_mm(c - 1)
                    take = {0: 1, 1: 2, 2: 1}.get(c, 2)
                    for _ in range(take):
                        if nxt:
                            nxt.pop(0)()
                while nxt:
                    nxt.pop(0)()
                post_stats_mm(KC - 1)
                epilogue(xap, xbuf, i)
                store_x(t, slot, dst, dst_key)
                pump(8)

        def pool_sublayer(l, sub, dst, dst_key):
            i = l * 3 + sub
            RA.reset()
            W = T + 16
            xe = RA.alloc([KC, W], F32)
            hE = RA.alloc([KC, W], F32)
            tE = RA.alloc([KC, W], F32)
            ua = RA.alloc([2, W], F32)
            ub_ = RA.alloc([2, W], F32)
            ua2 = RA.alloc([2, W], F32)
            ub2 = RA.alloc([2, W], F32)
            dT = RA.alloc([KC, T], BF16)
            pw = RA.alloc([4, 2, 256], BF16)
            sqE = RA.alloc([KC, W], BF16)
            rsE = RA.alloc([1, W], F32)[:, 0, :]
            shift = modT[l][:, (3 * sub) * 8:(3 * sub + 1) * 8]
            P.dma(POOL, pw, pool_w.rearrange("g (cc p) d -> p g cc d", p=128), writes=[B("pw")])
            src = state["src"].rearrange("(ch p) s -> p ch s", p=128)
            def p_load(t):
                slot = t % 2
                e0 = max(t * T - 8, 0)
                e1 = min((t + 1) * T + 8, S)
                c0 = e0 - (t * T - 8)
                c1 = c0 + (e1 - e0)
                rd = [B(state["src_key"], tt) for tt in range(max(t - 1, 0), min(t + 2, NT))]
                P.dma(POOL, xe[:, :, c0:c1], src[:, :, e0:e1], reads=rd, writes=[B("xe")])

            def p_front(t):
                slot = t % 2
                e0 = max(t * T - 8, 0)
                e1 = min((t + 1) * T + 8, S)
                c0 = e0 - (t * T - 8)
                c1 = c0 + (e1 - e0)
                P.op(ACT, lambda e, c0=c0, c1=c1: e.activation(out=sqE[:, :, c0:c1], in_=xe[:, :, c0:c1], func=AF.Square),
                     reads=[B("xe")], writes=[B("sqE")])
                P.op(ACT, lambda e, slot=slot: e.activation(out=xslot[slot], in_=xe[:, :, 8:8 + T], func=AF.Identity),
                     reads=[B("xe")], writes=XB(slot))
                for kc in range(KC):
                    P.op(PE, lambda e, kc=kc: e.matmul(ps[0][:], lhsT=ones_bf, rhs=sqE[:, kc, 8:8 + T], start=(kc == 0), stop=(kc == KC - 1)),
                         reads=[B("sqE"), B("ones_bf")], writes=[PB[0]])
                if c0 == 0:
                    for kc in range(KC):
                        P.op(PE, lambda e, kc=kc: e.matmul(ps[1][:, 0:8], lhsT=ones_bf, rhs=sqE[:, kc, 0:8], start=(kc == 0), stop=(kc == KC - 1)),
                             reads=[B("sqE"), B("ones_bf")], writes=[PB[1]])
                if c1 == W:
                    for kc in range(KC):
                        P.op(PE, lambda e, kc=kc: e.matmul(ps[1][:, 8:16], lhsT=ones_bf, rhs=sqE[:, kc, W - 8:W], start=(kc == 0), stop=(kc == KC - 1)),
                             reads=[B("sqE"), B("ones_bf")], writes=[PB[1]] if c0 != 0 else (), pwrites=[PB[1]] if c0 == 0 else ())
                P.op(ACT, lambda e: e.activation(out=rsE[:, 8:8 + T], in_=ps[0][:], func=AF.Sqrt, bias=epsc, scale=1.0 / D),
                     reads=[PB[0], B("epsc")], writes=[B("rsE")])
                if c0 == 0:
                    P.op(ACT, lambda e: e.activation(out=rsE[:, 0:8], in_=ps[1][:, 0:8], func=AF.Sqrt, bias=epsc, scale=1.0 / D),
                         reads=[PB[1], B("epsc")], pwrites=[B("rsE")])
                if c1 == W:
                    P.op(ACT, lambda e: e.activation(out=rsE[:, W - 8:W], in_=ps[1][:, 8:16], func=AF.Sqrt, bias=epsc, scale=1.0 / D),
                         reads=[PB[1], B("epsc")], pwrites=[B("rsE")])

            def p_front_b(t):
                e0 = max(t * T - 8, 0)
                e1 = min((t + 1) * T + 8, S)
                c0 = e0 - (t * T - 8)
                c1 = c0 + (e1 - e0)
                P.op(DVE, lambda e, c0=c0, c1=c1: e.reciprocal(out=rsE[:, c0:c1], in_=rsE[:, c0:c1]), reads=[B("rsE")], writes=[B("rsE")])
                P.op(DVE, lambda e, c0=c0, c1=c1: e.tensor_tensor(out=tE[:, :, c0:c1], in0=xe[:, :, c0:c1],
                                                                  in1=rsE[:, c0:c1].unsqueeze(1).broadcast_to([128, KC, c1 - c0]), op=ALU.mult),
                     reads=[B("xe"), B("rsE")], writes=[B("tE")])

            def p_mid(t):
                slot = t % 2
                e0 = max(t * T - 8, 0)
                e1 = min((t + 1) * T + 8, S)
                c0 = e0 - (t * T - 8)
                c1 = c0 + (e1 - e0)
                for c in range(KC):
                    P.op(ACT, lambda e, c=c, c0=c0, c1=c1: e.activation(out=hE[:, c, c0:c1], in_=tE[:, c, c0:c1], func=AF.Identity,
                                                                        bias=shift[:, c:c + 1], scale=vecA[:, i, c:c + 1]),
                         reads=[B("tE"), B("vecA", i), B("modT", l)], writes=[B("hE")] if c == 0 else (), pwrites=[B("hE")] if c else ())
                if c0 > 0:
                    P.op(POOL, lambda e, c0=c0: e.memset(hE[:, :, 0:c0], 0.0), reads=[B("hE")], writes=[B("hE")])
                if c1 < W:
                    P.op(POOL, lambda e, c1=c1: e.memset(hE[:, :, c1:W], 0.0), reads=[B("hE")], writes=[B("hE")])
                for g in range(4):
                    eng = DVE if g < 3 else POOL
                    hg = hE[:, 2 * g:2 * g + 2, :]
                    bufa, bufb = (ua, ub_) if g < 3 else (ua2, ub2)
                    ka, kb = ("ua", "ub") if g < 3 else ("ua2", "ub2")
                    P.op(eng, lambda e, hg=hg, bufa=bufa: e.tensor_tensor(out=bufa[:, :, 1:W], in0=hg[:, :, 0:W - 1], in1=hg[:, :, 1:W], op=ALU.add),
                         reads=[B("hE")], writes=[B(ka)])
                    cur, curk, oth, othk = bufa, ka, bufb, kb
                    lo, hi = 1, W
                    for step in range(g):
                        sh = 1 << step
                        nlo, nhi = lo + sh, hi - sh
                        P.op(eng, lambda e, cur=cur, oth=oth, nlo=nlo, nhi=nhi, sh=sh: e.tensor_tensor(
                            out=oth[:, :, nlo:nhi], in0=cur[:, :, nlo - sh:nhi - sh], in1=cur[:, :, nlo + sh:nhi + sh], op=ALU.add),
                            reads=[B(curk)], writes=[B(othk)])
                        cur, curk, oth, othk = oth, othk, cur, curk
                        lo, hi = nlo, nhi
                    w = POOL_WINDOWS[g]
                    P.op(DVE, lambda e, cur=cur, hg=hg, g=g, w=w: e.scalar_tensor_tensor(
                        out=dT[:, 2 * g:2 * g + 2, :], in0=cur[:, :, 8:8 + T], scalar=1.0 / w, in1=hg[:, :, 8:8 + T],
                        op0=ALU.mult, op1=ALU.subtract), reads=[B(curk), B("hE")], writes=[B("dT", g)])
                    fix = []
                    if t == 0:
                        fix += [(tt, tt + w // 2) for tt in range(w // 2)]
                    if t == NT - 1:
                        fix += [(T - 1 - u, u + 1 + w // 2) for u in range(w // 2 - 1)]
                    for (col, cntv) in fix:
                        P.op(DVE, lambda e, cur=cur, hg=hg, g=g, col=col, cntv=cntv: e.scalar_tensor_tensor(
                            out=dT[:, 2 * g:2 * g + 2, col:col + 1], in0=cur[:, :, 8 + col:9 + col], scalar=1.0 / cntv,
                            in1=hg[:, :, 8 + col:9 + col], op0=ALU.mult, op1=ALU.subtract),
                            reads=[B(curk), B("hE"), B("dT", g)], writes=[B("dT", g)])

            def p_back(t):
                slot = t % 2
                e0 = max(t * T - 8, 0)
                e1 = min((t + 1) * T + 8, S)
                c0 = e0 - (t * T - 8)
                c1 = c0 + (e1 - e0)
                for g in range(4):
                    for dch in range(2):
                        ch = 2 * g + dch
                        yb = 4 + ch % 2
                        for cc in range(2):
                            P.op(PE, lambda e, g=g, dch=dch, cc=cc, yb=yb: e.matmul(
                                ps[yb][:], lhsT=pw[:, g, cc, dch * 128:(dch + 1) * 128], rhs=dT[:, 2 * g + cc, :],
                                start=(cc == 0), stop=(cc == 1)), reads=[B("pw"), B("dT", g)], writes=[PB[yb]])
                        P.op(DVE, lambda e, ch=ch, yb=yb: e.tensor_scalar(out=ysb[:, ch, :], in0=ps[yb][:], scalar1=smallv[:, ch:ch + 1],
                                                                        scalar2=smallv[:, 8 + ch:9 + ch], op0=ALU.add, op1=ALU.mult),
                             reads=[PB[yb], B("smallv")], writes=[YB[ch]])
                        post_stats_sq(ch)
                        if ch >= 2:
                            post_stats_mm(ch - 2)
                post_stats_mm(KC - 2)
                post_stats_mm(KC - 1)
                epilogue(xslot[slot], XB(slot), i)
                store_x(t, slot, dst, dst_key)
                pump(8)

            p_load(0)
            p_front(0)
            p_front_b(0)
            for t in range(NT):
                if t + 1 < NT:
                    p_load(t + 1)
                p_mid(t)
                if t + 1 < NT:
                    p_front(t + 1)
                p_back(t)
                if t + 1 < NT:
                    p_front_b(t + 1)
                if t == 0 and state.get("defer_mod") is not None:
                    compute_mod(state["defer_mod"], RA)
                    state["defer_mod"] = None

        def mla_sublayer(l, sub, dst, dst_key):
            i = l * 3 + sub
            RA.reset()
            cq_all = RA.alloc([2, S], BF16)
            ckvT = RA.alloc([1, S], BF16)[:, 0, :]
            krope = RA.alloc([1, S], BF16)[:, 0, :]
            winx = RA.alloc([KC, 448], BF16)
            wuq = RA.alloc([2, 2048], BF16)
            wuk = RA.alloc([NH, 128], BF16, parts=64)
            wuv = RA.alloc([1, NH * 64], BF16)[:, 0, :]
            wo_ring = [RA.alloc([NH, 128], BF16) for _ in range(2)]
            Vg = RA.alloc([NKC, 4, 65], BF16)
            qnope = RA.alloc([1, T], BF16, parts=64)[:, 0, :]
            qlat = [RA.alloc([1, T], BF16)[:, 0, :] for _ in range(2)]
            qrope = [RA.alloc([1, T], BF16)[:, 0, :] for _ in range(2)]
            pT = [RA.alloc([1, T], BF16)[:, 0, :] for _ in range(3)] + [pT_extra]
            osb = RA.alloc([1, T], F32, parts=65)[:, 0, :]
            rc = RA.alloc([1, T], F32, parts=64)[:, 0, :]
            oT = RA.alloc([NH, T], BF16)
            tab = RA.alloc([2, T], F32, parts=32)
            t1 = RA.alloc([1, T], F32)[:, 0, :]
            t2 = RA.alloc([1, T], F32)[:, 0, :]
            cq32 = RA.alloc([2, T], F32)
            print("mla region used", RA.off, "of", REGION)
            qn = smallv[:, 16:18]
            kvn = smallv[:, 18:19]
            P.dma(POOL, winx, mla_w_in_x.rearrange("(kc p) n -> p kc n", p=128), writes=[B("winx")])
            P.dma(POOL, wuq, w_uq_x.rearrange("(k p) n -> p k n", p=128), writes=[B("wuq")])
            P.dma(POOL, wuk, w_ukT, writes=[B("wuk")])
            P.dma(POOL, wuv, w_uv, writes=[B("wuv")])
            wov = w_o.rearrange("(h v) d -> v h d", v=64)
            for c_ in range(KC):
                P.dma(POOL, wo_s[c_].rearrange("v (h d) -> v h d", d=128), wov[:, :, c_ * 128:(c_ + 1) * 128], pwrites=[B("wo_s")])
            wo_issued = [0]

            def issue_wo(upto):
                while wo_issued[0] < min(upto, NT * KC):
                    m_ = wo_issued[0]
                    P.dma(SP, wo_ring[m_ % 2][0:64], wo_s[m_ % KC].rearrange("v (h d) -> v h d", d=128),
                          reads=[B("wo_s")], writes=[B("wo_ring", m_ % 2)])
                    wo_issued[0] += 1
            hb = [B("hT", c) for c in range(KC)]
            P.op(POOL, lambda e: e.memset(krope, 0.0), writes=[B("krope", t_) for t_ in range(NT)])
            P.op(POOL, lambda e: e.memset(oT, 0.0), writes=[B("oT", h_) for h_ in range(NH)])
            for r_ in range(2):
                P.op(POOL, lambda e, r_=r_: e.memset(wo_ring[r_], 0.0), writes=[B("wo_ring", r_)])
            for q_ in range(2):
                P.op(POOL, lambda e, q_=q_: e.memset(qrope[q_], 0.0), writes=[B("qrope", q_)])

            TAB = [B("tab_c"), B("tab_s")]

            def load_tab(t):
                P.dma(SP, tab[:, 0, :], rope_cos[:, t * T:(t + 1) * T], writes=[TAB[0]])
                P.dma(SP, tab[:, 1, :], rope_sin[:, t * T:(t + 1) * T], writes=[TAB[1]])

            def rope_combine(psa, psb, pba, pbb, dst_ap, dst_buf, eng2=POOL):
                P.op(DVE, lambda e: e.tensor_tensor(out=t1[0:32, :], in0=psa, in1=tab[:, 0, :], op=ALU.mult),
                     reads=[pba, TAB[0]], writes=[B("t1")])
                P.op(DVE, lambda e: e.tensor_tensor(out=t2[0:32, :], in0=psb, in1=tab[:, 1, :], op=ALU.mult),
                     reads=[pbb, TAB[1]], writes=[B("t2")])
                P.op(eng2, lambda e: e.tensor_tensor(out=dst_ap, in0=t1[0:32, :], in1=t2[0:32, :], op=ALU.add),
                     reads=[B("t1"), B("t2")], writes=[dst_buf])

            def qside(h, tok):
                for k2 in range(2):
                    P.op(PE, lambda e, k2=k2: e.matmul(ps[4][0:64, :], lhsT=wuq[:, k2, h * 64:(h + 1) * 64], rhs=cq_all[:, k2, tok],
                                                       start=(k2 == 0), stop=(k2 == 1)),
                         reads=[B("wuq"), B("cq_all")], writes=[PB[4]])
                for k2 in range(2):
                    P.op(PE, lambda e, k2=k2: e.matmul(ps[5][0:32, :], lhsT=wuq[:, k2, 1024 + h * 32:1024 + (h + 1) * 32], rhs=cq_all[:, k2, tok],
                                                       start=(k2 == 0), stop=(k2 == 1)),
                         reads=[B("wuq"), B("cq_all")], writes=[PB[5]])
                P.op(DVE, lambda e: e.tensor_copy(out=qnope, in_=ps[4][0:64, :]), reads=[PB[4]], writes=[B("qnope")])
                P.op(DVE, lambda e: e.tensor_tensor(out=t1[0:32, :], in0=ps[5][0:32, :], in1=tab[:, 0, :], op=ALU.mult),
                     reads=[PB[5], TAB[0]], writes=[B("t1")])
                for k2 in range(2):
                    P.op(PE, lambda e, k2=k2: e.matmul(ps[4][0:32, :], lhsT=wuq[:, k2, 1536 + h * 32:1536 + (h + 1) * 32], rhs=cq_all[:, k2, tok],
                                                       start=(k2 == 0), stop=(k2 == 1)),
                         reads=[B("wuq"), B("cq_all")], writes=[PB[4]])
                P.op(PE, lambda e: e.matmul(ps[5][:], lhsT=wuk[:, h, :], rhs=qnope, start=True, stop=True),
                     reads=[B("wuk"), B("qnope")], writes=[PB[5]])
                P.op(DVE, lambda e: e.tensor_tensor(out=t2[0:32, :], in0=ps[4][0:32, :], in1=tab[:, 1, :], op=ALU.mult),
                     reads=[PB[4], TAB[1]], writes=[B("t2")])
                P.op(DVE, lambda e: e.tensor_copy(out=qlat[h % 2], in_=ps[5][:]), reads=[PB[5]], writes=[B("qlat", h % 2)])
                P.op(POOL, lambda e: e.tensor_tensor(out=qrope[h % 2][0:32, :], in0=t1[0:32, :], in1=t2[0:32, :], op=ALU.add),
                     reads=[B("t1"), B("t2")], writes=[B("qrope", h % 2)])

            load_x(0, 0)
            load_tab(0)
            prologue(xslot[0], XB(0), i, hT, hb)
            for t in range(NT):
                slot = t % 2
                tok = slice(t * T, (t + 1) * T)
                outs = [(ps[0][:], 0, 128, PB[0]), (ps[1][:], 128, 128, PB[1]), (ps[2][:], 256, 128, PB[2]),
                        (ps[3][0:32, :], 384, 32, PB[3]), (ps[4][0:32, :], 416, 32, PB[4])]
                for (pap, c0, m, pb) in outs:
                    for kc in range(KC):
                        P.op(PE, lambda e, pap=pap, c0=c0, m=m, kc=kc: e.matmul(pap, lhsT=winx[:, kc, c0:c0 + m], rhs=hT[:, kc, :],
                                                                             start=(kc == 0), stop=(kc == KC - 1)),
                             reads=[B("winx"), hb[kc]], writes=[pb])
                if t + 1 < NT:
                    load_x(t + 1, (t + 1) % 2)
                    prologue(xslot[(t + 1) % 2], XB((t + 1) % 2), i, hT, hb)
                for k2 in range(2):
                    P.op(DVE, lambda e, k2=k2: e.tensor_copy(out=cq32[:, k2, :], in_=ps[k2][:]), reads=[PB[k2]],
                         writes=[B("cq32")] if k2 == 0 else (), pwrites=[B("cq32")] if k2 else ())
                    r, rb = sqr[k2], B("sqr", k2)
                    P.op(ACT, lambda e, k2=k2, r=r: e.activation(out=r, in_=ps[k2][:], func=AF.Square), reads=[PB[k2]], writes=[rb])
                    P.op(PE, lambda e, k2=k2, r=r: e.matmul(ps[7][:], lhsT=ones_bf, rhs=r, start=(k2 == 0), stop=(k2 == 1)),
                         reads=[rb, B("ones_bf")], writes=[PB[7]])
                P.op(ACT, lambda e: e.activation(out=rsB, in_=ps[7][:], func=AF.Sqrt, bias=epsc, scale=1.0 / 256), reads=[PB[7], B("epsc")], writes=[B("rsB")])
                P.op(DVE, lambda e: e.reciprocal(out=rsB, in_=rsB), reads=[B("rsB")], writes=[B("rsB")])
                P.op(DVE, lambda e: e.tensor_tensor(out=cq32, in0=cq32, in1=rsB.unsqueeze(1).broadcast_to([128, 2, T]), op=ALU.mult),
                     reads=[B("cq32"), B("rsB")], writes=[B("cq32")])
                for k2 in range(2):
                    P.op(ACT, lambda e, k2=k2, tok=tok: e.activation(out=cq_all[:, k2, tok], in_=cq32[:, k2, :], func=AF.Identity, scale=qn[:, k2:k2 + 1]),
                         reads=[B("cq32"), B("smallv")], pwrites=[B("cq_all")])
                P.op(DVE, lambda e: e.tensor_copy(out=t1, in_=ps[2][:]), reads=[PB[2]], writes=[B("t1")])
                P.op(ACT, lambda e: e.activation(out=sqr[0], in_=ps[2][:], func=AF.Square), reads=[PB[2]], writes=[B("sqr", 0)])
                P.op(PE, lambda e: e.matmul(ps[7][:], lhsT=ones_bf, rhs=sqr[0], start=True, stop=True), reads=[B("sqr", 0), B("ones_bf")], writes=[PB[7]])
                P.op(ACT, lambda e: e.activation(out=rsB, in_=ps[7][:], func=AF.Sqrt, bias=epsc, scale=1.0 / 128), reads=[PB[7], B("epsc")], writes=[B("rsB")])
                P.op(DVE, lambda e: e.reciprocal(out=rsB, in_=rsB), reads=[B("rsB")], writes=[B("rsB")])
                P.op(DVE, lambda e: e.tensor_tensor(out=t1, in0=t1, in1=rsB, op=ALU.mult), reads=[B("t1"), B("rsB")], writes=[B("t1")])
                P.op(ACT, lambda e, tok=tok: e.activation(out=ckvT[:, tok], in_=t1, func=AF.Identity, scale=kvn[:, 0:1]),
                     reads=[B("t1"), B("smallv")], pwrites=[B("ckvT")])
                rope_combine(ps[3][0:32, :], ps[4][0:32, :], PB[3], PB[4], krope[0:32, tok], B("krope", t))
                if t + 1 < NT:
                    load_tab(t + 1)

            kr_all = [B("krope", t) for t in range(NT)]
            for t in range(NT):
                slot = t % 2
                tok = slice(t * T, (t + 1) * T)
                load_x(t, slot)
                if t == 0:
                    load_tab(t)
                    qside(0, tok)
                sc_i = [0]
                pending = [None]
                issue_wo(t * KC + 2)
                for hg in range(4):
                    P.op(POOL, lambda e: e.memset(Vg[:, :, :, 64:65], 1.0), reads=[B("Vg")], writes=[B("Vg")])
                    for kp in range(NKC // 2):
                        for kk in range(2):
                            kc = kp * 2 + kk
                            P.op(PE, lambda e, kc=kc, kk=kk, hg=hg: e.matmul(ps[4][:, kk * 256:(kk + 1) * 256], lhsT=ckvT[:, kc * 128:(kc + 1) * 128],
                                                                           rhs=wuv[:, hg * 256:(hg + 1) * 256], start=True, stop=True),
                                 reads=[B("ckvT"), B("wuv")], writes=[PB[4]] if kk == 0 else (), pwrites=[PB[4]] if kk else ())
                        P.op(DVE, lambda e, kp=kp: e.tensor_copy(out=Vg[:, 2 * kp:2 * kp + 2, :, 0:64],
                                                                 in_=ps[4][:].rearrange("p (k h v) -> p k h v", k=2, h=4)),
                             reads=[PB[4]], pwrites=[B("Vg")])
                    items = [(hh, kc) for hh in range(4) for kc in range(NKC)]
                    base = sc_i[0]

                    def S_(idx):
                        hh, kc = items[idx]
                        h = hg * 4 + hh
                        sb_ = (base + idx) % 4
                        ql, qlb = qlat[h % 2], B("qlat", h % 2)
                        qr, qrb = qrope[h % 2], B("qrope", h % 2)
                        P.op(PE, lambda e: e.matmul(ps[sb_][:], lhsT=ckvT[:, kc * 128:(kc + 1) * 128], rhs=ql, start=True, stop=False),
                             reads=[B("ckvT"), qlb], writes=[PB[sb_]])
                        P.op(PE, lambda e: e.matmul(ps[sb_][:], lhsT=krope[:, kc * 128:(kc + 1) * 128], rhs=qr, start=False, stop=True),
                             reads=kr_all + [qrb], writes=[PB[sb_]])

                    def make_norm(h, ob):
                        def norm():
                            P.op(DVE, lambda e: e.tensor_copy(out=osb, in_=ps[ob][0:65, :]), reads=[PB[ob]], writes=[B("osb")])
                            P.op(PE, lambda e: e.matmul(ps[5][0:64, :], lhsT=ones_f[64:65, 0:64], rhs=osb[64:65, :], start=True, stop=True),
                                 reads=[B("ones_f"), B("osb")], writes=[PB[5]])
                            P.op(DVE, lambda e: e.reciprocal(out=rc, in_=ps[5][0:64, :]), reads=[PB[5]], writes=[B("rc")])
                            P.op(POOL, lambda e: e.tensor_tensor(out=oT[0:64, h, :], in0=osb[0:64, :], in1=rc, op=ALU.mult),
                                 reads=[B("osb"), B("rc")], writes=[B("oT", h)])
                        return norm

                    LOOK = 3
                    for idx in range(LOOK):
                        S_(idx)
                    for idx, (hh, kc) in enumerate(items):
                        h = hg * 4 + hh
                        ob = 6 + h % 2
                        if idx + LOOK < len(items):
                            S_(idx + LOOK)
                        if kc == 4 and pending[0] is not None:
                            pending[0]()
                            pending[0] = None
                        if kc == 8 and h + 1 < NH:
                            qside(h + 1, tok)
                        if kc == 8 and h + 1 == NH and t + 1 < NT:
                            load_tab(t + 1)
                            qside(0, slice((t + 1) * T, (t + 2) * T))
                        sb_ = (base + idx) % 4
                        p_, pb_ = pT[sb_], B("pT", sb_)
                        P.op(ACT, lambda e, p_=p_, sb_=sb_: e.activation(out=p_, in_=ps[sb_][:], func=AF.Exp, scale=ATTN_SCALE),
                             reads=[PB[sb_]], writes=[pb_])
                        P.op(PE, lambda e, kc=kc, hh=hh, p_=p_, ob=ob: e.matmul(ps[ob][0:65, :], lhsT=Vg[:, kc, hh, :], rhs=p_,
                                                                              start=(kc == 0), stop=(kc == NKC - 1)),
                             reads=[B("Vg"), pb_], writes=[PB[ob]])
                        if kc == NKC - 1:
                            if pending[0] is not None:
                                pending[0]()
                            pending[0] = make_norm(h, ob)
                    sc_i[0] = base + len(items)
                    if hg == 3 and pending[0] is not None:
                        pending[0]()
                        pending[0] = None
                for c in range(KC):
                    m_ = t * KC + c
                    issue_wo(m_ + 1)
                    ws_, wb_ = wo_ring[m_ % 2], B("wo_ring", m_ % 2)
                    yb = 4 + c % 2
                    for h in range(NH):
                        P.op(PE, lambda e, h=h, ws_=ws_, yb=yb: e.matmul(ps[yb][:], lhsT=ws_[:, h, :], rhs=oT[:, h, :], start=(h == 0), stop=(h == NH - 1)),
                             reads=[wb_, B("oT", h)], writes=[PB[yb]])
                    issue_wo(m_ + 3)
                    P.op(DVE, lambda e, c=c, yb=yb: e.tensor_copy(out=ysb[:, c, :], in_=ps[yb][:]), reads=[PB[yb]],
                         writes=[YB[c]])
                    post_stats_sq(c)
                    if c > 0:
                        post_stats_mm(c - 1)
                post_stats_mm(KC - 1)
                epilogue(xslot[slot], XB(slot), i)
                store_x(t, slot, dst, dst_key)
                pump(8)

        layers_needed = sorted({l for (l, s_) in sublayers})
        for k in ffn_ids:
            precast(k)
        if ffn_ids and sublayers[0][1] != 1:
            flush_precast(ffn_ids[0])
        RA.reset()
        state["defer_mod"] = None
        for l in layers_needed:
            if l == 1 and (0, 1) in sublayers:
                state["defer_mod"] = 1
                continue
            RA.reset()
            compute_mod(l, RA)
        for n, (l, sub) in enumerate(sublayers):
            last = (n == len(sublayers) - 1)
            dst, dst_key = (outT, "outT") if last else (xres[n % 2], f"xres{n % 2}")
            rb = [b for k_, b in bufs.items() if k_[0] in REGION_KEYS]
            fop = P.op(POOL, lambda e: e.memset(epsc, EPS), writes=rb + [B("epsc")])
            state["fence_op"] = fop
            nb0 = set(bufs.keys())
            if sub == 1 and l % 2 == 0:
                pool_sublayer(l, sub, dst, dst_key)
            elif sub == 1:
                mla_sublayer(l, sub, dst, dst_key)
            else:
                ffn_sublayer(l, sub, dst, dst_key)
            state["src"], state["src_key"] = dst, dst_key
        pump(len(pq))
        P.op(POOL, lambda e: e.memset(epsc, EPS), reads=[B("outT", t) for t in range(NT)], writes=[B("epsc")])
        P.op(SP, lambda e: e.nop(), reads=[B("epsc")] + [B("outT", t) for t in range(NT)])
        if max_ops is not None:
            P.ops = P.ops[:max_ops]
        P.emit(st)
        print("ops", len(P.ops), "sems", P.n_sems, "waits", P.n_waits)
    return nc


def _rope_tables():
    inv = (1.0 / (np.float32(10000.0) ** (np.arange(0, 32, 2, dtype=np.float32) / np.float32(32)))).astype(np.float32)
    ang = (np.arange(S, dtype=np.float32)[:, None] * inv[None, :]).astype(np.float32)
    cos = np.cos(ang).astype(np.float32).T
    sin = np.sin(ang).astype(np.float32).T
    return (np.ascontiguousarray(np.concatenate([cos, cos], axis=0)),
            np.ascontiguousarray(np.concatenate([-sin, sin], axis=0)))


def _shared_inputs(inp):
    f = lambda a: np.ascontiguousarray(a, dtype=np.float32)
    def vecT(v):
        return np.swapaxes(v.reshape(v.shape[:-1] + (8, 128)), -1, -2)
    ada_b = inp["ada_b"].reshape(2, 9, 1024)
    ada_bT = np.transpose(vecT(ada_b), (0, 2, 1, 3)).reshape(2, 128, 72)
    norm_gT = np.transpose(vecT(inp["norm_g"]), (0, 2, 1, 3)).reshape(2, 128, 48)
    w_in = inp["mla_w_in"][0]
    kr = w_in[:, 384:416]
    krp = np.concatenate([kr[:, 16:32], kr[:, 0:16]], axis=1)
    mla_w_in_x = np.concatenate([w_in[:, :384], kr, krp], axis=1)
    wuq = inp["mla_w_uq"][0]
    nope = wuq[:, :, :64].reshape(256, 1024)
    rope = wuq[:, :, 64:]
    ropep = np.concatenate([rope[:, :, 16:32], rope[:, :, 0:16]], axis=2)
    w_uq_x = np.concatenate([nope, rope.reshape(256, 512), ropep.reshape(256, 512)], axis=1)
    w_ukT = np.transpose(inp["mla_w_uk"][0], (2, 1, 0))
    cos, sin = _rope_tables()
    return {
        "ada_w": f(inp["ada_w"]), "ada_bT": f(ada_bT), "norm_gT": f(norm_gT),
        "ffn_w_in": f(inp["ffn_w_in"]), "ffn_w_out": f(inp["ffn_w_out"]),
        "pool_w": f(inp["pool_w"][0]), "pool_bT": f(vecT(inp["pool_b"][0].reshape(1024))),
        "pool_scT": f(vecT(inp["pool_scale"][0])),
        "mla_w_in_x": f(mla_w_in_x), "q_normT": f(inp["mla_q_norm"][0].reshape(2, 128).T),
        "kv_normT": f(inp["mla_kv_norm"][0].reshape(1, 128).T),
        "w_uq_x": f(w_uq_x), "w_ukT": f(w_ukT), "w_uv": f(inp["mla_w_uv"][0].reshape(128, 1024)),
        "w_o": f(inp["mla_w_o"][0]), "rope_cos": cos, "rope_sin": sin,
    }


FUSED = True
ALL_SUBLAYERS = [(0, 0), (0, 1), (0, 2), (1, 0), (1, 1), (1, 2)]
_NC_CACHE = {}


def run_sublayers(xT_list, c, shared, sublayers, core_ids=None):
    key = tuple(sublayers)
    if key not in _NC_CACHE:
        _NC_CACHE[key] = build_program(list(sublayers))
    nc = _NC_CACHE[key]
    n = len(xT_list)
    in_maps = []
    for b in range(n):
        m = dict(shared)
        m["xT"] = xT_list[b]
        m["cT"] = np.ascontiguousarray(c[b].reshape(8, 128).T, dtype=np.float32)
        in_maps.append(m)
    res = run_bass_kernel_spmd(nc, in_maps, core_ids=list(range(n)) if core_ids is None else core_ids)
    return [r["outT"] for r in res.results]


def kernel(**inputs):
    inp = {k: np.asarray(v) for k, v in inputs.items()}
    x = inp["x"]
    shared = _shared_inputs(inp)
    xT_list = [np.ascontiguousarray(x[b].T) for b in range(x.shape[0])]
    if FUSED:
        outs = run_sublayers(xT_list, inp["c"], shared, ALL_SUBLAYERS)
    else:
        mid = run_sublayers(xT_list, inp["c"], shared, ALL_SUBLAYERS[:3])
        outs = run_sublayers([np.ascontiguousarray(m) for m in mid], inp["c"], shared, ALL_SUBLAYERS[3:])
    return np.stack([np.ascontiguousarray(o.T) for o in outs], axis=0).astype(np.float32)
```

```python
from contextlib import ExitStack
import numpy as np
import concourse.bass as bass
import concourse.mybir as mybir
from concourse.bass_utils import run_bass_kernel_spmd

F32 = mybir.dt.float32
BF16 = mybir.dt.bfloat16
U8 = mybir.dt.uint8
ALU = mybir.AluOpType
AF = mybir.ActivationFunctionType

PE, ACT, DVE, POOL, SP = "pe", "act", "dve", "pool", "sp"

D = 1024
S = 4096
T = 512
NT = S // T
KC = 8
DFF = 2816
FC = 22
NH = 16
EPS = 1e-6
ATTN_SCALE = float(96 ** -0.5)
POOL_WINDOWS = (2, 4, 8, 16)
NKC = S // 128


class Buf:
    __slots__ = ("name", "last_writer", "pwriters", "readers", "dsem", "dcount", "excl")

    def __init__(self, name):
        self.name = name
        self.excl = False
        self.last_writer = None
        self.pwriters = []
        self.readers = []
        self.dsem = None
        self.dcount = 0


class Op:
    __slots__ = ("idx", "eng", "fn", "deps", "is_dma", "sem_buf", "token", "signal")

    def __init__(self, idx, eng, fn, is_dma, sem_buf):
        self.idx = idx
        self.eng = eng
        self.fn = fn
        self.deps = []
        self.is_dma = is_dma
        self.sem_buf = sem_buf
        self.token = None
        self.signal = False


class Prog:
    SEM_ROLL = 6000
    DMA_SEM_ROLL = 2048
    SWDGE_WINDOW = 4

    def __init__(self, nc, same_engine_sync=True):
        self.nc = nc
        self.ops = []
        self.same_engine_sync = same_engine_sync
        self.pool_dmas = []
        self.pool_slots = []

    def op(self, eng, fn, reads=(), writes=(), pwrites=(), dma=False, sem_buf=None):
        o = Op(len(self.ops), eng, fn, dma, sem_buf)
        if any(b.excl for b in reads):
            writes = list(writes) + [b for b in reads if b.excl and b not in writes and b not in pwrites]
            reads = [b for b in reads if not b.excl]
        deps = {}
        for b in reads:
            if b.last_writer is not None:
                deps[b.last_writer.idx] = b.last_writer
            for w in b.pwriters:
                deps[w.idx] = w
        for b in writes:
            if b.last_writer is not None:
                deps[b.last_writer.idx] = b.last_writer
            for w in b.pwriters:
                deps[w.idx] = w
            for r in b.readers:
                deps[r.idx] = r
        for b in pwrites:
            if b.last_writer is not None:
                deps[b.last_writer.idx] = b.last_writer
            for r in b.readers:
                deps[r.idx] = r
        o.deps = list(deps.values())
        for b in reads:
            b.readers.append(o)
        for b in writes:
            b.last_writer = o
            b.pwriters = []
            b.readers = []
        for b in pwrites:
            b.pwriters.append(o)
        self.ops.append(o)
        return o

    def dma(self, eng, out, in_, reads=(), writes=(), pwrites=(), sem_buf=None):
        if sem_buf is None:
            sem_buf = writes[0] if writes else (pwrites[0] if pwrites else reads[0])
        if eng == POOL:
            q = self.pool_dmas
            if not self.pool_slots:
                self.pool_slots = [Buf(f"swdge_slot{i_}") for i_ in range(self.SWDGE_WINDOW)]
            sem_buf = self.pool_slots[len(q) % self.SWDGE_WINDOW]
        o = self.op(eng, lambda e: e.dma_start(out=out, in_=in_), reads, writes, pwrites,
                    dma=True, sem_buf=sem_buf)
        if eng == POOL:
            if len(q) >= self.SWDGE_WINDOW:
                o.deps.append(q[-self.SWDGE_WINDOW])
            q.append(o)
        return o

    def emit(self, stack):
        nc = self.nc
        ops = self.ops
        for o in ops:
            if o.is_dma:
                o.signal = True
            for d in o.deps:
                d.signal = True
        eng_sems, eng_cnt, nsem = {}, {}, [0]

        def new_sem(tag):
            nsem[0] += 1
            return stack.enter_context(nc.semaphore(f"s_{tag}_{nsem[0]}"))

        for o in ops:
            if not o.signal:
                continue
            if o.is_dma:
                b = o.sem_buf
                if b.dsem is None or b.dcount >= self.DMA_SEM_ROLL:
                    b.dsem = new_sem("d")
                    b.dcount = 0
                b.dcount += 16
                o.token = (b.dsem, b.dcount)
            else:
                if o.eng not in eng_sems or eng_cnt[o.eng] >= self.SEM_ROLL:
                    eng_sems[o.eng] = new_sem(o.eng)
                    eng_cnt[o.eng] = 0
                eng_cnt[o.eng] += 1
                o.token = (eng_sems[o.eng], eng_cnt[o.eng])
        self.n_sems = nsem[0]
        streams = {}
        for o in ops:
            streams.setdefault(o.eng, []).append(o)
        block = stack.enter_context(nc.Block())
        same = self.same_engine_sync
        nwaits = [0]

        def make(eng_name, lst):
            def body(e):
                waited_eng = {}
                waited_dma = {}
                for o in lst:
                    need = {}
                    need_dma = {}
                    for d in o.deps:
                        if d.is_dma:
                            sem, val = d.token
                            k = id(sem)
                            if waited_dma.get(k, 0) >= val:
                                continue
                            if k not in need_dma or need_dma[k][1] < val:
                                need_dma[k] = (sem, val)
                        else:
                            if d.eng == eng_name and (eng_name == PE or not same):
                                continue
                            if waited_eng.get(d.eng, -1) >= d.idx:
                                continue
                            if d.eng not in need or need[d.eng].idx < d.idx:
                                need[d.eng] = d
                    for k, (sem, val) in need_dma.items():
                        waited_dma[k] = val
                        e.wait_ge(sem, val)
                        nwaits[0] += 1
                    for src, d in need.items():
                        waited_eng[src] = d.idx
                        e.wait_ge(d.token[0], d.token[1])
                        nwaits[0] += 1
                    ins = o.fn(e)
                    if o.signal:
                        ins.then_inc(o.token[0], 16 if o.is_dma else 1)
            return body

        reg = {PE: block.tensor, ACT: block.scalar, DVE: block.vector, POOL: block.gpsimd, SP: block.sync}
        for eng_name, lst in streams.items():
            reg[eng_name](make(eng_name, lst))
        self.n_waits = nwaits[0]


class Arena:
    def __init__(self, tensor, size):
        self.t = tensor
        self.size = size
        self.off = 0

    def reset(self, off=0):
        self.off = off

    def alloc(self, shape, dtype, parts=128):
        esz = 2 if dtype == BF16 else 4
        n = 1
        for s in shape:
            n *= s
        nbytes = (n * esz + 63) // 64 * 64
        assert self.off + nbytes <= self.size, ("arena overflow", self.off, nbytes, self.size)
        ap = self.t[0:parts, self.off:self.off + n * esz].bitcast(dtype)
        self.off += nbytes
        if len(shape) == 2:
            ap = ap.rearrange("p (a b) -> p a b", a=shape[0])
        elif len(shape) == 3:
            ap = ap.rearrange("p (a b c) -> p a b c", a=shape[0], b=shape[1])
        return ap


def build_program(sublayers, debug=False, max_ops=None, marks=None):
    nc = bass.Bass("TRN2", target_bir_lowering=False)

    def din(name, shape, dt=F32):
        return nc.dram_tensor(name, list(shape), dt, kind="ExternalInput").ap()

    xT = din("xT", [D, S])
    cT = din("cT", [128, KC])
    ada_w = din("ada_w", [2, D, 9 * D])
    ada_bT = din("ada_bT", [2, 128, 72])
    norm_gT = din("norm_gT", [2, 128, 48])
    ffn_w_in = din("ffn_w_in", [2, 2, D, 2 * DFF])
    ffn_w_out = din("ffn_w_out", [2, 2, DFF, D])
    pool_w = din("pool_w", [4, 256, 256])
    pool_bT = din("pool_bT", [128, 8])
    pool_scT = din("pool_scT", [128, 8])
    mla_w_in_x = din("mla_w_in_x", [D, 448])
    q_normT = din("q_normT", [128, 2])
    kv_normT = din("kv_normT", [128, 1])
    w_uq_x = din("w_uq_x", [256, 2048])
    w_ukT = din("w_ukT", [64, NH, 128])
    w_uv = din("w_uv", [128, NH * 64])
    w_o = din("w_o", [NH * 64, D])
    rope_cos = din("rope_cos", [32, S])
    rope_sin = din("rope_sin", [32, S])
    outT = nc.dram_tensor("outT", [D, S], F32, kind="ExternalOutput").ap()

    xres = [nc.dram_tensor(f"xres{i}", [D, S], F32).ap() for i in range(2)]
    ffn_ids = sorted({(l, s // 2) for (l, s) in sublayers if s != 1})
    win_s = {k: nc.dram_tensor(f"win_s{k[0]}{k[1]}", [FC, 128, KC * 256], BF16).ap() for k in ffn_ids}
    wout_s = {k: nc.dram_tensor(f"wout_s{k[0]}{k[1]}", [KC, 128, FC * 128], BF16).ap() for k in ffn_ids}
    wo_s = nc.dram_tensor("wo_s", [KC, 64, NH * 128], BF16).ap()

    P = Prog(nc)
    bufs = {}

    REGION_KEYS = {"ada_ring", "actT", "win_ring", "wout_ring", "xe", "hE", "tE", "ua", "ub", "ua2", "ub2", "dT", "pw",
                   "sqE", "rsE", "cq_all", "ckvT", "krope", "winx", "wuq", "wuk", "wuv", "wo_ring", "Vg", "qnope",
                   "qlat", "qrope", "pT", "osb", "rc", "oT", "tab_c", "tab_s", "t1", "t2", "cq32"}
    state = {"fence_op": None}

    def B(*key):
        if key not in bufs:
            b = Buf(str(key))
            if key[0] in REGION_KEYS:
                b.last_writer = state["fence_op"]
            bufs[key] = b
        return bufs[key]

    st = ExitStack()
    with st:
        COMMON = 89 * 1024
        REGION = 118 * 1024
        common_t = st.enter_context(nc.sbuf_tensor("common", [128, COMMON], U8))
        region_t = st.enter_context(nc.sbuf_tensor("region", [128, REGION], U8))
        CA = Arena(common_t, COMMON)
        RA = Arena(region_t, REGION)
        ps = [st.enter_context(nc.psum_tensor(f"ps{i}", [128, 512], F32)) for i in range(8)]
        PB = [B("psum", i) for i in range(8)]
        for b_ in PB:
            b_.excl = True

        xslot = [CA.alloc([KC, T], F32) for _ in range(2)]
        hT = CA.alloc([KC, T], BF16)
        tmp32 = CA.alloc([KC, T], F32)
        sq8 = tmp32.rearrange("p a b -> p (a b)")[:, 0:KC * T // 2].bitcast(BF16).rearrange("p (a b) -> p a b", a=KC)
        ysb = CA.alloc([KC, T], F32)
        rsA = CA.alloc([1, T], F32)[:, 0, :]
        rsB = CA.alloc([1, T], F32)[:, 0, :]
        sg = [CA.alloc([1, T], F32)[:, 0, :] for _ in range(2)]
        sqr = [CA.alloc([1, T], BF16)[:, 0, :] for _ in range(4)]
        ones_bf = CA.alloc([1, 128], BF16)[:, 0, :]
        ones_f = CA.alloc([1, 128], F32)[:, 0, :]
        epsc = CA.alloc([1, 1], F32)[:, 0, :]
        c_sb = CA.alloc([1, KC], F32)[:, 0, :]
        sc_bf = CA.alloc([1, KC], BF16)[:, 0, :]
        modT = [CA.alloc([1, 72], F32)[:, 0, :] for _ in range(2)]
        adab = [CA.alloc([1, 72], F32)[:, 0, :] for _ in range(2)]
        ng = [CA.alloc([1, 48], F32)[:, 0, :] for _ in range(2)]
        vecA = CA.alloc([6, KC], F32)
        vecB = CA.alloc([6, KC], F32)
        smallv = CA.alloc([1, 32], F32)[:, 0, :]
        pT_extra = CA.alloc([1, T], BF16)[:, 0, :]
        print("common arena used", CA.off, "of", COMMON)

        P.op(POOL, lambda e: e.memset(ones_bf, 1.0), writes=[B("ones_bf")])
        P.op(POOL, lambda e: e.memset(ones_f, 1.0), writes=[B("ones_f")])
        P.op(POOL, lambda e: e.memset(epsc, EPS), writes=[B("epsc")])
        P.dma(SP, c_sb, cT, writes=[B("c_sb")])
        for l in range(2):
            P.dma(SP, adab[l], ada_bT[l], writes=[B("adab", l)])
            P.dma(SP, ng[l], norm_gT[l], writes=[B("ng", l)])
        P.dma(SP, smallv[:, 0:8], pool_bT, pwrites=[B("smallv")])
        P.dma(SP, smallv[:, 8:16], pool_scT, pwrites=[B("smallv")])
        P.dma(SP, smallv[:, 16:18], q_normT, pwrites=[B("smallv")])
        P.dma(SP, smallv[:, 18:19], kv_normT, pwrites=[B("smallv")])
        P.op(ACT, lambda e: e.activation(out=sc_bf, in_=c_sb, func=AF.Silu), reads=[B("c_sb")], writes=[B("sc_bf")])

        pq = []

        def precast(k):
            l, f = k
            wi = ffn_w_in[l, f].rearrange("(kc p) n -> p kc n", p=128)
            wo = ffn_w_out[l, f].rearrange("(fc p) n -> p fc n", p=128)
            for j in range(FC):
                dst = win_s[k][j].rearrange("p (kc x) -> p kc x", x=256)
                for gu in range(2):
                    c0 = gu * DFF + j * 128
                    pq.append((("win_s", k), dst[:, :, gu * 128:(gu + 1) * 128], wi[:, :, c0:c0 + 128]))
            for c in range(KC):
                dst = wout_s[k][c].rearrange("p (fc d) -> p fc d", d=128)
                for h0 in (0, 11):
                    pq.append((("wout_s", k), dst[:, h0:h0 + 11, :], wo[:, h0:h0 + 11, c * 128:(c + 1) * 128]))

        def pump(n):
            for _ in range(min(n, len(pq))):
                key, dst, src = pq.pop(0)
                P.dma(POOL, dst, src, pwrites=[B(*key)])

        def flush_precast(k):
            while any(key[1] == k for (key, _, _) in pq):
                pump(1)

        def compute_mod(l, RAm):
            ada_ring = [RAm.alloc([KC, 512], BF16) for _ in range(2)]
            aw = ada_w[l].rearrange("(kc p) n -> p kc n", p=128)
            mps = ps[7]
            for bi in range(18):
                slot = ada_ring[bi % 2]
                sb = B("ada_ring", bi % 2)
                P.dma(POOL, slot, aw[:, :, bi * 512:(bi + 1) * 512], writes=[sb])
                for cl in range(4):
                    gc = bi * 4 + cl
                    for kc in range(KC):
                        first = (gc == 0 and kc == 0)
                        P.op(PE, lambda e, slot=slot, cl=cl, kc=kc, gc=gc: e.matmul(
                            mps[:, gc:gc + 1], lhsT=slot[:, kc, cl * 128:(cl + 1) * 128],
                            rhs=sc_bf[:, kc:kc + 1], start=(kc == 0), stop=(kc == KC - 1)),
                            reads=[sb, B("sc_bf")], writes=[PB[7]] if first else (),
                            pwrites=() if first else [PB[7]])
            P.op(DVE, lambda e: e.tensor_tensor(out=modT[l], in0=mps[:, 0:72], in1=adab[l], op=ALU.add),
                 reads=[PB[7], B("adab", l)], writes=[B("modT", l)])
            for sub in range(3):
                i = l * 3 + sub
                wgt = 1.0 if sub == 1 else 0.5
                scale = modT[l][:, (3 * sub + 1) * 8:(3 * sub + 2) * 8]
                gate = modT[l][:, (3 * sub + 2) * 8:(3 * sub + 3) * 8]
                gpre = ng[l][:, (2 * sub) * 8:(2 * sub + 1) * 8]
                gpost = ng[l][:, (2 * sub + 1) * 8:(2 * sub + 2) * 8]
                P.op(DVE, lambda e, i=i, scale=scale, gpre=gpre: e.scalar_tensor_tensor(
                    out=vecA[:, i, :], in0=scale, scalar=1.0, in1=gpre, op0=ALU.add, op1=ALU.mult),
                    reads=[B("modT", l), B("ng", l)], writes=[B("vecA", i)])
                P.op(DVE, lambda e, i=i, gate=gate, gpost=gpost: e.scalar_tensor_tensor(
                    out=vecB[:, i, :], in0=gate, scalar=1.0, in1=gpost, op0=ALU.add, op1=ALU.mult),
                    reads=[B("modT", l), B("ng", l)], writes=[B("vecB", i)])
                P.op(DVE, lambda e, i=i, wgt=wgt: e.tensor_scalar(
                    out=vecB[:, i, :], in0=vecB[:, i, :], scalar1=wgt, scalar2=None, op0=ALU.mult),
                    reads=[B("vecB", i)], writes=[B("vecB", i)])

        state.update({"src": xT, "src_key": "xT"})

        def dram_tile(ap, t):
            return ap.rearrange("(ch p) s -> p ch s", p=128)[:, :, t * T:(t + 1) * T]

        def XB(slot):
            return [B("xslot", slot, c) for c in range(KC)]

        def load_x(t, slot):
            P.dma(POOL, xslot[slot], dram_tile(state["src"], t), reads=[B(state["src_key"], t)],
                  writes=XB(slot))

        def store_x(t, slot, dst, dst_key):
            P.dma(POOL, dram_tile(dst, t), xslot[slot], reads=XB(slot), writes=[B(dst_key, t)],
                  sem_buf=B("xstore", slot))

        def prologue_steps(xap, xbuf, i, hdst, hbufs):
            l, sub = divmod(i, 3)
            shift = modT[l][:, (3 * sub) * 8:(3 * sub + 1) * 8]

            def s_sq():
                P.op(POOL, lambda e: e.tensor_tensor(out=sq8, in0=xap, in1=xap, op=ALU.mult), reads=xbuf, writes=[B("tmp32")])

            def s_mm():
                for kc in range(KC):
                    P.op(PE, lambda e, kc=kc: e.matmul(ps[6][:], lhsT=ones_bf, rhs=sq8[:, kc, :], start=(kc == 0), stop=(kc == KC - 1)),
                         reads=[B("tmp32"), B("ones_bf")], writes=[PB[6]])

            def s_sqrt():
                P.op(ACT, lambda e: e.activation(out=rsA, in_=ps[6][:], func=AF.Sqrt, bias=epsc, scale=1.0 / D),
                     reads=[PB[6], B("epsc")], writes=[B("rsA")])
                P.op(DVE, lambda e: e.reciprocal(out=rsA, in_=rsA), reads=[B("rsA")], writes=[B("rsA")])

            def s_mul():
                P.op(DVE, lambda e: e.tensor_tensor(out=tmp32, in0=xap, in1=rsA.unsqueeze(1).broadcast_to([128, KC, T]), op=ALU.mult),
                     reads=xbuf + [B("rsA")], writes=[B("tmp32")])

            def mk(c):
                def s_mod():
                    P.op(ACT, lambda e: e.activation(out=hdst[:, c, :], in_=tmp32[:, c, :], func=AF.Identity,
                                                     bias=shift[:, c:c + 1], scale=vecA[:, i, c:c + 1]),
                         reads=[B("tmp32"), B("vecA", i), B("modT", l)], writes=[hbufs[c]])
                return s_mod

            return [s_sq, s_mm, s_sqrt, s_mul] + [mk(c) for c in range(KC)]

        def prologue(xap, xbuf, i, hdst, hbufs):
            for st_ in prologue_steps(xap, xbuf, i, hdst, hbufs):
                st_()

        YB = [B("ysb", c) for c in range(KC)]

        def post_stats_sq(c):
            r, rb = sqr[c % 4], B("sqr", c % 4)
            P.op(POOL, lambda e: e.tensor_tensor(out=r, in0=ysb[:, c, :], in1=ysb[:, c, :], op=ALU.mult), reads=[YB[c]], writes=[rb])

        def post_stats_mm(c):
            r, rb = sqr[c % 4], B("sqr", c % 4)
            P.op(PE, lambda e: e.matmul(ps[7][:], lhsT=ones_bf, rhs=r, start=(c == 0), stop=(c == KC - 1)),
                 reads=[rb, B("ones_bf")], writes=[PB[7]])

        def epilogue(xap, xbuf, i):
            P.op(ACT, lambda e: e.activation(out=rsB, in_=ps[7][:], func=AF.Sqrt, bias=epsc, scale=1.0 / D),
                 reads=[PB[7], B("epsc")], writes=[B("rsB")])
            P.op(DVE, lambda e: e.reciprocal(out=rsB, in_=rsB), reads=[B("rsB")], writes=[B("rsB")])
            P.op(DVE, lambda e: e.tensor_tensor(out=ysb, in0=ysb, in1=rsB.unsqueeze(1).broadcast_to([128, KC, T]), op=ALU.mult),
                 reads=YB + [B("rsB")], writes=YB)
            for c in range(KC):
                P.op(DVE, lambda e, c=c: e.scalar_tensor_tensor(out=xap[:, c, :], in0=ysb[:, c, :], scalar=vecB[:, i, c:c + 1],
                                                                in1=xap[:, c, :], op0=ALU.mult, op1=ALU.add),
                     reads=[YB[c], B("vecB", i), xbuf[c]], writes=[xbuf[c]])

        def ffn_sublayer(l, sub, dst, dst_key):
            i = l * 3 + sub
            k = (l, sub // 2)
            flush_precast(k)
            RA.reset()
            actT = RA.alloc([FC, T], BF16)
            win_ring = [RA.alloc([KC, 256], BF16) for _ in range(3)]
            wout_ring = [RA.alloc([FC, 128], BF16) for _ in range(2)]
            hb = [B("hT", c) for c in range(KC)]
            n_in, n_out = NT * FC, NT * KC
            issued = {"in": 0, "out": 0}

            def issue_in(upto):
                while issued["in"] < min(upto, n_in):
                    n = issued["in"]
                    j_ = n % FC
                    P.dma(SP, win_ring[n % 3], win_s[k][j_].rearrange("p (kc x) -> p kc x", x=256),
                          reads=[B("win_s", k)], writes=[B("win_ring", n % 3)])
                    issued["in"] += 1

            def issue_out(upto):
                while issued["out"] < min(upto, n_out):
                    m = issued["out"]
                    c_ = m % KC
                    P.dma(SP, wout_ring[m % 2], wout_s[k][c_].rearrange("p (fc d) -> p fc d", d=128),
                          reads=[B("wout_s", k)], writes=[B("wout_ring", m % 2)])
                    issued["out"] += 1

            load_x(0, 0)
            issue_in(3)
            for st_ in prologue_steps(xslot[0], XB(0), i, hT, hb):
                st_()
            for t in range(NT):
                slot = t % 2
                xap, xbuf = xslot[slot], XB(slot)
                nxt = []
                if t + 1 < NT:
                    load_x(t + 1, (t + 1) % 2)
                    nxt = prologue_steps(xslot[(t + 1) % 2], XB((t + 1) % 2), i, hT, hb)
                issue_out(t * KC + 2)
                for j in range(FC):
                    n = t * FC + j
                    issue_in(n + 3)
                    wslot, wbuf = win_ring[n % 3], B("win_ring", n % 3)
                    gb, ub = j % 2, 2 + j % 2
                    for kc in range(KC):
                        P.op(PE, lambda e, kc=kc, wslot=wslot, gb=gb: e.matmul(ps[gb][:], lhsT=wslot[:, kc, 0:128], rhs=hT[:, kc, :],
                                                                             start=(kc == 0), stop=(kc == KC - 1)),
                             reads=[wbuf, hb[kc]], writes=[PB[gb]])
                    for kc in range(KC):
                        P.op(PE, lambda e, kc=kc, wslot=wslot, ub=ub: e.matmul(ps[ub][:], lhsT=wslot[:, kc, 128:256], rhs=hT[:, kc, :],
                                                                             start=(kc == 0), stop=(kc == KC - 1)),
                             reads=[wbuf, hb[kc]], writes=[PB[ub]])
                    sgt, sgb = sg[j % 2], B("sg", j % 2)
                    P.op(ACT, lambda e, gb=gb, sgt=sgt: e.activation(out=sgt, in_=ps[gb][:], func=AF.Silu), reads=[PB[gb]], writes=[sgb])
                    P.op(DVE, lambda e, ub=ub, sgt=sgt, j=j: e.tensor_tensor(out=actT[:, j, :], in0=sgt, in1=ps[ub][:], op=ALU.mult),
                         reads=[sgb, PB[ub]], writes=[B("actT", j)])
                    if j == 13 and nxt:
                        nxt.pop(0)()
                for c in range(KC):
                    m = t * KC + c
                    issue_out(m + 2)
                    wslot, wbuf = wout_ring[m % 2], B("wout_ring", m % 2)
                    yb = 4 + c % 2
                    for fc in range(FC):
                        P.op(PE, lambda e, fc=fc, wslot=wslot, yb=yb: e.matmul(ps[yb][:], lhsT=wslot[:, fc, :], rhs=actT[:, fc, :],
                                                                             start=(fc == 0), stop=(fc == FC - 1)),
                             reads=[wbuf, B("actT", fc)], writes=[PB[yb]])
                    P.op(DVE, lambda e, c=c, yb=yb: e.tensor_copy(out=ysb[:, c, :], in_=ps[yb][:]), reads=[PB[yb]], writes=[YB[c]])
                    post_stats_sq(c)
                    if c > 0:
                        post_stats_mm(c - 1)
                    take = {0: 1, 1: 2, 2: 1}.get(c, 2)
                    for _ in range(take):
                        if nxt:
                            nxt.pop(0)()
                while nxt:
                    nxt.pop(0)()
                post_stats_mm(KC - 1)
                epilogue(xap, xbuf, i)
                store_x(t, slot, dst, dst_key)
                pump(8)

        def pool_sublayer(l, sub, dst, dst_key):
            i = l * 3 + sub
            RA.reset()
            W = T + 16
            xe = RA.alloc([KC, W], F32)
            hE = RA.alloc([KC, W], F32)
            tE = RA.alloc([KC, W], F32)
            ua = RA.alloc([2, W], F32)
            ub_ = RA.alloc([2, W], F32)
            ua2 = RA.alloc([2, W], F32)
            ub2 = RA.alloc([2, W], F32)
            dT = RA.alloc([KC, T], BF16)
            pw = RA.alloc([4, 2, 256], BF16)
            sqE = RA.alloc([KC, W], BF16)
            rsE = RA.alloc([1, W], F32)[:, 0, :]
            shift = modT[l][:, (3 * sub) * 8:(3 * sub + 1) * 8]
            P.dma(POOL, pw, pool_w.rearrange("g (cc p) d -> p g cc d", p=128), writes=[B("pw")])
            P.op(DVE, lambda e: e.tensor_tensor(out=smallv[:, 19:27], in0=smallv[:, 0:8], in1=smallv[:, 8:16], op=ALU.mult),
                 reads=[B("smallv")], writes=[B("bsc")])
            src = state["src"].rearrange("(ch p) s -> p ch s", p=128)
            def p_load(t):
                slot = t % 2
                e0 = max(t * T - 8, 0)
                e1 = min((t + 1) * T + 8, S)
                c0 = e0 - (t * T - 8)
                c1 = c0 + (e1 - e0)
                rd = [B(state["src_key"], tt) for tt in range(max(t - 1, 0), min(t + 2, NT))]
                P.dma(POOL, xe[:, :, c0:c1], src[:, :, e0:e1], reads=rd, writes=[B("xe")])

            def p_front(t):
                slot = t % 2
                e0 = max(t * T - 8, 0)
                e1 = min((t + 1) * T + 8, S)
                c0 = e0 - (t * T - 8)
                c1 = c0 + (e1 - e0)
                P.op(ACT, lambda e, c0=c0, c1=c1: e.activation(out=sqE[:, :, c0:c1], in_=xe[:, :, c0:c1], func=AF.Square),
                     reads=[B("xe")], writes=[B("sqE")])
                P.op(ACT, lambda e, slot=slot: e.activation(out=xslot[slot], in_=xe[:, :, 8:8 + T], func=AF.Identity),
                     reads=[B("xe")], writes=XB(slot))
                for kc in range(KC):
                    P.op(PE, lambda e, kc=kc: e.matmul(ps[0][:], lhsT=ones_bf, rhs=sqE[:, kc, 8:8 + T], start=(kc == 0), stop=(kc == KC - 1)),
                         reads=[B("sqE"), B("ones_bf")], writes=[PB[0]])
                if c0 == 0:
                    for kc in range(KC):
                        P.op(PE, lambda e, kc=kc: e.matmul(ps[1][:, 0:8], lhsT=ones_bf, rhs=sqE[:, kc, 0:8], start=(kc == 0), stop=(kc == KC - 1)),
                             reads=[B("sqE"), B("ones_bf")], writes=[PB[1]])
                if c1 == W:
                    for kc in range(KC):
                        P.op(PE, lambda e, kc=kc: e.matmul(ps[1][:, 8:16], lhsT=ones_bf, rhs=sqE[:, kc, W - 8:W], start=(kc == 0), stop=(kc == KC - 1)),
                             reads=[B("sqE"), B("ones_bf")], writes=[PB[1]] if c0 != 0 else (), pwrites=[PB[1]] if c0 == 0 else ())
                P.op(ACT, lambda e: e.activation(out=rsE[:, 8:8 + T], in_=ps[0][:], func=AF.Sqrt, bias=epsc, scale=1.0 / D),
                     reads=[PB[0], B("epsc")], writes=[B("rsE")])
                if c0 == 0:
                    P.op(ACT, lambda e: e.activation(out=rsE[:, 0:8], in_=ps[1][:, 0:8], func=AF.Sqrt, bias=epsc, scale=1.0 / D),
                         reads=[PB[1], B("epsc")], pwrites=[B("rsE")])
                if c1 == W:
                    P.op(ACT, lambda e: e.activation(out=rsE[:, W - 8:W], in_=ps[1][:, 8:16], func=AF.Sqrt, bias=epsc, scale=1.0 / D),
                         reads=[PB[1], B("epsc")], pwrites=[B("rsE")])

            def p_front_b(t):
                e0 = max(t * T - 8, 0)
                e1 = min((t + 1) * T + 8, S)
                c0 = e0 - (t * T - 8)
                c1 = c0 + (e1 - e0)
                P.op(DVE, lambda e, c0=c0, c1=c1: e.reciprocal(out=rsE[:, c0:c1], in_=rsE[:, c0:c1]), reads=[B("rsE")], writes=[B("rsE")])
                P.op(DVE, lambda e, c0=c0, c1=c1: e.tensor_tensor(out=tE[:, :, c0:c1], in0=xe[:, :, c0:c1],
                                                                  in1=rsE[:, c0:c1].unsqueeze(1).broadcast_to([128, KC, c1 - c0]), op=ALU.mult),
                     reads=[B("xe"), B("rsE")], writes=[B("tE")])

            def p_mid(t):
                slot = t % 2
                e0 = max(t * T - 8, 0)
                e1 = min((t + 1) * T + 8, S)
                c0 = e0 - (t * T - 8)
                c1 = c0 + (e1 - e0)
                for c in range(KC):
                    P.op(ACT, lambda e, c=c, c0=c0, c1=c1: e.activation(out=hE[:, c, c0:c1], in_=tE[:, c, c0:c1], func=AF.Identity,
                                                                        bias=shift[:, c:c + 1], scale=vecA[:, i, c:c + 1]),
                         reads=[B("tE"), B("vecA", i), B("modT", l)], writes=[B("hE")] if c == 0 else (), pwrites=[B("hE")] if c else ())
                if c0 > 0:
                    P.op(POOL, lambda e, c0=c0: e.memset(hE[:, :, 0:c0], 0.0), reads=[B("hE")], writes=[B("hE")])
                if c1 < W:
                    P.op(POOL, lambda e, c1=c1: e.memset(hE[:, :, c1:W], 0.0), reads=[B("hE")], writes=[B("hE")])
                for g in range(4):
                    eng = DVE if g < 3 else POOL
                    hg = hE[:, 2 * g:2 * g + 2, :]
                    bufa, bufb = (ua, ub_) if g < 3 else (ua2, ub2)
                    ka, kb = ("ua", "ub") if g < 3 else ("ua2", "ub2")
                    P.op(eng, lambda e, hg=hg, bufa=bufa: e.tensor_tensor(out=bufa[:, :, 1:W], in0=hg[:, :, 0:W - 1], in1=hg[:, :, 1:W], op=ALU.add),
                         reads=[B("hE")], writes=[B(ka)])
                    cur, curk, oth, othk = bufa, ka, bufb, kb
                    lo, hi = 1, W
                    for step in range(g):
                        sh = 1 << step
                        nlo, nhi = lo + sh, hi - sh
                        P.op(eng, lambda e, cur=cur, oth=oth, nlo=nlo, nhi=nhi, sh=sh: e.tensor_tensor(
                            out=oth[:, :, nlo:nhi], in0=cur[:, :, nlo - sh:nhi - sh], in1=cur[:, :, nlo + sh:nhi + sh], op=ALU.add),
                            reads=[B(curk)], writes=[B(othk)])
                        cur, curk, oth, othk = oth, othk, cur, curk
                        lo, hi = nlo, nhi
                    w = POOL_WINDOWS[g]
                    P.op(DVE, lambda e, cur=cur, hg=hg, g=g, w=w: e.scalar_tensor_tensor(
                        out=dT[:, 2 * g:2 * g + 2, :], in0=cur[:, :, 8:8 + T], scalar=1.0 / w, in1=hg[:, :, 8:8 + T],
                        op0=ALU.mult, op1=ALU.subtract), reads=[B(curk), B("hE")], writes=[B("dT", g)])
                    fix = []
                    if t == 0:
                        fix += [(tt, tt + w // 2) for tt in range(w // 2)]
                    if t == NT - 1:
                        fix += [(T - 1 - u, u + 1 + w // 2) for u in range(w // 2 - 1)]
                    for (col, cntv) in fix:
                        P.op(DVE, lambda e, cur=cur, hg=hg, g=g, col=col, cntv=cntv: e.scalar_tensor_tensor(
                            out=dT[:, 2 * g:2 * g + 2, col:col + 1], in0=cur[:, :, 8 + col:9 + col], scalar=1.0 / cntv,
                            in1=hg[:, :, 8 + col:9 + col], op0=ALU.mult, op1=ALU.subtract),
                            reads=[B(curk), B("hE"), B("dT", g)], writes=[B("dT", g)])

            def p_back(t):
                slot = t % 2
                e0 = max(t * T - 8, 0)
                e1 = min((t + 1) * T + 8, S)
                c0 = e0 - (t * T - 8)
                c1 = c0 + (e1 - e0)
                for g in range(4):
                    for dch in range(2):
                        ch = 2 * g + dch
                        yb = 4 + ch % 2
                        for cc in range(2):
                            P.op(PE, lambda e, g=g, dch=dch, cc=cc, yb=yb: e.matmul(
                                ps[yb][:], lhsT=pw[:, g, cc, dch * 128:(dch + 1) * 128], rhs=dT[:, 2 * g + cc, :],
                                start=(cc == 0), stop=(cc == 1)), reads=[B("pw"), B("dT", g)], writes=[PB[yb]])
                        P.op(ACT, lambda e, ch=ch, yb=yb: e.activation(out=ysb[:, ch, :], in_=ps[yb][:], func=AF.Identity,
                                                                       bias=smallv[:, 19 + ch:20 + ch], scale=smallv[:, 8 + ch:9 + ch]),
                             reads=[PB[yb], B("smallv"), B("bsc")], writes=[YB[ch]])
                        post_stats_sq(ch)
                        if ch >= 2:
                            post_stats_mm(ch - 2)
                post_stats_mm(KC - 2)
                post_stats_mm(KC - 1)
                epilogue(xslot[slot], XB(slot), i)
                store_x(t, slot, dst, dst_key)
                pump(8)

            p_load(0)
            p_front(0)
            p_front_b(0)
            for t in range(NT):
                if t + 1 < NT:
                    p_load(t + 1)
                p_mid(t)
                if t + 1 < NT:
                    p_front(t + 1)
                p_back(t)
                if t + 1 < NT:
                    p_front_b(t + 1)
                if t == 0 and state.get("defer_mod") is not None:
                    compute_mod(state["defer_mod"], RA)
                    state["defer_mod"] = None

        def mla_sublayer(l, sub, dst, dst_key):
            i = l * 3 + sub
            RA.reset()
            cq_all = RA.alloc([2, S], BF16)
            ckvT = RA.alloc([1, S], BF16)[:, 0, :]
            krope = RA.alloc([1, S], BF16)[:, 0, :]
            winx = RA.alloc([KC, 448], BF16)
            wuq = RA.alloc([2, 2048], BF16)
            wuk = RA.alloc([NH, 128], BF16, parts=64)
            wuv = RA.alloc([1, NH * 64], BF16)[:, 0, :]
            wo_ring = [RA.alloc([NH, 128], BF16) for _ in range(2)]
            Vg = RA.alloc([NKC, 4, 65], BF16)
            qnope = RA.alloc([1, T], BF16, parts=64)[:, 0, :]
            qlat = [RA.alloc([1, T], BF16)[:, 0, :] for _ in range(2)]
            qrope = [RA.alloc([1, T], BF16)[:, 0, :] for _ in range(2)]
            pT = [RA.alloc([1, T], BF16)[:, 0, :] for _ in range(3)] + [pT_extra]
            osb = RA.alloc([1, T], F32, parts=65)[:, 0, :]
            rc = RA.alloc([1, T], F32, parts=64)[:, 0, :]
            oT = RA.alloc([NH, T], BF16)
            tab = RA.alloc([2, T], F32, parts=32)
            t1 = RA.alloc([1, T], F32)[:, 0, :]
            t2 = RA.alloc([1, T], F32)[:, 0, :]
            cq32 = RA.alloc([2, T], F32)
            print("mla region used", RA.off, "of", REGION)
            qn = smallv[:, 16:18]
            kvn = smallv[:, 18:19]
            P.dma(POOL, winx, mla_w_in_x.rearrange("(kc p) n -> p kc n", p=128), writes=[B("winx")])
            P.dma(POOL, wuq, w_uq_x.rearrange("(k p) n -> p k n", p=128), writes=[B("wuq")])
            P.dma(POOL, wuk, w_ukT, writes=[B("wuk")])
            P.dma(POOL, wuv, w_uv, writes=[B("wuv")])
            wov = w_o.rearrange("(h v) d -> v h d", v=64)
            for c_ in range(KC):
                P.dma(POOL, wo_s[c_].rearrange("v (h d) -> v h d", d=128), wov[:, :, c_ * 128:(c_ + 1) * 128], pwrites=[B("wo_s")])
            wo_issued = [0]

            def issue_wo(upto):
                while wo_issued[0] < min(upto, NT * KC):
                    m_ = wo_issued[0]
                    P.dma(SP, wo_ring[m_ % 2][0:64], wo_s[m_ % KC].rearrange("v (h d) -> v h d", d=128),
                          reads=[B("wo_s")], writes=[B("wo_ring", m_ % 2)])
                    wo_issued[0] += 1
            hb = [B("hT", c) for c in range(KC)]
            P.op(POOL, lambda e: e.memset(krope, 0.0), writes=[B("krope", t_) for t_ in range(NT)])
            P.op(POOL, lambda e: e.memset(oT, 0.0), writes=[B("oT", h_) for h_ in range(NH)])
            for r_ in range(2):
                P.op(POOL, lambda e, r_=r_: e.memset(wo_ring[r_], 0.0), writes=[B("wo_ring", r_)])
            for q_ in range(2):
                P.op(POOL, lambda e, q_=q_: e.memset(qrope[q_], 0.0), writes=[B("qrope", q_)])

            TAB = [B("tab_c"), B("tab_s")]

            def load_tab(t):
                P.dma(SP, tab[:, 0, :], rope_cos[:, t * T:(t + 1) * T], writes=[TAB[0]])
                P.dma(SP, tab[:, 1, :], rope_sin[:, t * T:(t + 1) * T], writes=[TAB[1]])

            def rope_combine(psa, psb, pba, pbb, dst_ap, dst_buf, eng2=POOL):
                P.op(DVE, lambda e: e.tensor_tensor(out=t1[0:32, :], in0=psa, in1=tab[:, 0, :], op=ALU.mult),
                     reads=[pba, TAB[0]], writes=[B("t1")])
                P.op(DVE, lambda e: e.tensor_tensor(out=t2[0:32, :], in0=psb, in1=tab[:, 1, :], op=ALU.mult),
                     reads=[pbb, TAB[1]], writes=[B("t2")])
                P.op(eng2, lambda e: e.tensor_tensor(out=dst_ap, in0=t1[0:32, :], in1=t2[0:32, :], op=ALU.add),
                     reads=[B("t1"), B("t2")], writes=[dst_buf])

            def qside(h, tok):
                for k2 in range(2):
                    P.op(PE, lambda e, k2=k2: e.matmul(ps[4][0:64, :], lhsT=wuq[:, k2, h * 64:(h + 1) * 64], rhs=cq_all[:, k2, tok],
                                                       start=(k2 == 0), stop=(k2 == 1)),
                         reads=[B("wuq"), B("cq_all")], writes=[PB[4]])
                for k2 in range(2):
                    P.op(PE, lambda e, k2=k2: e.matmul(ps[5][0:32, :], lhsT=wuq[:, k2, 1024 + h * 32:1024 + (h + 1) * 32], rhs=cq_all[:, k2, tok],
                                                       start=(k2 == 0), stop=(k2 == 1)),
                         reads=[B("wuq"), B("cq_all")], writes=[PB[5]])
                P.op(DVE, lambda e: e.tensor_copy(out=qnope, in_=ps[4][0:64, :]), reads=[PB[4]], writes=[B("qnope")])
                P.op(DVE, lambda e: e.tensor_tensor(out=t1[0:32, :], in0=ps[5][0:32, :], in1=tab[:, 0, :], op=ALU.mult),
                     reads=[PB[5], TAB[0]], writes=[B("t1")])
                for k2 in range(2):
                    P.op(PE, lambda e, k2=k2: e.matmul(ps[4][0:32, :], lhsT=wuq[:, k2, 1536 + h * 32:1536 + (h + 1) * 32], rhs=cq_all[:, k2, tok],
                                                       start=(k2 == 0), stop=(k2 == 1)),
                         reads=[B("wuq"), B("cq_all")], writes=[PB[4]])
                P.op(PE, lambda e: e.matmul(ps[5][:], lhsT=wuk[:, h, :], rhs=qnope, start=True, stop=True),
                     reads=[B("wuk"), B("qnope")], writes=[PB[5]])
                P.op(DVE, lambda e: e.tensor_tensor(out=t2[0:32, :], in0=ps[4][0:32, :], in1=tab[:, 1, :], op=ALU.mult),
                     reads=[PB[4], TAB[1]], writes=[B("t2")])
                P.op(DVE, lambda e: e.tensor_copy(out=qlat[h % 2], in_=ps[5][:]), reads=[PB[5]], writes=[B("qlat", h % 2)])
                P.op(POOL, lambda e: e.tensor_tensor(out=qrope[h % 2][0:32, :], in0=t1[0:32, :], in1=t2[0:32, :], op=ALU.add),
                     reads=[B("t1"), B("t2")], writes=[B("qrope", h % 2)])

            load_x(0, 0)
            load_tab(0)
            prologue(xslot[0], XB(0), i, hT, hb)
            for t in range(NT):
                slot = t % 2
                tok = slice(t * T, (t + 1) * T)
                outs = [(ps[0][:], 0, 128, PB[0]), (ps[1][:], 128, 128, PB[1]), (ps[2][:], 256, 128, PB[2]),
                        (ps[3][0:32, :], 384, 32, PB[3]), (ps[4][0:32, :], 416, 32, PB[4])]
                for (pap, c0, m, pb) in outs:
                    for kc in range(KC):
                        P.op(PE, lambda e, pap=pap, c0=c0, m=m, kc=kc: e.matmul(pap, lhsT=winx[:, kc, c0:c0 + m], rhs=hT[:, kc, :],
                                                                             start=(kc == 0), stop=(kc == KC - 1)),
                             reads=[B("winx"), hb[kc]], writes=[pb])
                if t + 1 < NT:
                    load_x(t + 1, (t + 1) % 2)
                    prologue(xslot[(t + 1) % 2], XB((t + 1) % 2), i, hT, hb)
                for k2 in range(2):
                    P.op(DVE, lambda e, k2=k2: e.tensor_copy(out=cq32[:, k2, :], in_=ps[k2][:]), reads=[PB[k2]],
                         writes=[B("cq32")] if k2 == 0 else (), pwrites=[B("cq32")] if k2 else ())
                    r, rb = sqr[k2], B("sqr", k2)
                    P.op(ACT, lambda e, k2=k2, r=r: e.activation(out=r, in_=ps[k2][:], func=AF.Square), reads=[PB[k2]], writes=[rb])
                    P.op(PE, lambda e, k2=k2, r=r: e.matmul(ps[7][:], lhsT=ones_bf, rhs=r, start=(k2 == 0), stop=(k2 == 1)),
                         reads=[rb, B("ones_bf")], writes=[PB[7]])
                P.op(ACT, lambda e: e.activation(out=rsB, in_=ps[7][:], func=AF.Sqrt, bias=epsc, scale=1.0 / 256), reads=[PB[7], B("epsc")], writes=[B("rsB")])
                P.op(DVE, lambda e: e.reciprocal(out=rsB, in_=rsB), reads=[B("rsB")], writes=[B("rsB")])
                P.op(DVE, lambda e: e.tensor_tensor(out=cq32, in0=cq32, in1=rsB.unsqueeze(1).broadcast_to([128, 2, T]), op=ALU.mult),
                     reads=[B("cq32"), B("rsB")], writes=[B("cq32")])
                for k2 in range(2):
                    P.op(ACT, lambda e, k2=k2, tok=tok: e.activation(out=cq_all[:, k2, tok], in_=cq32[:, k2, :], func=AF.Identity, scale=qn[:, k2:k2 + 1]),
                         reads=[B("cq32"), B("smallv")], pwrites=[B("cq_all")])
                P.op(DVE, lambda e: e.tensor_copy(out=t1, in_=ps[2][:]), reads=[PB[2]], writes=[B("t1")])
                P.op(ACT, lambda e: e.activation(out=sqr[0], in_=ps[2][:], func=AF.Square), reads=[PB[2]], writes=[B("sqr", 0)])
                P.op(PE, lambda e: e.matmul(ps[7][:], lhsT=ones_bf, rhs=sqr[0], start=True, stop=True), reads=[B("sqr", 0), B("ones_bf")], writes=[PB[7]])
                P.op(ACT, lambda e: e.activation(out=rsB, in_=ps[7][:], func=AF.Sqrt, bias=epsc, scale=1.0 / 128), reads=[PB[7], B("epsc")], writes=[B("rsB")])
                P.op(DVE, lambda e: e.reciprocal(out=rsB, in_=rsB), reads=[B("rsB")], writes=[B("rsB")])
                P.op(DVE, lambda e: e.tensor_tensor(out=t1, in0=t1, in1=rsB, op=ALU.mult), reads=[B("t1"), B("rsB")], writes=[B("t1")])
                P.op(ACT, lambda e, tok=tok: e.activation(out=ckvT[:, tok], in_=t1, func=AF.Identity, scale=kvn[:, 0:1]),
                     reads=[B("t1"), B("smallv")], pwrites=[B("ckvT")])
                rope_combine(ps[3][0:32, :], ps[4][0:32, :], PB[3], PB[4], krope[0:32, tok], B("krope", t))
                if t + 1 < NT:
                    load_tab(t + 1)

            kr_all = [B("krope", t) for t in range(NT)]
            for t in range(NT):
                slot = t % 2
                tok = slice(t * T, (t + 1) * T)
                load_x(t, slot)
                if t == 0:
                    load_tab(t)
                    qside(0, tok)
                sc_i = [0]
                pending = [None]
                issue_wo(t * KC + 2)
                for hg in range(4):
                    P.op(POOL, lambda e: e.memset(Vg[:, :, :, 64:65], 1.0), reads=[B("Vg")], writes=[B("Vg")])
                    for kp in range(NKC // 2):
                        for kk in range(2):
                            kc = kp * 2 + kk
                            P.op(PE, lambda e, kc=kc, kk=kk, hg=hg: e.matmul(ps[4][:, kk * 256:(kk + 1) * 256], lhsT=ckvT[:, kc * 128:(kc + 1) * 128],
                                                                           rhs=wuv[:, hg * 256:(hg + 1) * 256], start=True, stop=True),
                                 reads=[B("ckvT"), B("wuv")], writes=[PB[4]] if kk == 0 else (), pwrites=[PB[4]] if kk else ())
                        P.op(DVE, lambda e, kp=kp: e.tensor_copy(out=Vg[:, 2 * kp:2 * kp + 2, :, 0:64],
                                                                 in_=ps[4][:].rearrange("p (k h v) -> p k h v", k=2, h=4)),
                             reads=[PB[4]], pwrites=[B("Vg")])
                    items = [(hh, kc) for hh in range(4) for kc in range(NKC)]
                    base = sc_i[0]

                    def S_(idx):
                        hh, kc = items[idx]
                        h = hg * 4 + hh
                        sb_ = (base + idx) % 4
                        ql, qlb = qlat[h % 2], B("qlat", h % 2)
                        qr, qrb = qrope[h % 2], B("qrope", h % 2)
                        P.op(PE, lambda e: e.matmul(ps[sb_][:], lhsT=ckvT[:, kc * 128:(kc + 1) * 128], rhs=ql, start=True, stop=False),
                             reads=[B("ckvT"), qlb], writes=[PB[sb_]])
                        P.op(PE, lambda e: e.matmul(ps[sb_][:], lhsT=krope[:, kc * 128:(kc + 1) * 128], rhs=qr, start=False, stop=True),
                             reads=kr_all + [qrb], writes=[PB[sb_]])

                    def make_norm(h, ob):
                        def norm():
                            P.op(DVE, lambda e: e.tensor_copy(out=osb, in_=ps[ob][0:65, :]), reads=[PB[ob]], writes=[B("osb")])
                            P.op(PE, lambda e: e.matmul(ps[5][0:64, :], lhsT=ones_f[64:65, 0:64], rhs=osb[64:65, :], start=True, stop=True),
                                 reads=[B("ones_f"), B("osb")], writes=[PB[5]])
                            P.op(DVE, lambda e: e.reciprocal(out=rc, in_=ps[5][0:64, :]), reads=[PB[5]], writes=[B("rc")])
                            P.op(POOL, lambda e: e.tensor_tensor(out=oT[0:64, h, :], in0=osb[0:64, :], in1=rc, op=ALU.mult),
                                 reads=[B("osb"), B("rc")], writes=[B("oT", h)])
                        return norm

                    LOOK = 3
                    for idx in range(LOOK):
                        S_(idx)
                    for idx, (hh, kc) in enumerate(items):
                        h = hg * 4 + hh
                        ob = 6 + h % 2
                        if idx + LOOK < len(items):
                            S_(idx + LOOK)
                        if kc == 4 and pending[0] is not None:
                            pending[0]()
                            pending[0] = None
                        if kc == 8 and h + 1 < NH:
                            qside(h + 1, tok)
                        if kc == 8 and h + 1 == NH and t + 1 < NT:
                            load_tab(t + 1)
                            qside(0, slice((t + 1) * T, (t + 2) * T))
                        sb_ = (base + idx) % 4
                        p_, pb_ = pT[sb_], B("pT", sb_)
                        P.op(ACT, lambda e, p_=p_, sb_=sb_: e.activation(out=p_, in_=ps[sb_][:], func=AF.Exp, scale=ATTN_SCALE),
                             reads=[PB[sb_]], writes=[pb_])
                        P.op(PE, lambda e, kc=kc, hh=hh, p_=p_, ob=ob: e.matmul(ps[ob][0:65, :], lhsT=Vg[:, kc, hh, :], rhs=p_,
                                                                              start=(kc == 0), stop=(kc == NKC - 1)),
                             reads=[B("Vg"), pb_], writes=[PB[ob]])
                        if kc == NKC - 1:
                            if pending[0] is not None:
                                pending[0]()
                            pending[0] = make_norm(h, ob)
                    sc_i[0] = base + len(items)
                    if hg == 3 and pending[0] is not None:
                        pending[0]()
                        pending[0] = None
                for c in range(KC):
                    m_ = t * KC + c
                    issue_wo(m_ + 1)
                    ws_, wb_ = wo_ring[m_ % 2], B("wo_ring", m_ % 2)
                    yb = 4 + c % 2
                    for h in range(NH):
                        P.op(PE, lambda e, h=h, ws_=ws_, yb=yb: e.matmul(ps[yb][:], lhsT=ws_[:, h, :], rhs=oT[:, h, :], start=(h == 0), stop=(h == NH - 1)),
                             reads=[wb_, B("oT", h)], writes=[PB[yb]])
                    issue_wo(m_ + 3)
                    P.op(DVE, lambda e, c=c, yb=yb: e.tensor_copy(out=ysb[:, c, :], in_=ps[yb][:]), reads=[PB[yb]],
                         writes=[YB[c]])
                    post_stats_sq(c)
                    if c > 0:
                        post_stats_mm(c - 1)
                post_stats_mm(KC - 1)
                epilogue(xslot[slot], XB(slot), i)
                store_x(t, slot, dst, dst_key)
                pump(8)

        layers_needed = sorted({l for (l, s_) in sublayers})
        for k in ffn_ids:
            precast(k)
        if ffn_ids and sublayers[0][1] != 1:
            flush_precast(ffn_ids[0])
        RA.reset()
        state["defer_mod"] = None
        for l in layers_needed:
            if l == 1 and (0, 1) in sublayers:
                state["defer_mod"] = 1
                continue
            RA.reset()
            compute_mod(l, RA)
        for n, (l, sub) in enumerate(sublayers):
            last = (n == len(sublayers) - 1)
            dst, dst_key = (outT, "outT") if last else (xres[n % 2], f"xres{n % 2}")
            kind = "ffn" if sub != 1 else ("pool" if l % 2 == 0 else "mla")
            if not (kind == "ffn" and state.get("prev_kind") == "ffn"):
                rb = [b for k_, b in bufs.items() if k_[0] in REGION_KEYS]
                fop = P.op(POOL, lambda e: e.memset(epsc, EPS), writes=rb + [B("epsc")])
                state["fence_op"] = fop
            state["prev_kind"] = kind
            nb0 = set(bufs.keys())
            if sub == 1 and l % 2 == 0:
                pool_sublayer(l, sub, dst, dst_key)
            elif sub == 1:
                mla_sublayer(l, sub, dst, dst_key)
            else:
                ffn_sublayer(l, sub, dst, dst_key)
            state["src"], state["src_key"] = dst, dst_key
        pump(len(pq))
        P.op(POOL, lambda e: e.memset(epsc, EPS), reads=[B("outT", t) for t in range(NT)], writes=[B("epsc")])
        P.op(SP, lambda e: e.nop(), reads=[B("epsc")] + [B("outT", t) for t in range(NT)])
        if max_ops is not None:
            P.ops = P.ops[:max_ops]
        P.emit(st)
        print("ops", len(P.ops), "sems", P.n_sems, "waits", P.n_waits)
    return nc


def _rope_tables():
    inv = (1.0 / (np.float32(10000.0) ** (np.arange(0, 32, 2, dtype=np.float32) / np.float32(32)))).astype(np.float32)
    ang = (np.arange(S, dtype=np.float32)[:, None] * inv[None, :]).astype(np.float32)
    cos = np.cos(ang).astype(np.float32).T
    sin = np.sin(ang).astype(np.float32).T
    return (np.ascontiguousarray(np.concatenate([cos, cos], axis=0)),
            np.ascontiguousarray(np.concatenate([-sin, sin], axis=0)))


def _shared_inputs(inp):
    f = lambda a: np.ascontiguousarray(a, dtype=np.float32)
    def vecT(v):
        return np.swapaxes(v.reshape(v.shape[:-1] + (8, 128)), -1, -2)
    ada_b = inp["ada_b"].reshape(2, 9, 1024)
    ada_bT = np.transpose(vecT(ada_b), (0, 2, 1, 3)).reshape(2, 128, 72)
    norm_gT = np.transpose(vecT(inp["norm_g"]), (0, 2, 1, 3)).reshape(2, 128, 48)
    w_in = inp["mla_w_in"][0]
    kr = w_in[:, 384:416]
    krp = np.concatenate([kr[:, 16:32], kr[:, 0:16]], axis=1)
    mla_w_in_x = np.concatenate([w_in[:, :384], kr, krp], axis=1)
    wuq = inp["mla_w_uq"][0]
    nope = wuq[:, :, :64].reshape(256, 1024)
    rope = wuq[:, :, 64:]
    ropep = np.concatenate([rope[:, :, 16:32], rope[:, :, 0:16]], axis=2)
    w_uq_x = np.concatenate([nope, rope.reshape(256, 512), ropep.reshape(256, 512)], axis=1)
    w_ukT = np.transpose(inp["mla_w_uk"][0], (2, 1, 0))
    cos, sin = _rope_tables()
    return {
        "ada_w": f(inp["ada_w"]), "ada_bT": f(ada_bT), "norm_gT": f(norm_gT),
        "ffn_w_in": f(inp["ffn_w_in"]), "ffn_w_out": f(inp["ffn_w_out"]),
        "pool_w": f(inp["pool_w"][0]), "pool_bT": f(vecT(inp["pool_b"][0].reshape(1024))),
        "pool_scT": f(vecT(inp["pool_scale"][0])),
        "mla_w_in_x": f(mla_w_in_x), "q_normT": f(inp["mla_q_norm"][0].reshape(2, 128).T),
        "kv_normT": f(inp["mla_kv_norm"][0].reshape(1, 128).T),
        "w_uq_x": f(w_uq_x), "w_ukT": f(w_ukT), "w_uv": f(inp["mla_w_uv"][0].reshape(128, 1024)),
        "w_o": f(inp["mla_w_o"][0]), "rope_cos": cos, "rope_sin": sin,
    }


FUSED = True
ALL_SUBLAYERS = [(0, 0), (0, 1), (0, 2), (1, 0), (1, 1), (1, 2)]
_NC_CACHE = {}


def run_sublayers(xT_list, c, shared, sublayers, core_ids=None):
    key = tuple(sublayers)
    if key not in _NC_CACHE:
        _NC_CACHE[key] = build_program(list(sublayers))
    nc = _NC_CACHE[key]
    n = len(xT_list)
    in_maps = []
    for b in range(n):
        m = dict(shared)
        m["xT"] = xT_list[b]
        m["cT"] = np.ascontiguousarray(c[b].reshape(8, 128).T, dtype=np.float32)
        in_maps.append(m)
    res = run_bass_kernel_spmd(nc, in_maps, core_ids=list(range(n)) if core_ids is None else core_ids)
    return [r["outT"] for r in res.results]


def kernel(**inputs):
    inp = {k: np.asarray(v) for k, v in inputs.items()}
    x = inp["x"]
    shared = _shared_inputs(inp)
    xT_list = [np.ascontiguousarray(x[b].T) for b in range(x.shape[0])]
    if FUSED:
        outs = run_sublayers(xT_list, inp["c"], shared, ALL_SUBLAYERS)
    else:
        mid = run_sublayers(xT_list, inp["c"], shared, ALL_SUBLAYERS[:3])
        outs = run_sublayers([np.ascontiguousarray(m) for m in mid], inp["c"], shared, ALL_SUBLAYERS[3:])
    return np.stack([np.ascontiguousarray(o.T) for o in outs], axis=0).astype(np.float32)
```
